# Optimizing a Trainium2 kernel written in Bass

```python
import jax, jax.numpy as jnp
from jax import lax
import numpy as np

D_MODEL = 2048
BATCH = 4
SEQ = 2048
DEPTH = 2
DEC_BATCH = 128
DEC_SEQ = 4
PAST_LEN = 16384
PAGE_SIZE = 128

MIX_WIDTH = D_MODEL
GLA_WIDTH = MIX_WIDTH // 2
HGRN_WIDTH = MIX_WIDTH - GLA_WIDTH
GLA_HEADS = 4
GLA_DK = GLA_WIDTH // 2 // GLA_HEADS
GLA_DV = GLA_WIDTH // GLA_HEADS
GLA_RANK = 16
GLA_TAU = 16.0
HGRN_EXPAND = 128
HGRN_HEADS = HGRN_WIDTH // HGRN_EXPAND
HGRN_DI = HGRN_WIDTH // HGRN_HEADS
D_FF = ((8 * D_MODEL // 3 + 255) // 256) * 256
CHUNK = 64
EPS = 1e-6

IN_SIZES = (GLA_HEADS * GLA_DK,
            GLA_HEADS * GLA_DK,
            GLA_WIDTH,
            GLA_RANK,
            GLA_WIDTH,
            HGRN_HEADS * HGRN_EXPAND,
            HGRN_HEADS * HGRN_EXPAND,
            HGRN_WIDTH,
            HGRN_WIDTH)
IN_WIDTH = sum(IN_SIZES)

kernel_name = "hymba_gla_hgrn2_macaron_step"


def _split_points():
    pts, acc = [], 0
    for s in IN_SIZES[:-1]:
        acc += s
        pts.append(acc)
    return pts


def rmsnorm(x, w):
    xf = x.astype(jnp.float32)
    y = xf * lax.rsqrt(jnp.mean(xf * xf, axis=-1, keepdims=True) + EPS)
    return (y * w.astype(jnp.float32)).astype(x.dtype)


def swiglu(x, w_in, w_out):
    gate, up = jnp.split(x @ w_in, 2, axis=-1)
    return (jax.nn.silu(gate) * up) @ w_out


def gated_linear_scan(q, k, v, log_a, s0):
    f32 = jnp.float32
    q, k, v, log_a, s0 = (z.astype(f32) for z in (q, k, v, log_a, s0))
    b, t, h, dk = q.shape
    dv = v.shape[-1]
    c = CHUNK if t % CHUNK == 0 else t
    n = t // c

    def to_chunks(z):
        return z.reshape(b, n, c, h, z.shape[-1]).transpose(1, 0, 3, 2, 4)

    qc, kc, vc, gc = to_chunks(q), to_chunks(k), to_chunks(v), to_chunks(log_a)
    causal = jnp.tril(jnp.ones((c, c), dtype=bool))[:, :, None]

    def step(s, inp):
        qi, ki, vi, gi = inp
        cum = jnp.cumsum(gi, axis=2)
        diff = cum[:, :, :, None, :] - cum[:, :, None, :, :]
        decay = jnp.where(causal, jnp.exp(jnp.where(causal, diff, 0.0)), 0.0)
        scores = jnp.sum(qi[:, :, :, None, :] * ki[:, :, None, :, :] * decay, axis=-1)
        o = (jnp.einsum('bhts,bhsv->bhtv', scores, vi)
             + jnp.einsum('bhtd,bhdv->bhtv', qi * jnp.exp(cum), s))
        last = cum[:, :, -1:, :]
        s_new = (jnp.exp(last[:, :, 0, :])[..., None] * s
                 + jnp.einsum('bhsd,bhsv->bhdv', ki * jnp.exp(last - cum), vi))
        return s_new, o

    s_fin, oc = lax.scan(step, s0, (qc, kc, vc, gc))
    o = oc.transpose(1, 0, 3, 2, 4).reshape(b, t, h, dv)
    return o, s_fin


def hybrid_mixer(h, w_in, gla_w_up, gla_b, gla_norm, lb, hgrn_norm, w_out, s_gla, s_hgrn):
    b, t, _ = h.shape
    proj = h @ w_in
    q, k, v, a_lr, r, hq, hf, hi, hg = jnp.split(proj, _split_points(), axis=-1)

    log_alpha = jax.nn.log_sigmoid((a_lr @ gla_w_up + gla_b).astype(jnp.float32)) / GLA_TAU
    o_gla, s_gla_new = gated_linear_scan(
        q.reshape(b, t, GLA_HEADS, GLA_DK) * (GLA_DK ** -0.5),
        k.reshape(b, t, GLA_HEADS, GLA_DK),
        v.reshape(b, t, GLA_HEADS, GLA_DV),
        log_alpha.reshape(b, t, GLA_HEADS, GLA_DK),
        s_gla)
    o_gla = rmsnorm(o_gla, gla_norm) * jax.nn.silu(r.reshape(b, t, GLA_HEADS, GLA_DV).astype(jnp.float32))
    o_gla = o_gla.reshape(b, t, GLA_WIDTH)

    lbf = lb.astype(jnp.float32)
    zf = hf.astype(jnp.float32)
    f = lbf + (1.0 - lbf) * jax.nn.sigmoid(zf)
    log_f = jnp.log(f)
    k_h = (1.0 - lbf) * jax.nn.sigmoid(-zf)
    o_h, s_hgrn_new = gated_linear_scan(
        jax.nn.silu(hq.reshape(b, t, HGRN_HEADS, HGRN_EXPAND)) * (HGRN_EXPAND ** -0.5),
        k_h.reshape(b, t, HGRN_HEADS, HGRN_EXPAND),
        hi.reshape(b, t, HGRN_HEADS, HGRN_DI),
        log_f.reshape(b, t, HGRN_HEADS, HGRN_EXPAND),
        s_hgrn)
    o_h = rmsnorm(o_h, hgrn_norm) * jax.nn.silu(hg.reshape(b, t, HGRN_HEADS, HGRN_DI).astype(jnp.float32))
    o_h = o_h.reshape(b, t, HGRN_WIDTH)

    merged = jnp.concatenate([o_gla, o_h], axis=-1).astype(h.dtype)
    return merged @ w_out, s_gla_new, s_hgrn_new


def trunk(x, s_gla, s_hgrn, norm_gains, ffn1_w_in, ffn1_w_out, ffn2_w_in, ffn2_w_out,
          mix_w_in, gla_w_gate_up, gla_b_gate, gla_norm, hgrn_gamma, hgrn_norm, mix_w_out):
    probs = jax.nn.softmax(hgrn_gamma.astype(jnp.float32), axis=0)
    lbs = jnp.cumsum(probs, axis=0) - probs[0:1]
    new_gla, new_hgrn = [], []
    for l in range(DEPTH):
        g = norm_gains[l]
        x = x + 0.5 * rmsnorm(swiglu(rmsnorm(x, g[0]), ffn1_w_in[l], ffn1_w_out[l]), g[1])
        m, sg, sh = hybrid_mixer(rmsnorm(x, g[2]), mix_w_in[l], gla_w_gate_up[l], gla_b_gate[l],
                                 gla_norm[l], lbs[l], hgrn_norm[l], mix_w_out[l], s_gla[l], s_hgrn[l])
        x = x + rmsnorm(m, g[3])
        x = x + 0.5 * rmsnorm(swiglu(rmsnorm(x, g[4]), ffn2_w_in[l], ffn2_w_out[l]), g[5])
        new_gla.append(sg)
        new_hgrn.append(sh)
    return x, jnp.stack(new_gla), jnp.stack(new_hgrn)


def setup_inputs(seed: int = 0) -> dict:
    key = jax.random.key(seed)
    ks = jax.random.split(key, 16)
    f32 = jnp.float32

    def nrm(k, shape, scale):
        return jax.random.normal(k, shape, f32) * scale

    return {
        "x_prompt": nrm(ks[0], (BATCH, SEQ, D_MODEL), 1.0),
        "x_sample": nrm(ks[1], (DEC_BATCH, DEC_SEQ, D_MODEL), 1.0),
        "state_gla": nrm(ks[2], (DEPTH, DEC_BATCH, GLA_HEADS, GLA_DK, GLA_DV), 1.0),
        "state_hgrn": nrm(ks[3], (DEPTH, DEC_BATCH, HGRN_HEADS, HGRN_EXPAND, HGRN_DI), 1.0),
        "norm_gains": 1.0 + nrm(ks[4], (DEPTH, 6, D_MODEL), 0.02),
        "ffn1_w_in": nrm(ks[5], (DEPTH, D_MODEL, 2 * D_FF), D_MODEL ** -0.5),
        "ffn1_w_out": nrm(ks[6], (DEPTH, D_FF, D_MODEL), D_FF ** -0.5),
        "ffn2_w_in": nrm(ks[7], (DEPTH, D_MODEL, 2 * D_FF), D_MODEL ** -0.5),
        "ffn2_w_out": nrm(ks[8], (DEPTH, D_FF, D_MODEL), D_FF ** -0.5),
        "mix_w_in": nrm(ks[9], (DEPTH, D_MODEL, IN_WIDTH), D_MODEL ** -0.5),
        "gla_w_gate_up": nrm(ks[10], (DEPTH, GLA_RANK, GLA_HEADS * GLA_DK), GLA_RANK ** -0.5),
        "gla_b_gate": nrm(ks[11], (DEPTH, GLA_HEADS * GLA_DK), 0.01),
        "gla_norm": 1.0 + nrm(ks[12], (DEPTH, GLA_DV), 0.02),
        "hgrn_gamma": nrm(ks[13], (DEPTH, HGRN_HEADS * HGRN_EXPAND), 0.1),
        "hgrn_norm": 1.0 + nrm(ks[14], (DEPTH, HGRN_DI), 0.02),
        "mix_w_out": nrm(ks[15], (DEPTH, MIX_WIDTH, D_MODEL), MIX_WIDTH ** -0.5),
    }


def reference(x_prompt, x_sample, state_gla, state_hgrn, norm_gains, ffn1_w_in, ffn1_w_out,
              ffn2_w_in, ffn2_w_out, mix_w_in, gla_w_gate_up, gla_b_gate, gla_norm,
              hgrn_gamma, hgrn_norm, mix_w_out):
    zero_gla = jnp.zeros((DEPTH, x_prompt.shape[0], GLA_HEADS, GLA_DK, GLA_DV), jnp.float32)
    zero_hgrn = jnp.zeros((DEPTH, x_prompt.shape[0], HGRN_HEADS, HGRN_EXPAND, HGRN_DI), jnp.float32)
    y_prompt, state_gla_prompt, state_hgrn_prompt = trunk(
        x_prompt, zero_gla, zero_hgrn, norm_gains, ffn1_w_in, ffn1_w_out, ffn2_w_in, ffn2_w_out,
        mix_w_in, gla_w_gate_up, gla_b_gate, gla_norm, hgrn_gamma, hgrn_norm, mix_w_out)
    y_sample, state_gla_sample, state_hgrn_sample = trunk(
        x_sample, state_gla, state_hgrn, norm_gains, ffn1_w_in, ffn1_w_out, ffn2_w_in, ffn2_w_out,
        mix_w_in, gla_w_gate_up, gla_b_gate, gla_norm, hgrn_gamma, hgrn_norm, mix_w_out)
    return (y_prompt, y_sample, state_gla_prompt, state_hgrn_prompt, state_gla_sample, state_hgrn_sample)
```

```python
import numpy as np
import concourse.bass as bass
import concourse.mybir as mybir
from concourse.bass_utils import run_bass_kernel_spmd

F32 = mybir.dt.float32
BF16 = mybir.dt.bfloat16
AF = mybir.ActivationFunctionType
ALU = mybir.AluOpType

D = 2048
T = 1088
NJ = 44
NCORES = 8
EPS = 1e-6


class _Rec:
    def __init__(self):
        self.calls = []

    def __getattr__(self, name):
        def call(*a, **kw):
            self.calls.append((name, a, kw))
            return self
        return call


class Prog:
    ENGS = ("tensor", "vector", "scalar", "gpsimd", "sync")

    def __init__(self, nc):
        self.nc = nc
        self.ops = []
        self.last_w = {}
        self.readers = {}
        self.dma_keys = {}
        self.bar = []
        self.last_eng = {}

    def barrier(self):
        self.bar = list(self.last_eng.values()) + list(self.dma_keys.values())

    def op(self, eng, fn, reads=(), writes=(), dma=None):
        i = len(self.ops)
        if eng != "tensor":
            extra = [k for k in reads if isinstance(k, tuple) and k[0] == "ps" and k not in writes]
            if extra:
                writes = list(writes) + extra
        deps = set((b, "raw") for b in self.bar)
        for k in reads:
            if k in self.last_w:
                deps.add((self.last_w[k], "raw"))
        for k in writes:
            if k in self.last_w:
                deps.add((self.last_w[k], "waw"))
            for r in self.readers.get(k, {}).values():
                deps.add((r, "war"))
        for k in writes:
            self.last_w[k] = i
            self.readers[k] = {}
        for k in reads:
            self.readers.setdefault(k, {})[eng if dma is None else ("dma", i)] = i
        if dma is not None:
            prev = self.dma_keys.get(dma)
            if prev is not None:
                deps.add((prev, "waw"))
            self.dma_keys[dma] = i
        else:
            self.last_eng[eng] = i
        rec = _Rec()
        fn(rec)
        assert len(rec.calls) == 1, rec.calls
        self.ops.append(dict(eng=eng, call=rec.calls[0], deps=deps, dma=dma, sig=False))
        return i

    def emit(self):
        nc = self.nc
        ops = self.ops
        for i, o in enumerate(ops):
            best = {}
            nn = []
            for (p, kind) in o["deps"]:
                po = ops[p]
                if po["dma"] is not None:
                    nn.append(p)
                    continue
                if po["eng"] == o["eng"] and o["dma"] is None:
                    if o["eng"] == "tensor" or kind == "war":
                        continue
                e2 = po["eng"]
                if e2 not in best or best[e2] < p:
                    best[e2] = p
            o["need"] = nn + list(best.values())
            for p in best.values():
                ops[p]["sig"] = True
        cnt = {e: 0 for e in self.ENGS}
        dcnt = {}
        for o in ops:
            if o["dma"] is not None:
                dcnt[o["dma"]] = dcnt.get(o["dma"], 0) + (1 if (isinstance(o["dma"], tuple) and o["dma"][0] == "cc") else 16)
                o["val"] = dcnt[o["dma"]]
            elif o["sig"]:
                cnt[o["eng"]] += 1
                o["val"] = cnt[o["eng"]]
        sems = {e: nc.alloc_semaphore("s_" + e) for e in self.ENGS}
        dsems = {k: nc.alloc_semaphore("d_%d" % n) for n, k in enumerate(dcnt)}
        self.stats = dict(cnt=dict(cnt), ndma=len(dcnt), nops=len(ops))

        def run(ename, eng):
            waited = {}
            for o in ops:
                if o["eng"] != ename:
                    continue
                for p in o["need"]:
                    po = ops[p]
                    if po["dma"] is not None:
                        s = dsems[po["dma"]]
                        key = ("d", po["dma"])
                    else:
                        s = sems[po["eng"]]
                        key = ("e", po["eng"])
                    v = po["val"]
                    if waited.get(key, 0) >= v:
                        continue
                    waited[key] = v
                    eng.wait_ge(s, v)
                name, a, kw = o["call"]
                ins = getattr(eng, name)(*a, **kw)
                if o["dma"] is not None:
                    ins.then_inc(dsems[o["dma"]], 1 if (isinstance(o["dma"], tuple) and o["dma"][0] == "cc") else 16)
                elif o["sig"]:
                    ins.then_inc(sems[ename], 1)
            if ename == "sync":
                for e in self.ENGS:
                    if e != "sync" and cnt[e] > 0:
                        eng.wait_ge(sems[e], cnt[e])
                for k, v in dcnt.items():
                    eng.wait_ge(dsems[k], v)

        with nc.Block() as block:
            @block.tensor
            def _(e):
                run("tensor", e)

            @block.vector
            def _(e):
                run("vector", e)

            @block.scalar
            def _(e):
                run("scalar", e)

            @block.gpsimd
            def _(e):
                run("gpsimd", e)

            @block.sync
            def _(e):
                run("sync", e)


GROUPS = [
    dict(name="gla", kind="gla", NQ=4, NU=8, upq=2, H0=0, mch0=0),
    dict(name="hga", kind="hgrn", NQ=4, NU=4, upq=1, H0=0, mch0=8),
    dict(name="hgb", kind="hgrn", NQ=4, NU=4, upq=1, H0=4, mch0=12),
]
C_Q, C_K, C_V, C_A, C_R, C_HQ, C_HF, C_HI, C_HG = 0, 512, 1024, 2048, 2064, 3088, 4112, 5136, 6160
NTILES = [(0, 512), (512, 512), (1024, 64)]


def build(stage=99, fused=True):
    nc = bass.Bass("TRN2", target_bir_lowering=False)

    def din(name, shape):
        return nc.dram_tensor(name, list(shape), F32, kind="ExternalInput").ap()

    def dout(name, shape):
        return nc.dram_tensor(name, list(shape), F32, kind="ExternalOutput").ap()

    xin = din("xin", [T, D])
    sgla = din("sgla", [2, 16, 4, 128, 256])
    shg = din("shg", [2, 16, 8, 128, 128])
    pin_gla = din("pin_gla", [2, 4, 128, 256])
    pin_hg = din("pin_hg", [2, 8, 128, 128])
    ng = din("norm_gains", [2, 6, D])
    w1i = din("ffn1_w_in", [2, D, 2 * 5632])
    w1o = din("ffn1_w_out", [2, 5632, D])
    w2i = din("ffn2_w_in", [2, D, 2 * 5632])
    w2o = din("ffn2_w_out", [2, 5632, D])
    wmi = din("mix_w_in", [2, D, 7184])
    wgu = din("gla_w_gate_up", [2, 16, 512])
    bgt = din("gla_b_gate", [2, 512])
    gnm = din("gla_norm", [2, 256])
    hgm = din("hgrn_gamma", [2, 1024])
    hnm = din("hgrn_norm", [2, 128])
    wmo = din("mix_w_out", [2, D, D])
    cst = din("consts", [128, 336])
    cmk = din("coremask", [128, 1])
    y = dout("y", [T, D])
    pst_gla = dout("pst_gla", [2, 4, 128, 256])
    pst_hg = dout("pst_hg", [2, 8, 128, 128])
    sst_gla = dout("sst_gla", [2, 16, 4, 128, 256])
    sst_hg = dout("sst_hg", [2, 16, 8, 128, 128])
    xsp = nc.dram_tensor("xspill", [128, 16 * T], F32).ap()

    P = Prog(nc)
    A = nc.alloc_sbuf_tensor("arena", [128, 52000], F32)
    PS = [nc.alloc_psum_tensor("ps%d" % i, [128, 512], F32) for i in range(8)]

    def W(off, n):
        return A[:, off:off + n]

    def WB(off, nw):
        return A[:, off:off + nw].bitcast(BF16)

    XTf = W(0, 16 * T)
    XT = XTf.rearrange("p (k t) -> p k t", t=T)
    PAR = W(17408, 512)
    GAIN = PAR[:, 0:192]
    NEGB = PAR[:, 192:200]
    GNW = PAR[:, 200:204]
    HNW = PAR[:, 204:206]
    LB = PAR[:, 206:222]
    OML = PAR[:, 222:238]
    ONEC = PAR[:, 300:301]
    EPSC = PAR[:, 301:302]
    RAW3 = PAR[:, 320:350]
    MASKC = PAR[:, 302:303]
    CON = W(17920, 512)
    IDF = CON[:, 0:128]
    CMASK = CON[:, 128:192]
    SMASK = CON[:, 192:256]
    BSEL = CON[:, 256:272]
    RMASK = CON[:, 272:336]
    IDB = WB(17920 + 336, 64)
    ONESB = WB(17920 + 400, 64)
    PB = 18432
    ACT = WB(PB, 11968).rearrange("p (j t) -> p j t", t=544)
    OUT = W(PB + 11968, 8704).rearrange("p (m t) -> p m t", t=544)
    HT = WB(PB + 11968, 4352).rearrange("p (k t) -> p k t", t=544)
    RSTD = W(PB + 20672, 544)
    TMP = [W(PB + 21216, 544), W(PB + 21760, 544)]
    SQM = [WB(PB + 22304, 272), WB(PB + 22576, 272)]
    RING = [WB(PB + 22848 + 2816 * s, 2816) for s in range(3)]
    RG = [r[:, 0:2048].rearrange("p (k c) -> p k c", c=128) for r in RING]
    RU = [r[:, 2048:4096].rearrange("p (k c) -> p k c", c=128) for r in RING]
    RO = [r[:, 0:5632].rearrange("p (j c) -> p j c", c=128) for r in RING]
    XSTG = [W(PB, 2048), W(PB + 2048, 2048)]
    STG = [W(PB + 4096, 128), W(PB + 4224, 128), W(PB + 4352, 128)]
    MERGED = WB(PB, 8704).rearrange("p (k t) -> p k t", t=T)
    HTM = WB(PB + 8704, 8704).rearrange("p (k t) -> p k t", t=T)
    OUTM = W(PB + 8704, 17408).rearrange("p (m t) -> p m t", t=T)
    QT = WB(0, 2176).rearrange("p (h t) -> p h t", t=T)
    KT = WB(2176, 2176).rearrange("p (h t) -> p h t", t=T)
    KTOK = WB(4352, 2304).rearrange("p (b c) -> p b c", c=512)
    VTOK = WB(6656, 4608).rearrange("p (b c) -> p b c", c=1024)
    GATE = WB(11264, 4352).rearrange("p (u t) -> p u t", t=T)
    ALR = WB(15616, 544)
    WUP = WB(16160, 256)
    TABo = 16416
    TM = W(TABo, 64).rearrange("p (h c) -> p h c", c=16)
    TPV = W(TABo + 64, 64).rearrange("p (h c) -> p h c", c=16)
    EE = W(TABo + 128, 192).rearrange("p (h e c) -> p h e c", e=3, c=16)
    ELS = W(TABo + 320, 64).rearrange("p (h c) -> p h c", c=16)
    WALR = WB(16928, 128).rearrange("p (k c) -> p k c", c=16)
    MB = PB + 17408
    ZG = W(MB, T)
    D1 = W(MB + 1088, T)
    EK = W(MB + 2176, T)
    KR = W(MB + 3264, T)
    SST = W(MB + 4352, 1024)
    SBF = WB(MB + 5376, 512)
    SRING = W(MB + 5888, 2048)
    SBFS = WB(MB + 7936, 1024)
    VM = WB(MB + 8960, 1024)
    SCS = WB(MB + 9984, 256)
    SQN = WB(MB + 10240, 256)
    RSTDN = W(MB + 10496, 512)
    T1 = W(MB + 11008, 512)
    SCSS = WB(MB + 11520, 128)
    WR = [WB(MB + 11648 + 2048 * s, 2048).rearrange("p (k c) -> p k c", c=256) for s in range(2)]

    state = dict(ring=0, wr=0)

    def V(fn, r=(), w=()):
        return P.op("vector", fn, r, w)

    def S(fn, r=(), w=()):
        return P.op("scalar", fn, r, w)

    def TE(fn, r=(), w=()):
        return P.op("tensor", fn, r, w)

    def DQ(fn, r=(), w=(), key=None):
        return P.op("sync", fn, r, w, dma=key)

    def DG(fn, r=(), w=(), key=None):
        return P.op("gpsimd", fn, r, w, dma=key)

    def mm(out, lhsT, rhs, start, stop, r, w=()):
        return TE(lambda e: e.matmul(out, lhsT=lhsT, rhs=rhs, start=start, stop=stop, skip_group_check=True), r, w)

    DQ(lambda e: e.dma_start(out=CON[:, 0:336], in_=cst), w=["con"], key="cst")
    V(lambda e: e.tensor_copy(out=IDB, in_=IDF), ["con"], ["idb"])
    DQ(lambda e: e.dma_start(out=MASKC, in_=cmk), w=["par"], key="cmk")
    V(lambda e: e.memset(ONESB, 1.0), (), ["onesb"])
    V(lambda e: e.memset(ONEC, 1.0), (), ["par"])
    V(lambda e: e.memset(EPSC, EPS), (), ["par"])
    ngv = ng.rearrange("l i (k p) -> (l i k) p", p=128)
    DQ(lambda e: e.dma_start(out=STG[0][0:96, :], in_=ngv[0:96, :]), w=["stg0"], key="p0")
    DQ(lambda e: e.dma_start(out=STG[1][0:96, :], in_=ngv[96:192, :]), w=["stg1"], key="p1")
    DQ(lambda e: e.dma_start(out=STG[2][0:8, :], in_=bgt.rearrange("l (h p) -> (l h) p", p=128)), w=["stg2a"], key="p2")
    DQ(lambda e: e.dma_start(out=STG[2][8:12, :], in_=gnm.rearrange("l (c p) -> (l c) p", p=128)), w=["stg2b"], key="p3")
    DQ(lambda e: e.dma_start(out=STG[2][12:14, :], in_=hnm), w=["stg2c"], key="p4")
    DQ(lambda e: e.dma_start(out=STG[2][14:30, :], in_=hgm.rearrange("l (h p) -> (l h) p", p=128)), w=["stg2d"], key="p5")
    TE(lambda e: e.transpose(out=PS[0][:, 0:96], in_=STG[0][0:96, :], identity=IDF[0:96, 0:96]), ["stg0", "con"], [("ps", 0)])
    V(lambda e: e.tensor_copy(out=GAIN[:, 0:96], in_=PS[0][:, 0:96]), [("ps", 0)], ["par"])
    TE(lambda e: e.transpose(out=PS[1][:, 0:96], in_=STG[1][0:96, :], identity=IDF[0:96, 0:96]), ["stg1", "con"], [("ps", 1)])
    V(lambda e: e.tensor_copy(out=GAIN[:, 96:192], in_=PS[1][:, 0:96]), [("ps", 1)], ["par"])
    TE(lambda e: e.transpose(out=PS[2][:, 0:30], in_=STG[2][0:30, :], identity=IDF[0:30, 0:30]),
       ["stg2a", "stg2b", "stg2c", "stg2d", "con"], [("ps", 2)])
    V(lambda e: e.tensor_copy(out=RAW3, in_=PS[2][:, 0:30]), [("ps", 2)], ["par"])
    V(lambda e: e.tensor_scalar(out=NEGB, in0=PAR[:, 320:328], scalar1=-1.0, scalar2=None, op0=ALU.mult), ["par"], ["par"])
    V(lambda e: e.tensor_copy(out=GNW, in_=PAR[:, 328:332]), ["par"], ["par"])
    V(lambda e: e.tensor_copy(out=HNW, in_=PAR[:, 332:334]), ["par"], ["par"])
    g0, g1 = PAR[:, 334:342], PAR[:, 342:350]
    mx, e0, e1, ss, rs, p0, p1, c1 = [PAR[:, 352 + 8 * i:360 + 8 * i] for i in range(8)]
    V(lambda e: e.tensor_tensor(out=mx, in0=g0, in1=g1, op=ALU.max), ["par"], ["par"])
    V(lambda e: e.tensor_tensor(out=e0, in0=g0, in1=mx, op=ALU.subtract), ["par"], ["par"])
    V(lambda e: e.tensor_tensor(out=e1, in0=g1, in1=mx, op=ALU.subtract), ["par"], ["par"])
    S(lambda e: e.activation(out=e0, in_=e0, func=AF.Exp), ["par"], ["par"])
    S(lambda e: e.activation(out=e1, in_=e1, func=AF.Exp), ["par"], ["par"])
    V(lambda e: e.tensor_tensor(out=ss, in0=e0, in1=e1, op=ALU.add), ["par"], ["par"])
    V(lambda e: e.reciprocal(out=rs, in_=ss), ["par"], ["par"])
    V(lambda e: e.tensor_tensor(out=p0, in0=e0, in1=rs, op=ALU.mult), ["par"], ["par"])
    V(lambda e: e.tensor_tensor(out=p1, in0=e1, in1=rs, op=ALU.mult), ["par"], ["par"])
    V(lambda e: e.tensor_tensor(out=c1, in0=p0, in1=p1, op=ALU.add), ["par"], ["par"])
    V(lambda e: e.tensor_tensor(out=LB[:, 0:8], in0=p0, in1=p0, op=ALU.subtract), ["par"], ["par"])
    V(lambda e: e.tensor_tensor(out=LB[:, 8:16], in0=c1, in1=p0, op=ALU.subtract), ["par"], ["par"])
    V(lambda e: e.tensor_scalar(out=OML, in0=LB, scalar1=-1.0, scalar2=1.0, op0=ALU.mult, op1=ALU.add), ["par"], ["par"])

    for blk in range(9):
        rows = 128 if blk < 8 else 64
        xs = XSTG[blk % 2]
        DQ(lambda e, xs=xs, blk=blk, rows=rows: e.dma_start(out=xs[0:rows, :], in_=xin[blk * 128:blk * 128 + rows, :]),
           w=[("xstg", blk % 2)], key=("xl", blk % 2))
        for q in range(4):
            bk = (blk * 4 + q) % 2
            for i in range(4):
                k = 4 * q + i
                TE(lambda e, bk=bk, i=i, k=k, xs=xs, rows=rows: e.transpose(
                    out=PS[bk][:, i * 128:i * 128 + rows], in_=xs[0:rows, k * 128:(k + 1) * 128], identity=IDF[0:rows, 0:rows]),
                   [("xstg", blk % 2), "con"], [("ps", bk)] if i in (0, 3) else [])
            V(lambda e, bk=bk, q=q, blk=blk, rows=rows: e.tensor_copy(
                out=XT[:, 4 * q:4 * q + 4, blk * 128:blk * 128 + rows],
                in_=PS[bk][:, 0:512].rearrange("p (i t) -> p i t", t=128)[:, :, 0:rows]),
              [("ps", bk)], ["xt"])
    P.barrier()

    def rstd_from(psA, nA, psB, nB, dstA, dstB, scale):
        S(lambda e: e.activation(out=dstA, in_=psA, func=AF.Sqrt, bias=EPSC, scale=scale), [("ps", nA), "par"], ["rstd"])
        S(lambda e: e.activation(out=dstB, in_=psB, func=AF.Sqrt, bias=EPSC, scale=scale), [("ps", nB), "par"], ["rstd"])

    ACT2 = WB(PB, 23936).rearrange("p (j t) -> p j t", t=T)
    HT2 = WB(0, 8704).rearrange("p (k t) -> p k t", t=T)
    HTT = WB(PB + 23936, 8704).rearrange("p (k t) -> p k t", t=T)
    RING2 = [WB(PB + 23936 + 2816 * s_, 2816) for s_ in range(3)]
    RG2 = [r[:, 0:2048].rearrange("p (k c) -> p k c", c=128) for r in RING2]
    RU2 = [r[:, 2048:4096].rearrange("p (k c) -> p k c", c=128) for r in RING2]
    RO2 = [r[:, 0:5632].rearrange("p (j c) -> p j c", c=128) for r in RING2]
    SQ2 = [WB(PB + 23936 + 8704 + 256 * i_, 256) for i_ in range(2)]
    SQ2B = [WB(PB + 23936 + 8448 + 544 * i_, 544) for i_ in range(2)]
    TMP2 = [W(8704 + 512 * i_, 512) for i_ in range(4)]
    RSTD2 = W(PB, T)
    RSTD2P = W(PB + 17408, T)
    XF = W(PB, 17408).rearrange("p (k t) -> p k t", t=T)

    def rstd_ln(ps_ap, bank, dst, scale, wkey):
        S(lambda e: e.activation(out=dst, in_=ps_ap, func=AF.Ln, bias=EPSC, scale=scale), [("ps", bank), "par"], [wkey])
        S(lambda e: e.activation(out=dst, in_=dst, func=AF.Exp, scale=-0.5), [wkey], [wkey])

    def ffn(l, wi, wo, gi_pre, gi_post, nxt=True):
        wiv = wi[l].rearrange("(k p) c -> p k c", p=128)
        wov = wo[l].rearrange("(j p) c -> p j c", p=128)
        gpre = GAIN[:, (l * 6 + gi_pre) * 16:(l * 6 + gi_pre) * 16 + 16]
        gpost = GAIN[:, (l * 6 + gi_post) * 16:(l * 6 + gi_post) * 16 + 16]
        DQ(lambda e: e.dma_start(out=xsp, in_=XTf), ["xt"], ["xsp"], key="spill")
        for nt, (t0, n) in enumerate(NTILES):
            b = 5 + nt
            if not state.get("ssq_ready"):
                for k in range(16):
                    sq = SQ2[k % 2]
                    S(lambda e, sq=sq, k=k, t0=t0, n=n: e.activation(out=sq[:, 0:n], in_=XT[:, k, t0:t0 + n], func=AF.Square), ["xt"], [("sq2", k % 2)])
                    mm(PS[b][:, 0:n], ONESB, sq[:, 0:n], k == 0, k == 15, [("sq2", k % 2), "onesb"], [("ps", b)] if k in (0, 15) else [])
            rstd_ln(PS[b][:, 0:n], b, RSTD2[:, t0:t0 + n], 1.0 / D, "rstd2")
        state["ssq_ready"] = False
        for k in range(16):
            V(lambda e, k=k: e.scalar_tensor_tensor(out=HTT[:, k, :], in0=XT[:, k, :], scalar=gpre[:, k:k + 1], in1=RSTD2, op0=ALU.mult, op1=ALU.mult),
              ["xt", "rstd2", "par"], ["htt"])
        V(lambda e: e.tensor_copy(out=HT2[:, :, :], in_=HTT[:, :, :]), ["htt"], ["ht2", "xt"])
        nfirst = 0
        for j in range(NJ):
            s = state["ring"] % 3
            state["ring"] += 1
            xw = ["htt"] if nfirst < 3 else []
            nfirst += 1
            DG(lambda e, s=s, j=j: e.dma_start(out=RG2[s], in_=wiv[:, :, j * 128:(j + 1) * 128]), w=[("ra", s)] + xw, key=("ring", s, 0))
            DG(lambda e, s=s, j=j: e.dma_start(out=RU2[s], in_=wiv[:, :, 5632 + j * 128:5632 + (j + 1) * 128]), w=[("rb", s)] + xw, key=("ring", s, 1))
            for nt, (t0, n) in enumerate(NTILES):
                pidx = (3 * j + nt) % 4
                bg, bu = 2 * pidx, 2 * pidx + 1
                for k in range(16):
                    mm(PS[bg][:, 0:n], RG2[s][:, k, :], HT2[:, k, t0:t0 + n], k == 0, k == 15, [("ra", s), "ht2"], [("ps", bg)] if k in (0, 15) else [])
                for k in range(16):
                    mm(PS[bu][:, 0:n], RU2[s][:, k, :], HT2[:, k, t0:t0 + n], k == 0, k == 15, [("rb", s), "ht2"], [("ps", bu)] if k in (0, 15) else [])
                tm = TMP2[pidx]
                S(lambda e, tm=tm, bg=bg, n=n: e.activation(out=tm[:, 0:n], in_=PS[bg][:, 0:n], func=AF.Silu), [("ps", bg)], [("tmp2", pidx)])
                V(lambda e, tm=tm, bu=bu, j=j, t0=t0, n=n: e.tensor_tensor(out=ACT2[:, j, t0:t0 + n], in0=tm[:, 0:n], in1=PS[bu][:, 0:n], op=ALU.mult),
                  [("tmp2", pidx), ("ps", bu)], [("act", j, nt)])
        for m in range(16):
            s = state["ring"] % 3
            state["ring"] += 1
            DG(lambda e, s=s, m=m: e.dma_start(out=RO2[s], in_=wov[:, :, m * 128:(m + 1) * 128]), w=[("ra", s), ("rb", s)], key=("ring", s, 0))
            for nt, (t0, n) in enumerate(NTILES):
                if nt < 2:
                    b, co = 3 * (m % 2) + nt, 0
                else:
                    b, co = 2, 64 * (m % 2)
                for j in range(NJ):
                    mm(PS[b][:, co:co + n], RO2[s][:, j, :], ACT2[:, j, t0:t0 + n], j == 0, j == NJ - 1, [("ra", s), ("act", j, nt)],
                       [("ps", b)] if j in (0, NJ - 1) else [])
                V(lambda e, m=m, b=b, co=co, t0=t0, n=n: e.tensor_copy(out=XT[:, m, t0:t0 + n], in_=PS[b][:, co:co + n]), [("ps", b)],
                  [("out2", m)] + (["ht2", "xt"] + [("tmp2", i_) for i_ in range(4)] if (m == 0 and nt == 0) else []))
            sqb = SQ2B[m % 2]
            S(lambda e, sqb=sqb, m=m: e.activation(out=sqb[:, 0:T], in_=XT[:, m, :], func=AF.Square), [("out2", m)], [("sq2b", m % 2)])
            for nt, (t0, n) in enumerate(NTILES):
                mm(PS[5 + nt][:, 0:n], ONESB, sqb[:, t0:t0 + n], m == 0, m == 15, [("sq2b", m % 2), "onesb"], [("ps", 5 + nt)] if m in (0, 15) else [])
        P.barrier()
        DQ(lambda e: e.dma_start(out=XF[:, :, :], in_=xsp.rearrange("p (k t) -> p k t", t=T)), ["xsp"], ["xf"], key="fill")
        for nt, (t0, n) in enumerate(NTILES):
            rstd_ln(PS[5 + nt][:, 0:n], 5 + nt, RSTD2P[:, t0:t0 + n], 1.0 / D, "rstd2p")
        V(lambda e: e.tensor_scalar(out=RSTD2P, in0=RSTD2P, scalar1=0.5, scalar2=None, op0=ALU.mult), ["rstd2p"], ["rstd2p"])
        for m in range(16):
            V(lambda e, m=m: e.tensor_tensor(out=XT[:, m, :], in0=XT[:, m, :], in1=RSTD2P, op=ALU.mult), [("out2", m), "rstd2p"], [("out2", m)])
            V(lambda e, m=m: e.scalar_tensor_tensor(out=XT[:, m, :], in0=XT[:, m, :], scalar=gpost[:, m:m + 1], in1=XF[:, m, :],
                                                    op0=ALU.mult, op1=ALU.add), [("out2", m), "xf", "par"], ["xt", ("xtm", m)])
            if nxt:
                sqb = SQ2B[m % 2]
                S(lambda e, sqb=sqb, m=m: e.activation(out=sqb[:, 0:T], in_=XT[:, m, :], func=AF.Square), [("xtm", m)], [("sq2b", m % 2)])
                for nt, (t0, n) in enumerate(NTILES):
                    mm(PS[5 + nt][:, 0:n], ONESB, sqb[:, t0:t0 + n], m == 0, m == 15, [("sq2b", m % 2), "onesb"], [("ps", 5 + nt)] if m in (0, 15) else [])
        state["ssq_ready"] = bool(nxt)

    def load_wr(src3, col0, ncols=256):
        s = state["wr"] % 2
        state["wr"] += 1
        DG(lambda e, s=s: e.dma_start(out=WR[s][:, :, 0:ncols], in_=src3[:, :, col0:col0 + ncols]), w=[("wr", s)], key=("wr", s))
        return s

    def proj_fm(wchunk, wkey, rhs3, rkey, bset):
        for nt, (t0, n) in enumerate(NTILES):
            b = bset[nt]
            for k in range(16):
                mm(PS[b][:, 0:n], wchunk[:, k, :], rhs3[:, k, t0:t0 + n], k == 0, k == 15, [wkey, rkey], [("ps", b)] if k in (0, 15) else [])

    def mixer(l):
        wmv = wmi[l].rearrange("(k p) c -> p k c", p=128)
        gpre = GAIN[:, (l * 6 + 2) * 16:(l * 6 + 2) * 16 + 16]
        gpost = GAIN[:, (l * 6 + 3) * 16:(l * 6 + 3) * 16 + 16]
        DQ(lambda e: e.dma_start(out=xsp, in_=XTf), ["xt"], ["xsp"], key="spill")
        for nt, (t0, n) in enumerate(NTILES):
            b = 5 + nt
            if state.get("ssq_ready"):
                continue
            for k in range(16):
                sq = SQM[k % 2]
                S(lambda e, sq=sq, k=k, t0=t0, n=n: e.activation(out=sq[:, 0:n], in_=XT[:, k, t0:t0 + n], func=AF.Square), ["xt"], [("sqa", k % 2)])
                mm(PS[b][:, 0:n], ONESB, sq[:, 0:n], k == 0, k == 15, [("sqa", k % 2), "onesb"], [("ps", b)] if k in (0, 15) else [])
        state["ssq_ready"] = False
        RS = W(MB, T)
        for nt, (t0, n) in enumerate(NTILES):
            rstd_ln(PS[5 + nt][:, 0:n], 5 + nt, RS[:, t0:t0 + n], 1.0 / D, "rsx")
        for k in range(16):
            V(lambda e, k=k: e.scalar_tensor_tensor(out=HTM[:, k, :], in0=XT[:, k, :], scalar=gpre[:, k:k + 1], in1=RS, op0=ALU.mult, op1=ALU.mult),
              ["xt", "rsx", "par"], ["htm"])
        P.barrier()

        for gi, grp in enumerate(GROUPS):
            NQ, NU, upq, H0, mch0 = grp["NQ"], grp["NU"], grp["upq"], grp["H0"], grp["mch0"]
            if gi > 0:
                P.barrier()
            gla = grp["kind"] == "gla"
            NV = NU * 128
            if gla:
                DG(lambda e: e.dma_start(out=WALR, in_=wmv[:, :, C_A:C_A + 16]), w=["walr"], key="walr")
                DG(lambda e: e.dma_start(out=WUP[0:16, 0:512], in_=wgu[l]), w=["wup"], key="wup")
                for nt, (t0, n) in enumerate(NTILES):
                    for k in range(16):
                        mm(PS[nt][0:16, 0:n], WALR[:, k, :], HTM[:, k, t0:t0 + n], k == 0, k == 15, ["walr", "htm"], [("ps", nt)] if k in (0, 15) else [])
                    S(lambda e, nt=nt, t0=t0, n=n: e.activation(out=ALR[0:16, t0:t0 + n], in_=PS[nt][0:16, 0:n], func=AF.Copy), [("ps", nt)], ["alr"])
            pset = 0
            D1B = [D1, W(MB + 5888, T)]
            KRB = [KR, W(MB + 5888 + 1088, T)]
            wsl = {}
            psc = [0]

            def nextset():
                bs = [3 * psc[0], 3 * psc[0] + 1, 3 * psc[0] + 2]
                psc[0] ^= 1
                return bs

            def stageA(h):
                H = H0 + h
                i2 = h % 2
                D1c, KRc = D1B[i2], KRB[i2]
                al = [("sring", 0, 0), ("sring", 0, 1), ("sbfs", 0)] if i2 == 1 else []
                if gla:
                    bset = nextset()
                    for nt, (t0, n) in enumerate(NTILES):
                        b = bset[nt]
                        mm(PS[b][:, 0:n], WUP[0:16, h * 128:(h + 1) * 128], ALR[0:16, t0:t0 + n], True, True, ["wup", "alr"], [("ps", b)])
                        S(lambda e, b=b, t0=t0, n=n, h=h: e.activation(out=ZG[:, t0:t0 + n], in_=PS[b][:, 0:n], func=AF.Exp,
                                                                     bias=NEGB[:, l * 4 + h:l * 4 + h + 1], scale=-1.0), [("ps", b), "par"], ["zg"])
                    S(lambda e: e.activation(out=ZG, in_=ZG, func=AF.Ln, bias=ONEC, scale=1.0), ["zg", "par"], ["zg"])
                    V(lambda e: e.tensor_scalar(out=ZG, in0=ZG, scalar1=-1.0 / 16.0, scalar2=None, op0=ALU.mult), ["zg"], ["zg"])
                else:
                    if h % 2 == 0:
                        wsl["f"] = load_wr(wmv, C_HF + (H0 + h) * 128)
                    sf = wsl["f"]
                    bset = nextset()
                    proj_fm(WR[sf][:, :, (h % 2) * 128:(h % 2) * 128 + 128], ("wr", sf), HTM, "htm", bset)
                    for nt, (t0, n) in enumerate(NTILES):
                        b = bset[nt]
                        S(lambda e, b=b, t0=t0, n=n: e.activation(out=ZG[:, t0:t0 + n], in_=PS[b][:, 0:n], func=AF.Sigmoid), [("ps", b)], ["zg"])
                        S(lambda e, b=b, t0=t0, n=n: e.activation(out=KRc[:, t0:t0 + n], in_=PS[b][:, 0:n], func=AF.Sigmoid, scale=-1.0), [("ps", b)], [("kr", i2)] + al)
                    V(lambda e, H=H: e.tensor_scalar(out=ZG, in0=ZG, scalar1=OML[:, l * 8 + H:l * 8 + H + 1], scalar2=LB[:, l * 8 + H:l * 8 + H + 1],
                                                     op0=ALU.mult, op1=ALU.add), ["zg", "par"], ["zg"])
                    S(lambda e: e.activation(out=ZG, in_=ZG, func=AF.Ln), ["zg"], ["zg"])
                V(lambda e: e.tensor_tensor_scan(out=D1c[:, 0:1024], data0=ONEC.to_broadcast([128, 1024]), data1=ZG[:, 0:1024], initial=0.0,
                                                 op0=ALU.mult, op1=ALU.add), ["zg", "par"], [("d1", i2)] + al)
                V(lambda e: e.tensor_tensor_scan(out=D1c[:, 1024:1088], data0=RMASK, data1=ZG[:, 1024:1088], initial=0.0,
                                                 op0=ALU.mult, op1=ALU.add), ["zg", "con"], [("d1", i2)] + al)

            def stageB(h):
                H = H0 + h
                i2 = h % 2
                D1c, KRc = D1B[i2], KRB[i2]
                dk, kk = ("d1", i2), ("kr", i2)
                alB = [("sring", 0, 0), ("sring", 0, 1), ("sbfs", 0)] if i2 == 1 else []
                Gv = D1c[:, 0:1024].rearrange("p (c t) -> p c t", t=64)
                Gs = D1c[:, 1024:1088].rearrange("p (b t) -> p b t", t=4)
                V(lambda e, h=h: e.tensor_copy(out=TM[:, h, :], in_=Gv[:, :, 31]), [dk] + alB, ["tab"])
                V(lambda e, h=h: e.memset(TPV[:, h, 0:1], 0.0), (), ["tab"])
                V(lambda e, h=h: e.tensor_copy(out=TPV[:, h, 1:16], in_=Gv[:, 0:15, 63]), [dk] + alB, ["tab"])
                V(lambda e, h=h: e.tensor_tensor(out=EE[:, h, 0, :], in0=TM[:, h, :], in1=TPV[:, h, :], op=ALU.subtract), ["tab"], ["tab"])
                V(lambda e, h=h: e.tensor_tensor(out=EE[:, h, 1, :], in0=Gv[:, :, 63], in1=TPV[:, h, :], op=ALU.subtract), ["tab", dk] + alB, ["tab"])
                V(lambda e, h=h: e.tensor_tensor(out=EE[:, h, 2, :], in0=Gv[:, :, 63], in1=TM[:, h, :], op=ALU.subtract), ["tab", dk] + alB, ["tab"])
                S(lambda e, h=h: e.activation(out=EE[:, h, :, :], in_=EE[:, h, :, :], func=AF.Exp), ["tab"], ["tab"])
                S(lambda e, h=h: e.activation(out=ELS[:, h, :], in_=Gs[:, :, 3], func=AF.Exp), [dk] + alB, ["tab"])
                V(lambda e, h=h: e.tensor_tensor(out=Gv, in0=Gv, in1=TM[:, h, :].unsqueeze(2).to_broadcast([128, 16, 64]), op=ALU.subtract),
                  [dk, "tab"] + alB, [dk] + alB)
                S(lambda e: e.activation(out=EK, in_=D1c, func=AF.Exp, scale=-1.0), [dk] + alB, ["ek"])
                S(lambda e: e.activation(out=D1c, in_=D1c, func=AF.Exp), [dk] + alB, [dk] + alB)
                if gla:
                    if h % 2 == 0:
                        wsl["k"] = load_wr(wmv, C_K + h * 128)
                    sk = wsl["k"]
                    bset = nextset()
                    proj_fm(WR[sk][:, :, (h % 2) * 128:(h % 2) * 128 + 128], ("wr", sk), HTM, "htm", bset)
                    for nt, (t0, n) in enumerate(NTILES):
                        b = bset[nt]
                        V(lambda e, b=b, t0=t0, n=n, h=h: e.tensor_tensor(out=KT[:, h, t0:t0 + n], in0=PS[b][:, 0:n], in1=EK[:, t0:t0 + n], op=ALU.mult),
                          [("ps", b), "ek"], ["kt"])
                else:
                    V(lambda e, h=h, H=H: e.scalar_tensor_tensor(out=KT[:, h, :], in0=KRc, scalar=OML[:, l * 8 + H:l * 8 + H + 1], in1=EK,
                                                                 op0=ALU.mult, op1=ALU.mult), [kk, "ek", "par"] + alB, ["kt"])
                if h % 2 == 0:
                    wsl["q"] = load_wr(wmv, (C_Q + h * 128) if gla else (C_HQ + (H0 + h) * 128))
                sq_ = wsl["q"]
                bset = nextset()
                proj_fm(WR[sq_][:, :, (h % 2) * 128:(h % 2) * 128 + 128], ("wr", sq_), HTM, "htm", bset)
                for nt, (t0, n) in enumerate(NTILES):
                    b = bset[nt]
                    if gla:
                        V(lambda e, b=b, t0=t0, n=n, h=h: e.scalar_tensor_tensor(out=QT[:, h, t0:t0 + n], in0=PS[b][:, 0:n], scalar=128.0 ** -0.5,
                                                                                 in1=D1c[:, t0:t0 + n], op0=ALU.mult, op1=ALU.mult), [("ps", b), dk] + alB, ["qt"])
                    else:
                        S(lambda e, b=b, t0=t0, n=n: e.activation(out=EK[:, t0:t0 + n], in_=PS[b][:, 0:n], func=AF.Silu), [("ps", b), "kt"], ["ek"])
                        V(lambda e, t0=t0, n=n, h=h: e.scalar_tensor_tensor(out=QT[:, h, t0:t0 + n], in0=EK[:, t0:t0 + n], scalar=128.0 ** -0.5,
                                                                            in1=D1c[:, t0:t0 + n], op0=ALU.mult, op1=ALU.mult), ["ek", dk] + alB, ["qt"])
                KH = EK[:, 0:544].bitcast(BF16)
                V(lambda e, h=h: e.tensor_tensor(out=KH[:, 0:1024].rearrange("p (c t) -> p c t", t=64),
                                                 in0=KT[:, h, 0:1024].rearrange("p (c t) -> p c t", t=64),
                                                 in1=EE[:, h, 2, :].unsqueeze(2).to_broadcast([128, 16, 64]), op=ALU.mult), ["kt", "tab", "qt"], ["ek"])
                V(lambda e, h=h: e.tensor_tensor(out=KH[:, 1024:1088].rearrange("p (b t) -> p b t", t=4),
                                                 in0=KT[:, h, 1024:1088].rearrange("p (b t) -> p b t", t=4),
                                                 in1=ELS[:, h, :].unsqueeze(2).to_broadcast([128, 16, 4]), op=ALU.mult), ["kt", "tab", "qt"], ["ek"])
                pb6 = PS[6][:].bitcast(BF16)
                pb7 = PS[7][:].bitcast(BF16)
                for blk in range(8):
                    TE(lambda e, blk=blk: e.transpose(out=pb6[:, blk * 128:(blk + 1) * 128], in_=KH[:, blk * 128:(blk + 1) * 128], identity=IDB),
                       ["ek", "idb"], [("ps", 6)] if blk in (0, 7) else [])
                TE(lambda e: e.transpose(out=pb7[0:64, 0:128], in_=KH[:, 1024:1088], identity=IDB), ["ek", "idb"], [("ps", 7)])
                S(lambda e, h=h: e.activation(out=KTOK[:, 0:8, h * 128:(h + 1) * 128], in_=pb6[:, 0:1024].rearrange("p (b c) -> p b c", c=128), func=AF.Copy),
                  [("ps", 6)], ["ktok"])
                S(lambda e, h=h: e.activation(out=KTOK[0:64, 8, h * 128:(h + 1) * 128], in_=pb7[0:64, 0:128], func=AF.Copy), [("ps", 7)], ["ktok"])

            stageA(0)
            for h in range(NQ):
                if h + 1 < NQ:
                    stageA(h + 1)
                stageB(h)
            pset = psc[0]
            gc0 = C_R if gla else C_HG + H0 * 128
            for u in range(NU):
                if u % 2 == 0:
                    sg = load_wr(wmv, gc0 + u * 128)
                bset = [3 * pset, 3 * pset + 1, 3 * pset + 2]
                pset ^= 1
                proj_fm(WR[sg][:, :, (u % 2) * 128:(u % 2) * 128 + 128], ("wr", sg), HTM, "htm", bset)
                for nt, (t0, n) in enumerate(NTILES):
                    b = bset[nt]
                    S(lambda e, b=b, t0=t0, n=n, u=u: e.activation(out=GATE[:, u, t0:t0 + n], in_=PS[b][:, 0:n], func=AF.Silu), [("ps", b)], ["gate"])
            vc0 = C_V if gla else C_HI + H0 * 128
            cnt = 0
            for vt in range(NV // 256):
                sv = load_wr(wmv, vc0 + vt * 256)
                for blk in range(9):
                    rows = 128 if blk < 8 else 64
                    b = 6 + cnt % 2
                    off = 0
                    first = True
                    cnt += 1
                    for k in range(16):
                        mm(PS[b][0:rows, off:off + 256], HTM[:, k, blk * 128:blk * 128 + rows], WR[sv][:, k, :], (k == 0 and first), (k == 15),
                           [("wr", sv), "htm"], [("ps", b)] if ((k == 0 and first) or k == 15) else [])
                    S(lambda e, b=b, off=off, rows=rows, blk=blk, vt=vt: e.activation(out=VTOK[0:rows, blk, vt * 256:(vt + 1) * 256],
                                                                                    in_=PS[b][0:rows, off:off + 256], func=AF.Copy), [("ps", b)], ["vtok"])
            Sv = SST[:, 0:NV].rearrange("p (u v) -> p u v", v=128)
            if gla:
                pin_v = pin_gla[l].rearrange("h d x -> d h x")
                pst_v = pst_gla[l].rearrange("h d x -> d h x")
                S4 = SST[:, 0:NV].rearrange("p (h x) -> p h x", x=256)
            else:
                pin_v = pin_hg[l, H0:H0 + 4].rearrange("h d x -> d h x")
                pst_v = pst_hg[l, H0:H0 + 4].rearrange("h d x -> d h x")
                S4 = SST[:, 0:NV].rearrange("p (h x) -> p h x", x=128)

            def hview(ap2, n):
                return ap2.unsqueeze(2).to_broadcast([128, NQ, upq * n])

            def norm_gate(bo, bn, t0):
                NUc = NU * 64
                S(lambda e: e.activation(out=SQN[:, 0:NUc], in_=PS[bo][:, 0:NUc], func=AF.Square), [("ps", bo)], ["sqn"])
                if gla:
                    for h in range(4):
                        mm(PS[bn][:, 256 + h * 64:256 + (h + 1) * 64], ONESB, SQN[:, (2 * h) * 64:(2 * h + 1) * 64], h == 0, False,
                           ["sqn", "onesb"], [("ps", bn)] if h == 0 else [])
                        mm(PS[bn][:, 256 + h * 64:256 + (h + 1) * 64], ONESB, SQN[:, (2 * h + 1) * 64:(2 * h + 2) * 64], False, h == 3,
                           ["sqn", "onesb"], [("ps", bn)] if h == 3 else [])
                    dv = 256.0
                else:
                    mm(PS[bn][:, 256:512], ONESB, SQN[:, 0:256], True, True, ["sqn", "onesb"], [("ps", bn)])
                    dv = 128.0
                rstd_ln(PS[bn][:, 256:512], bn, RSTDN[:, 0:256], 1.0 / dv, "rstdn")
                V(lambda e: e.tensor_tensor(out=T1[:, 0:NUc].rearrange("p (h c t) -> p h c t", h=NQ, c=upq),
                                            in0=PS[bo][:, 0:NUc].rearrange("p (h c t) -> p h c t", h=NQ, c=upq),
                                            in1=RSTDN[:, 0:256].rearrange("p (h t) -> p h t", t=64).unsqueeze(2).to_broadcast([128, NQ, upq, 64]),
                                            op=ALU.mult), [("ps", bo), "rstdn"], ["t1"])
                if gla:
                    T1v = T1[:, 0:NUc].rearrange("p (h c t) -> p h c t", h=4, c=2)
                    Mv = MERGED[:, 0:8, t0:t0 + 64].rearrange("p (h c) t -> p h c t", c=2)
                    Gtv = GATE[:, 0:8, t0:t0 + 64].rearrange("p (h c) t -> p h c t", c=2)
                    for c in range(2):
                        V(lambda e, c=c: e.scalar_tensor_tensor(out=Mv[:, :, c, :], in0=T1v[:, :, c, :], scalar=GNW[:, l * 2 + c:l * 2 + c + 1],
                                                                in1=Gtv[:, :, c, :], op0=ALU.mult, op1=ALU.mult), ["t1", "gate", "par"], ["merged"])
                else:
                    V(lambda e: e.scalar_tensor_tensor(out=MERGED[:, mch0:mch0 + 4, t0:t0 + 64], in0=T1[:, 0:256].rearrange("p (h t) -> p h t", t=64),
                                                       scalar=HNW[:, l:l + 1], in1=GATE[:, 0:4, t0:t0 + 64], op0=ALU.mult, op1=ALU.mult),
                      ["t1", "gate", "par"], ["merged"])

            nPb = NV // 512
            def cvars(c):
                par = c % 2
                bP = [4 * par + 2, 4 * par + 3][:nPb]
                return par, bP, c // 2, 64 * (c % 2)

            def state_P(c):
                par, bP, blk, r0 = cvars(c)
                for u in range(NU):
                    hq = u // upq
                    b = bP[(u * 128) // 512]
                    o = (u * 128) % 512
                    fi = (o == 0)
                    la = (o == 384) or (u == NU - 1)
                    mm(PS[b][:, o:o + 128], KTOK[r0:r0 + 64, blk, hq * 128:(hq + 1) * 128], VTOK[r0:r0 + 64, blk, u * 128:(u + 1) * 128], fi, la,
                       ["ktok", "vtok"], [("ps", b)] if (fi or la) else [])

            def state_U(c):
                par, bP, blk, r0 = cvars(c)
                X = upq * 128
                for h in range(NQ):
                    b = bP[(h * X) // 512]
                    o = (h * X) % 512
                    V(lambda e, h=h, b=b, o=o, c=c, X=X: e.scalar_tensor_tensor(out=SST[:, h * X:(h + 1) * X], in0=SST[:, h * X:(h + 1) * X],
                                                                                 scalar=EE[:, h, 1, c:c + 1], in1=PS[b][:, o:o + X],
                                                                                 op0=ALU.mult, op1=ALU.add), ["sst", "tab", "sbf", ("ps", b)], ["sst"])

            if fused:
                V(lambda e: e.memset(SST[:, 0:NV], 0.0), (), ["sst"])
                for c in range(16):
                    state_P(c)
                    state_U(c)
                bn = nc.dram_tensor("bnc%d%d" % (l, gi), [128, NV], F32).ap()
                gt = nc.dram_tensor("gth%d%d" % (l, gi), [256, NV], F32).ap()
                DG(lambda e: e.dma_start(out=bn, in_=SST[:, 0:NV]), ["sst"], [("bnc", l, gi)], key=("bn", gi))
                DG(lambda e: e.collective_compute("AllGather", ALU.bypass, replica_groups=[[0, 1], [2, 3], [4, 5], [6, 7]],
                                                  ins=[bn.opt()], outs=[gt.opt()]), [("bnc", l, gi)], [("gth", l, gi)], key=("cc", 0))

            def init_prompt_state():
                if fused:
                    DQ(lambda e: e.dma_start(out=SST[:, 0:NV], in_=gt[0:128, :]), [("gth", l, gi)], ["sst"], key="pin")
                    V(lambda e: e.tensor_scalar(out=SST[:, 0:NV], in0=SST[:, 0:NV], scalar1=MASKC, scalar2=None, op0=ALU.mult), ["sst", "par"], ["sst"])
                else:
                    DQ(lambda e: e.dma_start(out=S4, in_=pin_v), w=["sst"], key="pin")
            ss_ = slice(1024, 1088)
            bsc, bo = 0, 1
            for h in range(NQ):
                mm(PS[bsc][0:64, h * 64:(h + 1) * 64], KT[:, h, ss_], QT[:, h, ss_], h == 0, h == NQ - 1, ["kt", "qt"],
                   [("ps", bsc)] if h in (0, NQ - 1) else [])
            V(lambda e: e.tensor_tensor(out=SCSS[0:64, 0:NQ * 64].rearrange("p (h t) -> p h t", t=64),
                                        in0=PS[bsc][0:64, 0:NQ * 64].rearrange("p (h t) -> p h t", t=64),
                                        in1=SMASK[0:64, :].unsqueeze(1).to_broadcast([64, NQ, 64]), op=ALU.mult), [("ps", bsc), "con"], ["scss"])
            for u in range(NU):
                hq = u // upq
                mm(PS[bo][:, u * 64:(u + 1) * 64], VTOK[0:64, 8, u * 128:(u + 1) * 128], SCSS[0:64, hq * 64:(hq + 1) * 64], u == 0, False,
                   ["vtok", "scss"], [("ps", bo)] if u == 0 else [])
            SRB = [SRING, W(MB + 11648, 2048)]
            SBB = [SBFS, WB(MB + 11648 + 2048, 1024)]
            for bp in range(8):
                b0 = 2 * bp
                i2 = bp % 2
                xw_s = [("wr", 0)] if i2 == 1 else []
                xw_b = [("wr", 1)] if i2 == 1 else []
                SRc, SBc = SRB[i2], SBB[i2]
                SR4 = SRc[:, 0:2 * NV].rearrange("p (b u v) -> p b u v", b=2, v=128)
                SB4 = SBc[:, 0:2 * NV].rearrange("p (b u v) -> p b u v", b=2, v=128)
                xx = 256 if gla else 128
                SRd = SRc[:, 0:2 * NV].rearrange("p (b h x) -> p b h x", b=2, x=xx)
                sin_v, sout_v = [], []
                for bi in range(2):
                    if gla:
                        sin_v.append(sgla[l, b0 + bi].rearrange("h d x -> d h x"))
                        sout_v.append(sst_gla[l, b0 + bi].rearrange("h d x -> d h x"))
                    else:
                        sin_v.append(shg[l, b0 + bi, H0:H0 + 4].rearrange("h d x -> d h x"))
                        sout_v.append(sst_hg[l, b0 + bi, H0:H0 + 4].rearrange("h d x -> d h x"))
                kr = [("sring", i2, 0), ("sring", i2, 1)]
                for bi in range(2):
                    DQ(lambda e, bi=bi: e.dma_start(out=SRd[:, bi], in_=sin_v[bi]), w=[("sring", i2, bi)] + xw_s, key=("sin", i2, bi))
                S(lambda e: e.activation(out=SBc[:, 0:2 * NV], in_=SRc[:, 0:2 * NV], func=AF.Copy), kr + xw_s, [("sbfs", i2)] + xw_b)
                for bi in range(2):
                    bb_ = b0 + bi
                    for u in range(NU):
                        hq = u // upq
                        lastmm = (bp == 7 and bi == 1 and u == NU - 1)
                        mm(PS[bo][:, u * 64 + 4 * bb_:u * 64 + 4 * bb_ + 4], SB4[:, bi, u, :], QT[:, hq, 1024 + 4 * bb_:1024 + 4 * bb_ + 4], False, lastmm,
                           [("sbfs", i2), "qt"] + xw_b, [("ps", bo)] if lastmm else [])
                KW = NQ * 128
                X = upq * 128
                V(lambda e, b0=b0: e.tensor_tensor(out=VM[0:64, 0:2 * KW].rearrange("p (b x) -> p b x", b=2),
                                                   in0=KTOK[0:64, 8, 0:KW].unsqueeze(1).to_broadcast([64, 2, KW]),
                                                   in1=BSEL[0:64, b0:b0 + 2].unsqueeze(2).to_broadcast([64, 2, KW]), op=ALU.mult), ["ktok", "con"], ["vm"])
                KM3 = VM[0:64, 0:2 * KW].rearrange("p (b x) -> p b x", b=2)
                pb0 = 4 if (gla or bp % 2 == 0) else 6
                for bi in range(2):
                    for h in range(NQ):
                        f = bi * NQ + h
                        b = pb0 + (f * X) // 512
                        o = (f * X) % 512
                        mm(PS[b][:, o:o + X], KM3[:, bi, h * 128:(h + 1) * 128], VTOK[0:64, 8, h * X:(h + 1) * X], o == 0, True,
                           ["vm", "vtok"], [("ps", b)])
                for bi in range(2):
                    for h in range(NQ):
                        f = bi * NQ + h
                        b = pb0 + (f * X) // 512
                        o = (f * X) % 512
                        V(lambda e, bi=bi, h=h, b=b, o=o: e.scalar_tensor_tensor(out=SRc[:, bi * NV + h * X:bi * NV + (h + 1) * X],
                                                                                 in0=SRc[:, bi * NV + h * X:bi * NV + (h + 1) * X],
                                                                                 scalar=ELS[:, h, b0 + bi:b0 + bi + 1], in1=PS[b][:, o:o + X],
                                                                                 op0=ALU.mult, op1=ALU.add),
                          [("sring", i2, bi), "tab", ("sbfs", i2), ("ps", b)] + xw_s, [("sring", i2, bi)] + xw_s)
                for bi in range(2):
                    DG(lambda e, bi=bi: e.dma_start(out=sout_v[bi], in_=SRd[:, bi]), [("sring", i2, bi)] + xw_s, key=("sout", i2, bi))
            norm_gate(bo, bsc, 1024)
            init_prompt_state()
            for c in range(16):
                par = c % 2
                bsc, bo = 4 * par, 4 * par + 1
                bP = [4 * par + 2, 4 * par + 3][:nPb]
                blk, hf = c // 2, c % 2
                r0 = 64 * hf
                cs = slice(64 * c, 64 * c + 64)
                state_P(c)
                for h in range(NQ):
                    mm(PS[bsc][r0:r0 + 64, h * 64:(h + 1) * 64], KT[:, h, cs], QT[:, h, cs], h == 0, h == NQ - 1, ["kt", "qt"],
                       [("ps", bsc)] if h in (0, NQ - 1) else [])
                V(lambda e, bsc=bsc, r0=r0: e.tensor_tensor(out=SCS[r0:r0 + 64, 0:NQ * 64].rearrange("p (h t) -> p h t", t=64),
                                                           in0=PS[bsc][r0:r0 + 64, 0:NQ * 64].rearrange("p (h t) -> p h t", t=64),
                                                           in1=CMASK[r0:r0 + 64, :].unsqueeze(1).to_broadcast([64, NQ, 64]), op=ALU.mult),
                  [("ps", bsc), "con"], ["scs"])
                for h in range(NQ):
                    X = upq * 128
                    S(lambda e, h=h, c=c, X=X: e.activation(out=SBF[:, h * X:(h + 1) * X], in_=SST[:, h * X:(h + 1) * X], func=AF.Copy,
                                                            scale=EE[:, h, 0, c:c + 1]), ["sst", "tab"], ["sbf"])
                for u in range(NU):
                    hq = u // upq
                    mm(PS[bo][:, u * 64:(u + 1) * 64], VTOK[r0:r0 + 64, blk, u * 128:(u + 1) * 128], SCS[r0:r0 + 64, hq * 64:(hq + 1) * 64],
                       u == 0, False, ["vtok", "scs"], [("ps", bo)] if u == 0 else [])
                for u in range(NU):
                    hq = u // upq
                    mm(PS[bo][:, u * 64:(u + 1) * 64], SBF[:, u * 128:(u + 1) * 128], QT[:, hq, cs], False, u == NU - 1, ["sbf", "qt"],
                       [("ps", bo)] if u == NU - 1 else [])
                state_U(c)
                if c > 0:
                    pp = (c - 1) % 2
                    norm_gate(4 * pp + 1, 4 * pp, 64 * (c - 1))
            norm_gate(5, 4, 64 * 15)
            DQ(lambda e: e.dma_start(out=pst_v, in_=S4), ["sst"], key="pst")

        P.barrier()
        DQ(lambda e: e.dma_start(out=XTf, in_=xsp), ["xsp"], ["xt"], key="fill")
        wov = wmo[l].rearrange("(k p) c -> p k c", p=128)
        for mp in range(8):
            so = load_wr(wov, mp * 256)
            for ci in range(2):
                m = 2 * mp + ci
                bset = [3 * (m % 2), 3 * (m % 2) + 1, 3 * (m % 2) + 2]
                proj_fm(WR[so][:, :, ci * 128:(ci + 1) * 128], ("wr", so), MERGED, "merged", bset)
                for nt, (t0, n) in enumerate(NTILES):
                    V(lambda e, m=m, nt=nt, t0=t0, n=n, bset=bset: e.tensor_copy(out=OUTM[:, m, t0:t0 + n], in_=PS[bset[nt]][:, 0:n]),
                      [("ps", bset[nt])], [("outm", m)])
        for nt, (t0, n) in enumerate(NTILES):
            b = 6 if nt < 2 else 7
            o = 0 if nt != 1 else 0
            for m in range(16):
                sq = SQN if m % 2 == 0 else SCS
                S(lambda e, sq=sq, m=m, t0=t0, n=n: e.activation(out=sq[:, 0:n], in_=OUTM[:, m, t0:t0 + n], func=AF.Square), [("outm", m)], [("sqo", m % 2)])
                bb_ = [0, 1, 2][nt]
                mm(PS[bb_][:, 0:n], ONESB, sq[:, 0:n], m == 0, m == 15, [("sqo", m % 2), "onesb"], [("ps", bb_)] if m in (0, 15) else [])
        RS2 = W(MB + 8704, T)
        for nt, (t0, n) in enumerate(NTILES):
            rstd_ln(PS[nt][:, 0:n], nt, RS2[:, t0:t0 + n], 1.0 / D, "rs2")
        for m in range(16):
            V(lambda e, m=m: e.tensor_tensor(out=OUTM[:, m, :], in0=OUTM[:, m, :], in1=RS2, op=ALU.mult), [("outm", m), "rs2"], [("outm", m)])
            V(lambda e, m=m: e.scalar_tensor_tensor(out=XT[:, m, :], in0=OUTM[:, m, :], scalar=gpost[:, m:m + 1], in1=XT[:, m, :],
                                                    op0=ALU.mult, op1=ALU.add), [("outm", m), "xt", "par"], ["xt", ("xtm", m)])
            for nt, (t0, n) in enumerate(NTILES):
                sq = SQN if (3 * m + nt) % 2 == 0 else SCS
                S(lambda e, sq=sq, m=m, t0=t0, n=n: e.activation(out=sq[:, 0:n], in_=XT[:, m, t0:t0 + n], func=AF.Square), [("xtm", m)], [("sqo", (3 * m + nt) % 2)])
                mm(PS[5 + nt][:, 0:n], ONESB, sq[:, 0:n], m == 0, m == 15, [("sqo", (3 * m + nt) % 2), "onesb"], [("ps", 5 + nt)] if m in (0, 15) else [])
        state["ssq_ready"] = True
        P.barrier()

    nst = 0
    for l in range(2):
        if nst < stage:
            ffn(l, w1i, w1o, 0, 1)
            P.barrier()
        nst += 1
        if nst < stage:
            mixer(l)
        nst += 1
        if nst < stage:
            ffn(l, w2i, w2o, 4, 5, nxt=(l == 0))
            P.barrier()
        nst += 1

    YST = [W(PB, 2048), W(PB + 2048, 2048)]
    for blk in range(9):
        rows = 128 if blk < 8 else 64
        ys = YST[blk % 2]
        for q in range(4):
            bk = (blk * 4 + q) % 2
            for i in range(4):
                k = 4 * q + i
                TE(lambda e, bk=bk, i=i, k=k, blk=blk, rows=rows: e.transpose(out=PS[bk][0:rows, i * 128:(i + 1) * 128],
                                                                             in_=XT[:, k, blk * 128:blk * 128 + rows], identity=IDF),
                   ["xt", "con"], [("ps", bk)] if i in (0, 3) else [])
            V(lambda e, bk=bk, q=q, ys=ys, rows=rows: e.tensor_copy(out=ys[0:rows, q * 512:(q + 1) * 512], in_=PS[bk][0:rows, 0:512]),
              [("ps", bk)], [("yst", blk % 2)])
        DQ(lambda e, ys=ys, blk=blk, rows=rows: e.dma_start(out=y[blk * 128:blk * 128 + rows, :], in_=ys[0:rows, :]), [("yst", blk % 2)],
           key=("ys", blk % 2))
    P.emit()
    return nc, P


def make_consts():
    c = np.zeros((128, 336), np.float32)
    c[:, 0:128] = np.eye(128, dtype=np.float32)
    p = np.arange(128)[:, None]
    t = np.arange(64)[None, :]
    c[:, 128:192] = ((p % 64) <= t).astype(np.float32)
    c[:, 192:256] = (((p % 64) // 4 == t // 4) & ((p % 64) <= t)).astype(np.float32)
    b = np.arange(16)[None, :]
    c[:, 256:272] = ((p % 64) // 4 == b).astype(np.float32)
    c[:, 272:336] = (np.arange(64)[None, :] % 4 != 0).astype(np.float32)
    return c


_CACHE = {}


def _get_nc(stage=99):
    if stage not in _CACHE:
        _CACHE[stage] = build(stage)[0]
    return _CACHE[stage]


def kernel(**inputs):
    f = lambda a: np.ascontiguousarray(np.asarray(a, dtype=np.float32))
    xp = f(inputs["x_prompt"])
    xs = f(inputs["x_sample"])
    sg = f(inputs["state_gla"])
    sh = f(inputs["state_hgrn"])
    shared = {k: f(inputs[k]) for k in ("norm_gains", "ffn1_w_in", "ffn1_w_out", "ffn2_w_in", "ffn2_w_out", "mix_w_in",
                                        "gla_w_gate_up", "gla_b_gate", "gla_norm", "hgrn_gamma", "hgrn_norm", "mix_w_out")}
    cst = make_consts()
    zg = np.zeros((2, 4, 128, 256), np.float32)
    zh = np.zeros((2, 8, 128, 128), np.float32)
    in_maps = []
    for c in range(NCORES):
        s, half = c // 2, c % 2
        xin = np.concatenate([xp[s, 1024 * half:1024 * half + 1024], xs[16 * c:16 * c + 16].reshape(64, D)], axis=0)
        m = dict(shared)
        m["xin"] = np.ascontiguousarray(xin)
        m["sgla"] = np.ascontiguousarray(sg[:, 16 * c:16 * c + 16])
        m["shg"] = np.ascontiguousarray(sh[:, 16 * c:16 * c + 16])
        m["pin_gla"] = zg
        m["pin_hg"] = zh
        m["consts"] = cst
        m["coremask"] = np.full((128, 1), float(half), np.float32)
        in_maps.append(m)
    nc = _get_nc()
    r2 = run_bass_kernel_spmd(nc, in_maps, core_ids=list(range(NCORES))).results
    y_prompt = np.zeros((4, 2048, D), np.float32)
    y_sample = np.zeros((128, 4, D), np.float32)
    st_gp = np.zeros((2, 4, 4, 128, 256), np.float32)
    st_hp = np.zeros((2, 4, 8, 128, 128), np.float32)
    st_gs = np.zeros((2, 128, 4, 128, 256), np.float32)
    st_hs = np.zeros((2, 128, 8, 128, 128), np.float32)
    for c in range(NCORES):
        s, half = c // 2, c % 2
        r = r2[c]
        y_prompt[s, 1024 * half:1024 * half + 1024] = r["y"][0:1024]
        y_sample[16 * c:16 * c + 16] = r["y"][1024:1088].reshape(16, 4, D)
        st_gs[:, 16 * c:16 * c + 16] = r["sst_gla"]
        st_hs[:, 16 * c:16 * c + 16] = r["sst_hg"]
        if half == 1:
            st_gp[:, s] = r["pst_gla"]
            st_hp[:, s] = r["pst_hg"]
    return (y_prompt, y_sample, st_gp, st_hp, st_gs, st_hs)
```

```python
import numpy as np
import concourse.bass as bass
import concourse.mybir as mybir
from concourse.bass_utils import run_bass_kernel_spmd

F32 = mybir.dt.float32
BF16 = mybir.dt.bfloat16
AF = mybir.ActivationFunctionType
ALU = mybir.AluOpType

D = 2048
T = 1088
NJ = 44
NCORES = 8
EPS = 1e-6


class _Rec:
    def __init__(self):
        self.calls = []

    def __getattr__(self, name):
        def call(*a, **kw):
            self.calls.append((name, a, kw))
            return self
        return call


class Prog:
    ENGS = ("tensor", "vector", "scalar", "gpsimd", "sync")

    def __init__(self, nc):
        self.nc = nc
        self.ops = []
        self.last_w = {}
        self.readers = {}
        self.dma_keys = {}
        self.bar = []
        self.last_eng = {}

    def barrier(self):
        self.bar = list(self.last_eng.values()) + list(self.dma_keys.values())

    def op(self, eng, fn, reads=(), writes=(), dma=None):
        i = len(self.ops)
        if eng != "tensor":
            extra = [k for k in reads if isinstance(k, tuple) and k[0] == "ps" and k not in writes]
            if extra:
                writes = list(writes) + extra
        deps = set((b, "raw") for b in self.bar)
        for k in reads:
            if k in self.last_w:
                deps.add((self.last_w[k], "raw"))
        for k in writes:
            if k in self.last_w:
                deps.add((self.last_w[k], "waw"))
            for r in self.readers.get(k, {}).values():
                deps.add((r, "war"))
        for k in writes:
            self.last_w[k] = i
            self.readers[k] = {}
        for k in reads:
            self.readers.setdefault(k, {})[eng if dma is None else ("dma", i)] = i
        if dma is not None:
            prev = self.dma_keys.get(dma)
            if prev is not None:
                deps.add((prev, "waw"))
            self.dma_keys[dma] = i
        else:
            self.last_eng[eng] = i
        rec = _Rec()
        fn(rec)
        assert len(rec.calls) == 1, rec.calls
        self.ops.append(dict(eng=eng, call=rec.calls[0], deps=deps, dma=dma, sig=False))
        return i

    def emit(self):
        nc = self.nc
        ops = self.ops
        for i, o in enumerate(ops):
            best = {}
            nn = []
            for (p, kind) in o["deps"]:
                po = ops[p]
                if po["dma"] is not None:
                    nn.append(p)
                    continue
                if po["eng"] == o["eng"] and o["dma"] is None:
                    if o["eng"] == "tensor" or kind == "war":
                        continue
                e2 = po["eng"]
                if e2 not in best or best[e2] < p:
                    best[e2] = p
            o["need"] = nn + list(best.values())
            for p in best.values():
                ops[p]["sig"] = True
        cnt = {e: 0 for e in self.ENGS}
        dcnt = {}
        for o in ops:
            if o["dma"] is not None:
                dcnt[o["dma"]] = dcnt.get(o["dma"], 0) + (1 if (isinstance(o["dma"], tuple) and o["dma"][0] == "cc") else 16)
                o["val"] = dcnt[o["dma"]]
            elif o["sig"]:
                cnt[o["eng"]] += 1
                o["val"] = cnt[o["eng"]]
        sems = {e: nc.alloc_semaphore("s_" + e) for e in self.ENGS}
        dsems = {k: nc.alloc_semaphore("d_%d" % n) for n, k in enumerate(dcnt)}
        self.stats = dict(cnt=dict(cnt), ndma=len(dcnt), nops=len(ops))

        def run(ename, eng):
            waited = {}
            for o in ops:
                if o["eng"] != ename:
                    continue
                for p in o["need"]:
                    po = ops[p]
                    if po["dma"] is not None:
                        s = dsems[po["dma"]]
                        key = ("d", po["dma"])
                    else:
                        s = sems[po["eng"]]
                        key = ("e", po["eng"])
                    v = po["val"]
                    if waited.get(key, 0) >= v:
                        continue
                    waited[key] = v
                    eng.wait_ge(s, v)
                name, a, kw = o["call"]
                ins = getattr(eng, name)(*a, **kw)
                if o["dma"] is not None:
                    ins.then_inc(dsems[o["dma"]], 1 if (isinstance(o["dma"], tuple) and o["dma"][0] == "cc") else 16)
                elif o["sig"]:
                    ins.then_inc(sems[ename], 1)
            if ename == "sync":
                for e in self.ENGS:
                    if e != "sync" and cnt[e] > 0:
                        eng.wait_ge(sems[e], cnt[e])
                for k, v in dcnt.items():
                    eng.wait_ge(dsems[k], v)

        with nc.Block() as block:
            @block.tensor
            def _(e):
                run("tensor", e)

            @block.vector
            def _(e):
                run("vector", e)

            @block.scalar
            def _(e):
                run("scalar", e)

            @block.gpsimd
            def _(e):
                run("gpsimd", e)

            @block.sync
            def _(e):
                run("sync", e)


GROUPS = [
    dict(name="gla", kind="gla", NQ=4, NU=8, upq=2, H0=0, mch0=0),
    dict(name="hga", kind="hgrn", NQ=4, NU=4, upq=1, H0=0, mch0=8),
    dict(name="hgb", kind="hgrn", NQ=4, NU=4, upq=1, H0=4, mch0=12),
]
C_Q, C_K, C_V, C_A, C_R, C_HQ, C_HF, C_HI, C_HG = 0, 512, 1024, 2048, 2064, 3088, 4112, 5136, 6160
NTILES = [(0, 512), (512, 512), (1024, 64)]


def build(stage=99, fused=True):
    nc = bass.Bass("TRN2", target_bir_lowering=False)

    def din(name, shape):
        return nc.dram_tensor(name, list(shape), F32, kind="ExternalInput").ap()

    def dout(name, shape):
        return nc.dram_tensor(name, list(shape), F32, kind="ExternalOutput").ap()

    xin = din("xin", [T, D])
    sgla = din("sgla", [2, 16, 4, 128, 256])
    shg = din("shg", [2, 16, 8, 128, 128])
    pin_gla = din("pin_gla", [2, 4, 128, 256])
    pin_hg = din("pin_hg", [2, 8, 128, 128])
    ng = din("norm_gains", [2, 6, D])
    w1i = din("ffn1_w_in", [2, D, 2 * 5632])
    w1o = din("ffn1_w_out", [2, 5632, D])
    w2i = din("ffn2_w_in", [2, D, 2 * 5632])
    w2o = din("ffn2_w_out", [2, 5632, D])
    wmi = din("mix_w_in", [2, D, 7184])
    wgu = din("gla_w_gate_up", [2, 16, 512])
    bgt = din("gla_b_gate", [2, 512])
    gnm = din("gla_norm", [2, 256])
    hgm = din("hgrn_gamma", [2, 1024])
    hnm = din("hgrn_norm", [2, 128])
    wmo = din("mix_w_out", [2, D, D])
    cst = din("consts", [128, 336])
    cmk = din("coremask", [128, 1])
    y = dout("y", [T, D])
    pst_gla = dout("pst_gla", [2, 4, 128, 256])
    pst_hg = dout("pst_hg", [2, 8, 128, 128])
    sst_gla = dout("sst_gla", [2, 16, 4, 128, 256])
    sst_hg = dout("sst_hg", [2, 16, 8, 128, 128])
    xsp = nc.dram_tensor("xspill", [128, 16 * T], F32).ap()

    P = Prog(nc)
    A = nc.alloc_sbuf_tensor("arena", [128, 52000], F32)
    PS = [nc.alloc_psum_tensor("ps%d" % i, [128, 512], F32) for i in range(8)]

    def W(off, n):
        return A[:, off:off + n]

    def WB(off, nw):
        return A[:, off:off + nw].bitcast(BF16)

    XTf = W(0, 16 * T)
    XT = XTf.rearrange("p (k t) -> p k t", t=T)
    PAR = W(17408, 512)
    GAIN = PAR[:, 0:192]
    NEGB = PAR[:, 192:200]
    GNW = PAR[:, 200:204]
    HNW = PAR[:, 204:206]
    LB = PAR[:, 206:222]
    OML = PAR[:, 222:238]
    ONEC = PAR[:, 300:301]
    EPSC = PAR[:, 301:302]
    RAW3 = PAR[:, 320:350]
    MASKC = PAR[:, 302:303]
    CON = W(17920, 512)
    IDF = CON[:, 0:128]
    CMASK = CON[:, 128:192]
    SMASK = CON[:, 192:256]
    BSEL = CON[:, 256:272]
    RMASK = CON[:, 272:336]
    IDB = WB(17920 + 336, 64)
    ONESB = WB(17920 + 400, 64)
    PB = 18432
    ACT = WB(PB, 11968).rearrange("p (j t) -> p j t", t=544)
    OUT = W(PB + 11968, 8704).rearrange("p (m t) -> p m t", t=544)
    HT = WB(PB + 11968, 4352).rearrange("p (k t) -> p k t", t=544)
    RSTD = W(PB + 20672, 544)
    TMP = [W(PB + 21216, 544), W(PB + 21760, 544)]
    SQM = [WB(PB + 22304, 272), WB(PB + 22576, 272)]
    RING = [WB(PB + 22848 + 2816 * s, 2816) for s in range(3)]
    RG = [r[:, 0:2048].rearrange("p (k c) -> p k c", c=128) for r in RING]
    RU = [r[:, 2048:4096].rearrange("p (k c) -> p k c", c=128) for r in RING]
    RO = [r[:, 0:5632].rearrange("p (j c) -> p j c", c=128) for r in RING]
    XSTG = [W(PB, 2048), W(PB + 2048, 2048)]
    STG = [W(PB + 4096, 128), W(PB + 4224, 128), W(PB + 4352, 128)]
    MERGED = WB(PB, 8704).rearrange("p (k t) -> p k t", t=T)
    HTM = WB(PB + 8704, 8704).rearrange("p (k t) -> p k t", t=T)
    OUTM = W(PB + 8704, 17408).rearrange("p (m t) -> p m t", t=T)
    QT = WB(0, 2176).rearrange("p (h t) -> p h t", t=T)
    KT = WB(2176, 2176).rearrange("p (h t) -> p h t", t=T)
    KTOK = WB(4352, 2304).rearrange("p (b c) -> p b c", c=512)
    VTOK = WB(6656, 4608).rearrange("p (b c) -> p b c", c=1024)
    GATE = WB(11264, 4352).rearrange("p (u t) -> p u t", t=T)
    ALR = WB(15616, 544)
    WUP = WB(16160, 256)
    TABo = 16416
    TM = W(TABo, 64).rearrange("p (h c) -> p h c", c=16)
    TPV = W(TABo + 64, 64).rearrange("p (h c) -> p h c", c=16)
    EE = W(TABo + 128, 192).rearrange("p (h e c) -> p h e c", e=3, c=16)
    ELS = W(TABo + 320, 64).rearrange("p (h c) -> p h c", c=16)
    WALR = WB(16928, 128).rearrange("p (k c) -> p k c", c=16)
    MB = PB + 17408
    ZG = W(MB, T)
    D1 = W(MB + 1088, T)
    EK = W(MB + 2176, T)
    KR = W(MB + 3264, T)
    SST = W(MB + 4352, 1024)
    SBF = WB(MB + 5376, 512)
    SRING = W(MB + 5888, 2048)
    SBFS = WB(MB + 7936, 1024)
    VM = WB(MB + 8960, 1024)
    SCS = WB(MB + 9984, 256)
    SQN = WB(MB + 10240, 256)
    RSTDN = W(MB + 10496, 512)
    T1 = W(MB + 11008, 512)
    SCSS = WB(MB + 11520, 128)
    WR = [WB(MB + 11648 + 2048 * s, 2048).rearrange("p (k c) -> p k c", c=256) for s in range(2)]

    state = dict(ring=0, wr=0)

    def V(fn, r=(), w=()):
        return P.op("vector", fn, r, w)

    def S(fn, r=(), w=()):
        return P.op("scalar", fn, r, w)

    def TE(fn, r=(), w=()):
        return P.op("tensor", fn, r, w)

    def DQ(fn, r=(), w=(), key=None):
        return P.op("sync", fn, r, w, dma=key)

    def DG(fn, r=(), w=(), key=None):
        return P.op("gpsimd", fn, r, w, dma=key)

    def mm(out, lhsT, rhs, start, stop, r, w=()):
        return TE(lambda e: e.matmul(out, lhsT=lhsT, rhs=rhs, start=start, stop=stop, skip_group_check=True), r, w)

    DQ(lambda e: e.dma_start(out=CON[:, 0:336], in_=cst), w=["con"], key="cst")
    V(lambda e: e.tensor_copy(out=IDB, in_=IDF), ["con"], ["idb"])
    DQ(lambda e: e.dma_start(out=MASKC, in_=cmk), w=["par"], key="cmk")
    V(lambda e: e.memset(ONESB, 1.0), (), ["onesb"])
    V(lambda e: e.memset(ONEC, 1.0), (), ["par"])
    V(lambda e: e.memset(EPSC, EPS), (), ["par"])
    ngv = ng.rearrange("l i (k p) -> (l i k) p", p=128)
    DQ(lambda e: e.dma_start(out=STG[0][0:96, :], in_=ngv[0:96, :]), w=["stg0"], key="p0")
    DQ(lambda e: e.dma_start(out=STG[1][0:96, :], in_=ngv[96:192, :]), w=["stg1"], key="p1")
    DQ(lambda e: e.dma_start(out=STG[2][0:8, :], in_=bgt.rearrange("l (h p) -> (l h) p", p=128)), w=["stg2a"], key="p2")
    DQ(lambda e: e.dma_start(out=STG[2][8:12, :], in_=gnm.rearrange("l (c p) -> (l c) p", p=128)), w=["stg2b"], key="p3")
    DQ(lambda e: e.dma_start(out=STG[2][12:14, :], in_=hnm), w=["stg2c"], key="p4")
    DQ(lambda e: e.dma_start(out=STG[2][14:30, :], in_=hgm.rearrange("l (h p) -> (l h) p", p=128)), w=["stg2d"], key="p5")
    TE(lambda e: e.transpose(out=PS[0][:, 0:96], in_=STG[0][0:96, :], identity=IDF[0:96, 0:96]), ["stg0", "con"], [("ps", 0)])
    V(lambda e: e.tensor_copy(out=GAIN[:, 0:96], in_=PS[0][:, 0:96]), [("ps", 0)], ["par"])
    TE(lambda e: e.transpose(out=PS[1][:, 0:96], in_=STG[1][0:96, :], identity=IDF[0:96, 0:96]), ["stg1", "con"], [("ps", 1)])
    V(lambda e: e.tensor_copy(out=GAIN[:, 96:192], in_=PS[1][:, 0:96]), [("ps", 1)], ["par"])
    TE(lambda e: e.transpose(out=PS[2][:, 0:30], in_=STG[2][0:30, :], identity=IDF[0:30, 0:30]),
       ["stg2a", "stg2b", "stg2c", "stg2d", "con"], [("ps", 2)])
    V(lambda e: e.tensor_copy(out=RAW3, in_=PS[2][:, 0:30]), [("ps", 2)], ["par"])
    V(lambda e: e.tensor_scalar(out=NEGB, in0=PAR[:, 320:328], scalar1=-1.0, scalar2=None, op0=ALU.mult), ["par"], ["par"])
    V(lambda e: e.tensor_copy(out=GNW, in_=PAR[:, 328:332]), ["par"], ["par"])
    V(lambda e: e.tensor_copy(out=HNW, in_=PAR[:, 332:334]), ["par"], ["par"])
    g0, g1 = PAR[:, 334:342], PAR[:, 342:350]
    mx, e0, e1, ss, rs, p0, p1, c1 = [PAR[:, 352 + 8 * i:360 + 8 * i] for i in range(8)]
    V(lambda e: e.tensor_tensor(out=mx, in0=g0, in1=g1, op=ALU.max), ["par"], ["par"])
    V(lambda e: e.tensor_tensor(out=e0, in0=g0, in1=mx, op=ALU.subtract), ["par"], ["par"])
    V(lambda e: e.tensor_tensor(out=e1, in0=g1, in1=mx, op=ALU.subtract), ["par"], ["par"])
    S(lambda e: e.activation(out=e0, in_=e0, func=AF.Exp), ["par"], ["par"])
    S(lambda e: e.activation(out=e1, in_=e1, func=AF.Exp), ["par"], ["par"])
    V(lambda e: e.tensor_tensor(out=ss, in0=e0, in1=e1, op=ALU.add), ["par"], ["par"])
    V(lambda e: e.reciprocal(out=rs, in_=ss), ["par"], ["par"])
    V(lambda e: e.tensor_tensor(out=p0, in0=e0, in1=rs, op=ALU.mult), ["par"], ["par"])
    V(lambda e: e.tensor_tensor(out=p1, in0=e1, in1=rs, op=ALU.mult), ["par"], ["par"])
    V(lambda e: e.tensor_tensor(out=c1, in0=p0, in1=p1, op=ALU.add), ["par"], ["par"])
    V(lambda e: e.tensor_tensor(out=LB[:, 0:8], in0=p0, in1=p0, op=ALU.subtract), ["par"], ["par"])
    V(lambda e: e.tensor_tensor(out=LB[:, 8:16], in0=c1, in1=p0, op=ALU.subtract), ["par"], ["par"])
    V(lambda e: e.tensor_scalar(out=OML, in0=LB, scalar1=-1.0, scalar2=1.0, op0=ALU.mult, op1=ALU.add), ["par"], ["par"])

    for blk in range(9):
        rows = 128 if blk < 8 else 64
        xs = XSTG[blk % 2]
        DQ(lambda e, xs=xs, blk=blk, rows=rows: e.dma_start(out=xs[0:rows, :], in_=xin[blk * 128:blk * 128 + rows, :]),
           w=[("xstg", blk % 2)], key=("xl", blk % 2))
        for q in range(4):
            bk = (blk * 4 + q) % 2
            for i in range(4):
                k = 4 * q + i
                TE(lambda e, bk=bk, i=i, k=k, xs=xs, rows=rows: e.transpose(
                    out=PS[bk][:, i * 128:i * 128 + rows], in_=xs[0:rows, k * 128:(k + 1) * 128], identity=IDF[0:rows, 0:rows]),
                   [("xstg", blk % 2), "con"], [("ps", bk)] if i in (0, 3) else [])
            V(lambda e, bk=bk, q=q, blk=blk, rows=rows: e.tensor_copy(
                out=XT[:, 4 * q:4 * q + 4, blk * 128:blk * 128 + rows],
                in_=PS[bk][:, 0:512].rearrange("p (i t) -> p i t", t=128)[:, :, 0:rows]),
              [("ps", bk)], ["xt"])
    P.barrier()

    def rstd_from(psA, nA, psB, nB, dstA, dstB, scale):
        S(lambda e: e.activation(out=dstA, in_=psA, func=AF.Sqrt, bias=EPSC, scale=scale), [("ps", nA), "par"], ["rstd"])
        S(lambda e: e.activation(out=dstB, in_=psB, func=AF.Sqrt, bias=EPSC, scale=scale), [("ps", nB), "par"], ["rstd"])

    ACT2 = WB(PB, 23936).rearrange("p (j t) -> p j t", t=T)
    HT2 = WB(0, 8704).rearrange("p (k t) -> p k t", t=T)
    HTT = WB(PB + 23936, 8704).rearrange("p (k t) -> p k t", t=T)
    RING2 = [WB(PB + 23936 + 2816 * s_, 2816) for s_ in range(3)]
    RG2 = [r[:, 0:2048].rearrange("p (k c) -> p k c", c=128) for r in RING2]
    RU2 = [r[:, 2048:4096].rearrange("p (k c) -> p k c", c=128) for r in RING2]
    RO2 = [r[:, 0:5632].rearrange("p (j c) -> p j c", c=128) for r in RING2]
    SQ2 = [WB(PB + 23936 + 8704 + 256 * i_, 256) for i_ in range(2)]
    SQ2B = [WB(PB + 23936 + 8448 + 544 * i_, 544) for i_ in range(2)]
    TMP2 = [W(8704 + 512 * i_, 512) for i_ in range(4)]
    RSTD2 = W(PB, T)
    RSTD2P = W(PB + 17408, T)
    XF = W(PB, 17408).rearrange("p (k t) -> p k t", t=T)

    def rstd_ln(ps_ap, bank, dst, scale, wkey):
        S(lambda e: e.activation(out=dst, in_=ps_ap, func=AF.Ln, bias=EPSC, scale=scale), [("ps", bank), "par"], [wkey])
        S(lambda e: e.activation(out=dst, in_=dst, func=AF.Exp, scale=-0.5), [wkey], [wkey])

    def ffn(l, wi, wo, gi_pre, gi_post, nxt=True):
        wiv = wi[l].rearrange("(k p) c -> p k c", p=128)
        wov = wo[l].rearrange("(j p) c -> p j c", p=128)
        gpre = GAIN[:, (l * 6 + gi_pre) * 16:(l * 6 + gi_pre) * 16 + 16]
        gpost = GAIN[:, (l * 6 + gi_post) * 16:(l * 6 + gi_post) * 16 + 16]
        DQ(lambda e: e.dma_start(out=xsp, in_=XTf), ["xt"], ["xsp"], key="spill")
        for nt, (t0, n) in enumerate(NTILES):
            b = 5 + nt
            if not state.get("ssq_ready"):
                for k in range(16):
                    sq = SQ2[k % 2]
                    S(lambda e, sq=sq, k=k, t0=t0, n=n: e.activation(out=sq[:, 0:n], in_=XT[:, k, t0:t0 + n], func=AF.Square), ["xt"], [("sq2", k % 2)])
                    mm(PS[b][:, 0:n], ONESB, sq[:, 0:n], k == 0, k == 15, [("sq2", k % 2), "onesb"], [("ps", b)] if k in (0, 15) else [])
            rstd_ln(PS[b][:, 0:n], b, RSTD2[:, t0:t0 + n], 1.0 / D, "rstd2")
        state["ssq_ready"] = False
        for k in range(16):
            V(lambda e, k=k: e.scalar_tensor_tensor(out=HTT[:, k, :], in0=XT[:, k, :], scalar=gpre[:, k:k + 1], in1=RSTD2, op0=ALU.mult, op1=ALU.mult),
              ["xt", "rstd2", "par"], ["htt"])
        V(lambda e: e.tensor_copy(out=HT2[:, :, :], in_=HTT[:, :, :]), ["htt"], ["ht2", "xt"])
        nfirst = 0
        for j in range(NJ):
            s = state["ring"] % 3
            state["ring"] += 1
            xw = ["htt"] if nfirst < 3 else []
            nfirst += 1
            DG(lambda e, s=s, j=j: e.dma_start(out=RG2[s], in_=wiv[:, :, j * 128:(j + 1) * 128]), w=[("ra", s)] + xw, key=("ring", s, 0))
            DG(lambda e, s=s, j=j: e.dma_start(out=RU2[s], in_=wiv[:, :, 5632 + j * 128:5632 + (j + 1) * 128]), w=[("rb", s)] + xw, key=("ring", s, 1))
            for nt, (t0, n) in enumerate(NTILES):
                pidx = (3 * j + nt) % 4
                bg, bu = 2 * pidx, 2 * pidx + 1
                for k in range(16):
                    mm(PS[bg][:, 0:n], RG2[s][:, k, :], HT2[:, k, t0:t0 + n], k == 0, k == 15, [("ra", s), "ht2"], [("ps", bg)] if k in (0, 15) else [])
                for k in range(16):
                    mm(PS[bu][:, 0:n], RU2[s][:, k, :], HT2[:, k, t0:t0 + n], k == 0, k == 15, [("rb", s), "ht2"], [("ps", bu)] if k in (0, 15) else [])
                tm = TMP2[pidx]
                S(lambda e, tm=tm, bg=bg, n=n: e.activation(out=tm[:, 0:n], in_=PS[bg][:, 0:n], func=AF.Silu), [("ps", bg)], [("tmp2", pidx)])
                V(lambda e, tm=tm, bu=bu, j=j, t0=t0, n=n: e.tensor_tensor(out=ACT2[:, j, t0:t0 + n], in0=tm[:, 0:n], in1=PS[bu][:, 0:n], op=ALU.mult),
                  [("tmp2", pidx), ("ps", bu)], [("act", j, nt)])
        for m in range(16):
            s = state["ring"] % 3
            state["ring"] += 1
            DG(lambda e, s=s, m=m: e.dma_start(out=RO2[s], in_=wov[:, :, m * 128:(m + 1) * 128]), w=[("ra", s), ("rb", s)], key=("ring", s, 0))
            for nt, (t0, n) in enumerate(NTILES):
                if nt < 2:
                    b, co = 3 * (m % 2) + nt, 0
                else:
                    b, co = 2, 64 * (m % 2)
                for j in range(NJ):
                    mm(PS[b][:, co:co + n], RO2[s][:, j, :], ACT2[:, j, t0:t0 + n], j == 0, j == NJ - 1, [("ra", s), ("act", j, nt)],
                       [("ps", b)] if j in (0, NJ - 1) else [])
                V(lambda e, m=m, b=b, co=co, t0=t0, n=n: e.tensor_copy(out=XT[:, m, t0:t0 + n], in_=PS[b][:, co:co + n]), [("ps", b)],
                  [("out2", m)] + (["ht2", "xt"] + [("tmp2", i_) for i_ in range(4)] if (m == 0 and nt == 0) else []))
            sqb = SQ2B[m % 2]
            S(lambda e, sqb=sqb, m=m: e.activation(out=sqb[:, 0:T], in_=XT[:, m, :], func=AF.Square), [("out2", m)], [("sq2b", m % 2)])
            for mq in ([m - 1] if m > 0 else []) + ([15] if m == 15 else []):
                sqq = SQ2B[mq % 2]
                for nt, (t0, n) in enumerate(NTILES):
                    mm(PS[5 + nt][:, 0:n], ONESB, sqq[:, t0:t0 + n], mq == 0, mq == 15, [("sq2b", mq % 2), "onesb"], [("ps", 5 + nt)] if mq in (0, 15) else [])
        P.barrier()
        DQ(lambda e: e.dma_start(out=XF[:, :, :], in_=xsp.rearrange("p (k t) -> p k t", t=T)), ["xsp"], ["xf"], key="fill")
        for nt, (t0, n) in enumerate(NTILES):
            rstd_ln(PS[5 + nt][:, 0:n], 5 + nt, RSTD2P[:, t0:t0 + n], 1.0 / D, "rstd2p")
        V(lambda e: e.tensor_scalar(out=RSTD2P, in0=RSTD2P, scalar1=0.5, scalar2=None, op0=ALU.mult), ["rstd2p"], ["rstd2p"])
        for m in range(16):
            V(lambda e, m=m: e.tensor_tensor(out=XT[:, m, :], in0=XT[:, m, :], in1=RSTD2P, op=ALU.mult), [("out2", m), "rstd2p"], [("out2", m)])
            V(lambda e, m=m: e.scalar_tensor_tensor(out=XT[:, m, :], in0=XT[:, m, :], scalar=gpost[:, m:m + 1], in1=XF[:, m, :],
                                                    op0=ALU.mult, op1=ALU.add), [("out2", m), "xf", "par"], ["xt", ("xtm", m)])
            if nxt:
                sqb = SQ2B[m % 2]
                S(lambda e, sqb=sqb, m=m: e.activation(out=sqb[:, 0:T], in_=XT[:, m, :], func=AF.Square), [("xtm", m)], [("sq2b", m % 2)])
                for nt, (t0, n) in enumerate(NTILES):
                    mm(PS[5 + nt][:, 0:n], ONESB, sqb[:, t0:t0 + n], m == 0, m == 15, [("sq2b", m % 2), "onesb"], [("ps", 5 + nt)] if m in (0, 15) else [])
        state["ssq_ready"] = bool(nxt)

    def load_wr(src3, col0, ncols=256):
        s = state["wr"] % 2
        state["wr"] += 1
        DG(lambda e, s=s: e.dma_start(out=WR[s][:, :, 0:ncols], in_=src3[:, :, col0:col0 + ncols]), w=[("wr", s)], key=("wr", s))
        return s

    def proj_fm(wchunk, wkey, rhs3, rkey, bset):
        for nt, (t0, n) in enumerate(NTILES):
            b = bset[nt]
            for k in range(16):
                mm(PS[b][:, 0:n], wchunk[:, k, :], rhs3[:, k, t0:t0 + n], k == 0, k == 15, [wkey, rkey], [("ps", b)] if k in (0, 15) else [])

    def mixer(l):
        wmv = wmi[l].rearrange("(k p) c -> p k c", p=128)
        gpre = GAIN[:, (l * 6 + 2) * 16:(l * 6 + 2) * 16 + 16]
        gpost = GAIN[:, (l * 6 + 3) * 16:(l * 6 + 3) * 16 + 16]
        DQ(lambda e: e.dma_start(out=xsp, in_=XTf), ["xt"], ["xsp"], key="spill")
        for nt, (t0, n) in enumerate(NTILES):
            b = 5 + nt
            if state.get("ssq_ready"):
                continue
            for k in range(16):
                sq = SQM[k % 2]
                S(lambda e, sq=sq, k=k, t0=t0, n=n: e.activation(out=sq[:, 0:n], in_=XT[:, k, t0:t0 + n], func=AF.Square), ["xt"], [("sqa", k % 2)])
                mm(PS[b][:, 0:n], ONESB, sq[:, 0:n], k == 0, k == 15, [("sqa", k % 2), "onesb"], [("ps", b)] if k in (0, 15) else [])
        state["ssq_ready"] = False
        RS = W(MB, T)
        for nt, (t0, n) in enumerate(NTILES):
            rstd_ln(PS[5 + nt][:, 0:n], 5 + nt, RS[:, t0:t0 + n], 1.0 / D, "rsx")
        for k in range(16):
            V(lambda e, k=k: e.scalar_tensor_tensor(out=HTM[:, k, :], in0=XT[:, k, :], scalar=gpre[:, k:k + 1], in1=RS, op0=ALU.mult, op1=ALU.mult),
              ["xt", "rsx", "par"], ["htm"])
        P.barrier()

        for gi, grp in enumerate(GROUPS):
            NQ, NU, upq, H0, mch0 = grp["NQ"], grp["NU"], grp["upq"], grp["H0"], grp["mch0"]
            gla = grp["kind"] == "gla"
            NV = NU * 128
            if gla:
                DG(lambda e: e.dma_start(out=WALR, in_=wmv[:, :, C_A:C_A + 16]), w=["walr"], key="walr")
                DG(lambda e: e.dma_start(out=WUP[0:16, 0:512], in_=wgu[l]), w=["wup"], key="wup")
                for nt, (t0, n) in enumerate(NTILES):
                    for k in range(16):
                        mm(PS[nt][0:16, 0:n], WALR[:, k, :], HTM[:, k, t0:t0 + n], k == 0, k == 15, ["walr", "htm"], [("ps", nt)] if k in (0, 15) else [])
                    S(lambda e, nt=nt, t0=t0, n=n: e.activation(out=ALR[0:16, t0:t0 + n], in_=PS[nt][0:16, 0:n], func=AF.Copy), [("ps", nt)], ["alr"])
            pset = 0
            for h in range(NQ):
                H = H0 + h
                if gla:
                    bset = [3 * pset, 3 * pset + 1, 3 * pset + 2]
                    pset ^= 1
                    for nt, (t0, n) in enumerate(NTILES):
                        b = bset[nt]
                        mm(PS[b][:, 0:n], WUP[0:16, h * 128:(h + 1) * 128], ALR[0:16, t0:t0 + n], True, True, ["wup", "alr"], [("ps", b)])
                        S(lambda e, b=b, t0=t0, n=n, h=h: e.activation(out=ZG[:, t0:t0 + n], in_=PS[b][:, 0:n], func=AF.Exp,
                                                                     bias=NEGB[:, l * 4 + h:l * 4 + h + 1], scale=-1.0), [("ps", b), "par"], ["zg"])
                    S(lambda e: e.activation(out=ZG, in_=ZG, func=AF.Ln, bias=ONEC, scale=1.0), ["zg", "par"], ["zg"])
                    V(lambda e: e.tensor_scalar(out=ZG, in0=ZG, scalar1=-1.0 / 16.0, scalar2=None, op0=ALU.mult), ["zg"], ["zg"])
                else:
                    if h % 2 == 0:
                        sf = load_wr(wmv, C_HF + (H0 + h) * 128)
                    bset = [3 * pset, 3 * pset + 1, 3 * pset + 2]
                    pset ^= 1
                    proj_fm(WR[sf][:, :, (h % 2) * 128:(h % 2) * 128 + 128], ("wr", sf), HTM, "htm", bset)
                    for nt, (t0, n) in enumerate(NTILES):
                        b = bset[nt]
                        S(lambda e, b=b, t0=t0, n=n: e.activation(out=ZG[:, t0:t0 + n], in_=PS[b][:, 0:n], func=AF.Sigmoid), [("ps", b)], ["zg"])
                        S(lambda e, b=b, t0=t0, n=n: e.activation(out=KR[:, t0:t0 + n], in_=PS[b][:, 0:n], func=AF.Sigmoid, scale=-1.0), [("ps", b)], ["kr"])
                    V(lambda e, H=H: e.tensor_scalar(out=ZG, in0=ZG, scalar1=OML[:, l * 8 + H:l * 8 + H + 1], scalar2=LB[:, l * 8 + H:l * 8 + H + 1],
                                                     op0=ALU.mult, op1=ALU.add), ["zg", "par"], ["zg"])
                    S(lambda e: e.activation(out=ZG, in_=ZG, func=AF.Ln), ["zg"], ["zg"])
                V(lambda e: e.tensor_tensor_scan(out=D1[:, 0:1024], data0=ONEC.to_broadcast([128, 1024]), data1=ZG[:, 0:1024], initial=0.0,
                                                 op0=ALU.mult, op1=ALU.add), ["zg", "par"], ["d1"])
                V(lambda e: e.tensor_tensor_scan(out=D1[:, 1024:1088], data0=RMASK, data1=ZG[:, 1024:1088], initial=0.0,
                                                 op0=ALU.mult, op1=ALU.add), ["zg", "con"], ["d1"])
                Gv = D1[:, 0:1024].rearrange("p (c t) -> p c t", t=64)
                Gs = D1[:, 1024:1088].rearrange("p (b t) -> p b t", t=4)
                V(lambda e, h=h: e.tensor_copy(out=TM[:, h, :], in_=Gv[:, :, 31]), ["d1"], ["tab"])
                V(lambda e, h=h: e.memset(TPV[:, h, 0:1], 0.0), (), ["tab"])
                V(lambda e, h=h: e.tensor_copy(out=TPV[:, h, 1:16], in_=Gv[:, 0:15, 63]), ["d1"], ["tab"])
                V(lambda e, h=h: e.tensor_tensor(out=EE[:, h, 0, :], in0=TM[:, h, :], in1=TPV[:, h, :], op=ALU.subtract), ["tab"], ["tab"])
                V(lambda e, h=h: e.tensor_tensor(out=EE[:, h, 1, :], in0=Gv[:, :, 63], in1=TPV[:, h, :], op=ALU.subtract), ["tab", "d1"], ["tab"])
                V(lambda e, h=h: e.tensor_tensor(out=EE[:, h, 2, :], in0=Gv[:, :, 63], in1=TM[:, h, :], op=ALU.subtract), ["tab", "d1"], ["tab"])
                S(lambda e, h=h: e.activation(out=EE[:, h, :, :], in_=EE[:, h, :, :], func=AF.Exp), ["tab"], ["tab"])
                S(lambda e, h=h: e.activation(out=ELS[:, h, :], in_=Gs[:, :, 3], func=AF.Exp), ["d1"], ["tab"])
                V(lambda e, h=h: e.tensor_tensor(out=Gv, in0=Gv, in1=TM[:, h, :].unsqueeze(2).to_broadcast([128, 16, 64]), op=ALU.subtract),
                  ["d1", "tab"], ["d1"])
                S(lambda e: e.activation(out=EK, in_=D1, func=AF.Exp, scale=-1.0), ["d1"], ["ek"])
                S(lambda e: e.activation(out=D1, in_=D1, func=AF.Exp), ["d1"], ["d1"])
                if gla:
                    if h % 2 == 0:
                        sk = load_wr(wmv, C_K + h * 128)
                    bset = [3 * pset, 3 * pset + 1, 3 * pset + 2]
                    pset ^= 1
                    proj_fm(WR[sk][:, :, (h % 2) * 128:(h % 2) * 128 + 128], ("wr", sk), HTM, "htm", bset)
                    for nt, (t0, n) in enumerate(NTILES):
                        b = bset[nt]
                        V(lambda e, b=b, t0=t0, n=n, h=h: e.tensor_tensor(out=KT[:, h, t0:t0 + n], in0=PS[b][:, 0:n], in1=EK[:, t0:t0 + n], op=ALU.mult),
                          [("ps", b), "ek"], ["kt"])
                else:
                    V(lambda e, h=h, H=H: e.scalar_tensor_tensor(out=KT[:, h, :], in0=KR, scalar=OML[:, l * 8 + H:l * 8 + H + 1], in1=EK,
                                                                 op0=ALU.mult, op1=ALU.mult), ["kr", "ek", "par"], ["kt"])
                if h % 2 == 0:
                    sq_ = load_wr(wmv, (C_Q + h * 128) if gla else (C_HQ + (H0 + h) * 128))
                bset = [3 * pset, 3 * pset + 1, 3 * pset + 2]
                pset ^= 1
                proj_fm(WR[sq_][:, :, (h % 2) * 128:(h % 2) * 128 + 128], ("wr", sq_), HTM, "htm", bset)
                for nt, (t0, n) in enumerate(NTILES):
                    b = bset[nt]
                    if gla:
                        V(lambda e, b=b, t0=t0, n=n, h=h: e.scalar_tensor_tensor(out=QT[:, h, t0:t0 + n], in0=PS[b][:, 0:n], scalar=128.0 ** -0.5,
                                                                                 in1=D1[:, t0:t0 + n], op0=ALU.mult, op1=ALU.mult), [("ps", b), "d1"], ["qt"])
                    else:
                        S(lambda e, b=b, t0=t0, n=n: e.activation(out=KR[:, t0:t0 + n], in_=PS[b][:, 0:n], func=AF.Silu), [("ps", b), "kt"], ["kr"])
                        V(lambda e, t0=t0, n=n, h=h: e.scalar_tensor_tensor(out=QT[:, h, t0:t0 + n], in0=KR[:, t0:t0 + n], scalar=128.0 ** -0.5,
                                                                            in1=D1[:, t0:t0 + n], op0=ALU.mult, op1=ALU.mult), ["kr", "d1"], ["qt"])
                KH = KR[:, 0:544].bitcast(BF16)
                V(lambda e, h=h: e.tensor_tensor(out=KH[:, 0:1024].rearrange("p (c t) -> p c t", t=64),
                                                 in0=KT[:, h, 0:1024].rearrange("p (c t) -> p c t", t=64),
                                                 in1=EE[:, h, 2, :].unsqueeze(2).to_broadcast([128, 16, 64]), op=ALU.mult), ["kt", "tab", "qt"], ["kr"])
                V(lambda e, h=h: e.tensor_tensor(out=KH[:, 1024:1088].rearrange("p (b t) -> p b t", t=4),
                                                 in0=KT[:, h, 1024:1088].rearrange("p (b t) -> p b t", t=4),
                                                 in1=ELS[:, h, :].unsqueeze(2).to_broadcast([128, 16, 4]), op=ALU.mult), ["kt", "tab", "qt"], ["kr"])
                pb6 = PS[6][:].bitcast(BF16)
                pb7 = PS[7][:].bitcast(BF16)
                for blk in range(8):
                    TE(lambda e, blk=blk: e.transpose(out=pb6[:, blk * 128:(blk + 1) * 128], in_=KH[:, blk * 128:(blk + 1) * 128], identity=IDB),
                       ["kr", "idb"], [("ps", 6)] if blk in (0, 7) else [])
                TE(lambda e: e.transpose(out=pb7[0:64, 0:128], in_=KH[:, 1024:1088], identity=IDB), ["kr", "idb"], [("ps", 7)])
                S(lambda e, h=h: e.activation(out=KTOK[:, 0:8, h * 128:(h + 1) * 128], in_=pb6[:, 0:1024].rearrange("p (b c) -> p b c", c=128), func=AF.Copy),
                  [("ps", 6)], ["ktok"])
                S(lambda e, h=h: e.activation(out=KTOK[0:64, 8, h * 128:(h + 1) * 128], in_=pb7[0:64, 0:128], func=AF.Copy), [("ps", 7)], ["ktok"])
            gc0 = C_R if gla else C_HG + H0 * 128
            for u in range(NU):
                if u % 2 == 0:
                    sg = load_wr(wmv, gc0 + u * 128)
                bset = [3 * pset, 3 * pset + 1, 3 * pset + 2]
                pset ^= 1
                proj_fm(WR[sg][:, :, (u % 2) * 128:(u % 2) * 128 + 128], ("wr", sg), HTM, "htm", bset)
                for nt, (t0, n) in enumerate(NTILES):
                    b = bset[nt]
                    S(lambda e, b=b, t0=t0, n=n, u=u: e.activation(out=GATE[:, u, t0:t0 + n], in_=PS[b][:, 0:n], func=AF.Silu), [("ps", b)], ["gate"])
            vc0 = C_V if gla else C_HI + H0 * 128
            cnt = 0
            for vt in range(NV // 256):
                sv = load_wr(wmv, vc0 + vt * 256)
                for blk in range(9):
                    rows = 128 if blk < 8 else 64
                    b = 6 + cnt % 2
                    off = 0
                    first = True
                    cnt += 1
                    for k in range(16):
                        mm(PS[b][0:rows, off:off + 256], HTM[:, k, blk * 128:blk * 128 + rows], WR[sv][:, k, :], (k == 0 and first), (k == 15),
                           [("wr", sv), "htm"], [("ps", b)] if ((k == 0 and first) or k == 15) else [])
                    S(lambda e, b=b, off=off, rows=rows, blk=blk, vt=vt: e.activation(out=VTOK[0:rows, blk, vt * 256:(vt + 1) * 256],
                                                                                    in_=PS[b][0:rows, off:off + 256], func=AF.Copy), [("ps", b)], ["vtok"])
            Sv = SST[:, 0:NV].rearrange("p (u v) -> p u v", v=128)
            if gla:
                pin_v = pin_gla[l].rearrange("h d x -> d h x")
                pst_v = pst_gla[l].rearrange("h d x -> d h x")
                S4 = SST[:, 0:NV].rearrange("p (h x) -> p h x", x=256)
            else:
                pin_v = pin_hg[l, H0:H0 + 4].rearrange("h d x -> d h x")
                pst_v = pst_hg[l, H0:H0 + 4].rearrange("h d x -> d h x")
                S4 = SST[:, 0:NV].rearrange("p (h x) -> p h x", x=128)

            def hview(ap2, n):
                return ap2.unsqueeze(2).to_broadcast([128, NQ, upq * n])

            def norm_gate(bo, bn, t0):
                NUc = NU * 64
                S(lambda e: e.activation(out=SQN[:, 0:NUc], in_=PS[bo][:, 0:NUc], func=AF.Square), [("ps", bo)], ["sqn"])
                if gla:
                    for h in range(4):
                        mm(PS[bn][:, 256 + h * 64:256 + (h + 1) * 64], ONESB, SQN[:, (2 * h) * 64:(2 * h + 1) * 64], h == 0, False,
                           ["sqn", "onesb"], [("ps", bn)] if h == 0 else [])
                        mm(PS[bn][:, 256 + h * 64:256 + (h + 1) * 64], ONESB, SQN[:, (2 * h + 1) * 64:(2 * h + 2) * 64], False, h == 3,
                           ["sqn", "onesb"], [("ps", bn)] if h == 3 else [])
                    dv = 256.0
                else:
                    mm(PS[bn][:, 256:512], ONESB, SQN[:, 0:256], True, True, ["sqn", "onesb"], [("ps", bn)])
                    dv = 128.0
                rstd_ln(PS[bn][:, 256:512], bn, RSTDN[:, 0:256], 1.0 / dv, "rstdn")
                V(lambda e: e.tensor_tensor(out=T1[:, 0:NUc].rearrange("p (h c t) -> p h c t", h=NQ, c=upq),
                                            in0=PS[bo][:, 0:NUc].rearrange("p (h c t) -> p h c t", h=NQ, c=upq),
                                            in1=RSTDN[:, 0:256].rearrange("p (h t) -> p h t", t=64).unsqueeze(2).to_broadcast([128, NQ, upq, 64]),
                                            op=ALU.mult), [("ps", bo), "rstdn"], ["t1"])
                if gla:
                    T1v = T1[:, 0:NUc].rearrange("p (h c t) -> p h c t", h=4, c=2)
                    Mv = MERGED[:, 0:8, t0:t0 + 64].rearrange("p (h c) t -> p h c t", c=2)
                    Gtv = GATE[:, 0:8, t0:t0 + 64].rearrange("p (h c) t -> p h c t", c=2)
                    for c in range(2):
                        V(lambda e, c=c: e.scalar_tensor_tensor(out=Mv[:, :, c, :], in0=T1v[:, :, c, :], scalar=GNW[:, l * 2 + c:l * 2 + c + 1],
                                                                in1=Gtv[:, :, c, :], op0=ALU.mult, op1=ALU.mult), ["t1", "gate", "par"], ["merged"])
                else:
                    V(lambda e: e.scalar_tensor_tensor(out=MERGED[:, mch0:mch0 + 4, t0:t0 + 64], in0=T1[:, 0:256].rearrange("p (h t) -> p h t", t=64),
                                                       scalar=HNW[:, l:l + 1], in1=GATE[:, 0:4, t0:t0 + 64], op0=ALU.mult, op1=ALU.mult),
                      ["t1", "gate", "par"], ["merged"])

            nPb = NV // 512
            def cvars(c):
                par = c % 2
                bP = [4 * par + 2, 4 * par + 3][:nPb]
                return par, bP, c // 2, 64 * (c % 2)

            def state_P(c):
                par, bP, blk, r0 = cvars(c)
                for u in range(NU):
                    hq = u // upq
                    b = bP[(u * 128) // 512]
                    o = (u * 128) % 512
                    fi = (o == 0)
                    la = (o == 384) or (u == NU - 1)
                    mm(PS[b][:, o:o + 128], KTOK[r0:r0 + 64, blk, hq * 128:(hq + 1) * 128], VTOK[r0:r0 + 64, blk, u * 128:(u + 1) * 128], fi, la,
                       ["ktok", "vtok"], [("ps", b)] if (fi or la) else [])

            def state_U(c):
                par, bP, blk, r0 = cvars(c)
                X = upq * 128
                for h in range(NQ):
                    b = bP[(h * X) // 512]
                    o = (h * X) % 512
                    V(lambda e, h=h, b=b, o=o, c=c, X=X: e.scalar_tensor_tensor(out=SST[:, h * X:(h + 1) * X], in0=SST[:, h * X:(h + 1) * X],
                                                                                 scalar=EE[:, h, 1, c:c + 1], in1=PS[b][:, o:o + X],
                                                                                 op0=ALU.mult, op1=ALU.add), ["sst", "tab", "sbf", ("ps", b)], ["sst"])

            if fused:
                V(lambda e: e.memset(SST[:, 0:NV], 0.0), (), ["sst"])
                for c in range(16):
                    state_P(c)
                    state_U(c)
                bn = nc.dram_tensor("bnc%d%d" % (l, gi), [128, NV], F32).ap()
                gt = nc.dram_tensor("gth%d%d" % (l, gi), [256, NV], F32).ap()
                DG(lambda e: e.dma_start(out=bn, in_=SST[:, 0:NV]), ["sst"], [("bnc", l, gi)], key=("bn", gi))
                DG(lambda e: e.collective_compute("AllGather", ALU.bypass, replica_groups=[[0, 1], [2, 3], [4, 5], [6, 7]],
                                                  ins=[bn.opt()], outs=[gt.opt()]), [("bnc", l, gi)], [("gth", l, gi)], key=("cc", 0))

            def init_prompt_state():
                if fused:
                    DQ(lambda e: e.dma_start(out=SST[:, 0:NV], in_=gt[0:128, :]), [("gth", l, gi)], ["sst"], key="pin")
                    V(lambda e: e.tensor_scalar(out=SST[:, 0:NV], in0=SST[:, 0:NV], scalar1=MASKC, scalar2=None, op0=ALU.mult), ["sst", "par"], ["sst"])
                else:
                    DQ(lambda e: e.dma_start(out=S4, in_=pin_v), w=["sst"], key="pin")
            ss_ = slice(1024, 1088)
            bsc, bo = 0, 1
            for h in range(NQ):
                mm(PS[bsc][0:64, h * 64:(h + 1) * 64], KT[:, h, ss_], QT[:, h, ss_], h == 0, h == NQ - 1, ["kt", "qt"],
                   [("ps", bsc)] if h in (0, NQ - 1) else [])
            V(lambda e: e.tensor_tensor(out=SCSS[0:64, 0:NQ * 64].rearrange("p (h t) -> p h t", t=64),
                                        in0=PS[bsc][0:64, 0:NQ * 64].rearrange("p (h t) -> p h t", t=64),
                                        in1=SMASK[0:64, :].unsqueeze(1).to_broadcast([64, NQ, 64]), op=ALU.mult), [("ps", bsc), "con"], ["scss"])
            for u in range(NU):
                hq = u // upq
                mm(PS[bo][:, u * 64:(u + 1) * 64], VTOK[0:64, 8, u * 128:(u + 1) * 128], SCSS[0:64, hq * 64:(hq + 1) * 64], u == 0, False,
                   ["vtok", "scss"], [("ps", bo)] if u == 0 else [])
            SRB = [SRING, W(MB + 11648, 2048)]
            SBB = [SBFS, WB(MB + 11648 + 2048, 1024)]
            for bp in range(8):
                b0 = 2 * bp
                i2 = bp % 2
                xw_s = [("wr", 0)] if i2 == 1 else []
                xw_b = [("wr", 1)] if i2 == 1 else []
                SRc, SBc = SRB[i2], SBB[i2]
                SR4 = SRc[:, 0:2 * NV].rearrange("p (b u v) -> p b u v", b=2, v=128)
                SB4 = SBc[:, 0:2 * NV].rearrange("p (b u v) -> p b u v", b=2, v=128)
                xx = 256 if gla else 128
                SRd = SRc[:, 0:2 * NV].rearrange("p (b h x) -> p b h x", b=2, x=xx)
                sin_v, sout_v = [], []
                for bi in range(2):
                    if gla:
                        sin_v.append(sgla[l, b0 + bi].rearrange("h d x -> d h x"))
                        sout_v.append(sst_gla[l, b0 + bi].rearrange("h d x -> d h x"))
                    else:
                        sin_v.append(shg[l, b0 + bi, H0:H0 + 4].rearrange("h d x -> d h x"))
                        sout_v.append(sst_hg[l, b0 + bi, H0:H0 + 4].rearrange("h d x -> d h x"))
                kr = [("sring", i2, 0), ("sring", i2, 1)]
                for bi in range(2):
                    DQ(lambda e, bi=bi: e.dma_start(out=SRd[:, bi], in_=sin_v[bi]), w=[("sring", i2, bi)] + xw_s, key=("sin", i2, bi))
                S(lambda e: e.activation(out=SBc[:, 0:2 * NV], in_=SRc[:, 0:2 * NV], func=AF.Copy), kr + xw_s, [("sbfs", i2)] + xw_b)
                for bi in range(2):
                    bb_ = b0 + bi
                    for u in range(NU):
                        hq = u // upq
                        lastmm = (bp == 7 and bi == 1 and u == NU - 1)
                        mm(PS[bo][:, u * 64 + 4 * bb_:u * 64 + 4 * bb_ + 4], SB4[:, bi, u, :], QT[:, hq, 1024 + 4 * bb_:1024 + 4 * bb_ + 4], False, lastmm,
                           [("sbfs", i2), "qt"] + xw_b, [("ps", bo)] if lastmm else [])
                KW = NQ * 128
                X = upq * 128
                V(lambda e, b0=b0: e.tensor_tensor(out=VM[0:64, 0:2 * KW].rearrange("p (b x) -> p b x", b=2),
                                                   in0=KTOK[0:64, 8, 0:KW].unsqueeze(1).to_broadcast([64, 2, KW]),
                                                   in1=BSEL[0:64, b0:b0 + 2].unsqueeze(2).to_broadcast([64, 2, KW]), op=ALU.mult), ["ktok", "con"], ["vm"])
                KM3 = VM[0:64, 0:2 * KW].rearrange("p (b x) -> p b x", b=2)
                pb0 = 4 if (gla or bp % 2 == 0) else 6
                for bi in range(2):
                    for h in range(NQ):
                        f = bi * NQ + h
                        b = pb0 + (f * X) // 512
                        o = (f * X) % 512
                        mm(PS[b][:, o:o + X], KM3[:, bi, h * 128:(h + 1) * 128], VTOK[0:64, 8, h * X:(h + 1) * X], o == 0, True,
                           ["vm", "vtok"], [("ps", b)])
                for bi in range(2):
                    for h in range(NQ):
                        f = bi * NQ + h
                        b = pb0 + (f * X) // 512
                        o = (f * X) % 512
                        V(lambda e, bi=bi, h=h, b=b, o=o: e.scalar_tensor_tensor(out=SRc[:, bi * NV + h * X:bi * NV + (h + 1) * X],
                                                                                 in0=SRc[:, bi * NV + h * X:bi * NV + (h + 1) * X],
                                                                                 scalar=ELS[:, h, b0 + bi:b0 + bi + 1], in1=PS[b][:, o:o + X],
                                                                                 op0=ALU.mult, op1=ALU.add),
                          [("sring", i2, bi), "tab", ("sbfs", i2), ("ps", b)] + xw_s, [("sring", i2, bi)] + xw_s)
                for bi in range(2):
                    DG(lambda e, bi=bi: e.dma_start(out=sout_v[bi], in_=SRd[:, bi]), [("sring", i2, bi)] + xw_s, key=("sout", i2, bi))
            norm_gate(bo, bsc, 1024)
            init_prompt_state()
            for c in range(16):
                par = c % 2
                bsc, bo = 4 * par, 4 * par + 1
                bP = [4 * par + 2, 4 * par + 3][:nPb]
                blk, hf = c // 2, c % 2
                r0 = 64 * hf
                cs = slice(64 * c, 64 * c + 64)
                state_P(c)
                for h in range(NQ):
                    mm(PS[bsc][r0:r0 + 64, h * 64:(h + 1) * 64], KT[:, h, cs], QT[:, h, cs], h == 0, h == NQ - 1, ["kt", "qt"],
                       [("ps", bsc)] if h in (0, NQ - 1) else [])
                V(lambda e, bsc=bsc, r0=r0: e.tensor_tensor(out=SCS[r0:r0 + 64, 0:NQ * 64].rearrange("p (h t) -> p h t", t=64),
                                                           in0=PS[bsc][r0:r0 + 64, 0:NQ * 64].rearrange("p (h t) -> p h t", t=64),
                                                           in1=CMASK[r0:r0 + 64, :].unsqueeze(1).to_broadcast([64, NQ, 64]), op=ALU.mult),
                  [("ps", bsc), "con"], ["scs"])
                for h in range(NQ):
                    X = upq * 128
                    S(lambda e, h=h, c=c, X=X: e.activation(out=SBF[:, h * X:(h + 1) * X], in_=SST[:, h * X:(h + 1) * X], func=AF.Copy,
                                                            scale=EE[:, h, 0, c:c + 1]), ["sst", "tab"], ["sbf"])
                for u in range(NU):
                    hq = u // upq
                    mm(PS[bo][:, u * 64:(u + 1) * 64], VTOK[r0:r0 + 64, blk, u * 128:(u + 1) * 128], SCS[r0:r0 + 64, hq * 64:(hq + 1) * 64],
                       u == 0, False, ["vtok", "scs"], [("ps", bo)] if u == 0 else [])
                for u in range(NU):
                    hq = u // upq
                    mm(PS[bo][:, u * 64:(u + 1) * 64], SBF[:, u * 128:(u + 1) * 128], QT[:, hq, cs], False, u == NU - 1, ["sbf", "qt"],
                       [("ps", bo)] if u == NU - 1 else [])
                state_U(c)
                if c > 0:
                    pp = (c - 1) % 2
                    norm_gate(4 * pp + 1, 4 * pp, 64 * (c - 1))
            norm_gate(5, 4, 64 * 15)
            DQ(lambda e: e.dma_start(out=pst_v, in_=S4), ["sst"], key="pst")

        P.barrier()
        DQ(lambda e: e.dma_start(out=XTf, in_=xsp), ["xsp"], ["xt"], key="fill")
        wov = wmo[l].rearrange("(k p) c -> p k c", p=128)
        for mp in range(8):
            so = load_wr(wov, mp * 256)
            for ci in range(2):
                m = 2 * mp + ci
                bset = [3 * (m % 2), 3 * (m % 2) + 1, 3 * (m % 2) + 2]
                proj_fm(WR[so][:, :, ci * 128:(ci + 1) * 128], ("wr", so), MERGED, "merged", bset)
                for nt, (t0, n) in enumerate(NTILES):
                    V(lambda e, m=m, nt=nt, t0=t0, n=n, bset=bset: e.tensor_copy(out=OUTM[:, m, t0:t0 + n], in_=PS[bset[nt]][:, 0:n]),
                      [("ps", bset[nt])], [("outm", m)])
        for nt, (t0, n) in enumerate(NTILES):
            b = 6 if nt < 2 else 7
            o = 0 if nt != 1 else 0
            for m in range(16):
                sq = SQN if m % 2 == 0 else SCS
                S(lambda e, sq=sq, m=m, t0=t0, n=n: e.activation(out=sq[:, 0:n], in_=OUTM[:, m, t0:t0 + n], func=AF.Square), [("outm", m)], [("sqo", m % 2)])
                bb_ = [0, 1, 2][nt]
                mm(PS[bb_][:, 0:n], ONESB, sq[:, 0:n], m == 0, m == 15, [("sqo", m % 2), "onesb"], [("ps", bb_)] if m in (0, 15) else [])
        RS2 = W(MB + 8704, T)
        for nt, (t0, n) in enumerate(NTILES):
            rstd_ln(PS[nt][:, 0:n], nt, RS2[:, t0:t0 + n], 1.0 / D, "rs2")
        for m in range(16):
            V(lambda e, m=m: e.tensor_tensor(out=OUTM[:, m, :], in0=OUTM[:, m, :], in1=RS2, op=ALU.mult), [("outm", m), "rs2"], [("outm", m)])
            V(lambda e, m=m: e.scalar_tensor_tensor(out=XT[:, m, :], in0=OUTM[:, m, :], scalar=gpost[:, m:m + 1], in1=XT[:, m, :],
                                                    op0=ALU.mult, op1=ALU.add), [("outm", m), "xt", "par"], ["xt", ("xtm", m)])
            for nt, (t0, n) in enumerate(NTILES):
                sq = SQN if (3 * m + nt) % 2 == 0 else SCS
                S(lambda e, sq=sq, m=m, t0=t0, n=n: e.activation(out=sq[:, 0:n], in_=XT[:, m, t0:t0 + n], func=AF.Square), [("xtm", m)], [("sqo", (3 * m + nt) % 2)])
                mm(PS[5 + nt][:, 0:n], ONESB, sq[:, 0:n], m == 0, m == 15, [("sqo", (3 * m + nt) % 2), "onesb"], [("ps", 5 + nt)] if m in (0, 15) else [])
        state["ssq_ready"] = True
        P.barrier()

    nst = 0
    for l in range(2):
        if nst < stage:
            ffn(l, w1i, w1o, 0, 1)
            P.barrier()
        nst += 1
        if nst < stage:
            mixer(l)
        nst += 1
        if nst < stage:
            ffn(l, w2i, w2o, 4, 5, nxt=(l == 0))
            P.barrier()
        nst += 1

    YST = [W(PB, 2048), W(PB + 2048, 2048)]
    for blk in range(9):
        rows = 128 if blk < 8 else 64
        ys = YST[blk % 2]
        for q in range(4):
            bk = (blk * 4 + q) % 2
            for i in range(4):
                k = 4 * q + i
                TE(lambda e, bk=bk, i=i, k=k, blk=blk, rows=rows: e.transpose(out=PS[bk][0:rows, i * 128:(i + 1) * 128],
                                                                             in_=XT[:, k, blk * 128:blk * 128 + rows], identity=IDF),
                   ["xt", "con"], [("ps", bk)] if i in (0, 3) else [])
            V(lambda e, bk=bk, q=q, ys=ys, rows=rows: e.tensor_copy(out=ys[0:rows, q * 512:(q + 1) * 512], in_=PS[bk][0:rows, 0:512]),
              [("ps", bk)], [("yst", blk % 2)])
        DQ(lambda e, ys=ys, blk=blk, rows=rows: e.dma_start(out=y[blk * 128:blk * 128 + rows, :], in_=ys[0:rows, :]), [("yst", blk % 2)],
           key=("ys", blk % 2))
    P.emit()
    return nc, P


def make_consts():
    c = np.zeros((128, 336), np.float32)
    c[:, 0:128] = np.eye(128, dtype=np.float32)
    p = np.arange(128)[:, None]
    t = np.arange(64)[None, :]
    c[:, 128:192] = ((p % 64) <= t).astype(np.float32)
    c[:, 192:256] = (((p % 64) // 4 == t // 4) & ((p % 64) <= t)).astype(np.float32)
    b = np.arange(16)[None, :]
    c[:, 256:272] = ((p % 64) // 4 == b).astype(np.float32)
    c[:, 272:336] = (np.arange(64)[None, :] % 4 != 0).astype(np.float32)
    return c


_CACHE = {}


def _get_nc(stage=99):
    if stage not in _CACHE:
        _CACHE[stage] = build(stage)[0]
    return _CACHE[stage]


def kernel(**inputs):
    f = lambda a: np.ascontiguousarray(np.asarray(a, dtype=np.float32))
    xp = f(inputs["x_prompt"])
    xs = f(inputs["x_sample"])
    sg = f(inputs["state_gla"])
    sh = f(inputs["state_hgrn"])
    shared = {k: f(inputs[k]) for k in ("norm_gains", "ffn1_w_in", "ffn1_w_out", "ffn2_w_in", "ffn2_w_out", "mix_w_in",
                                        "gla_w_gate_up", "gla_b_gate", "gla_norm", "hgrn_gamma", "hgrn_norm", "mix_w_out")}
    cst = make_consts()
    zg = np.zeros((2, 4, 128, 256), np.float32)
    zh = np.zeros((2, 8, 128, 128), np.float32)
    in_maps = []
    for c in range(NCORES):
        s, half = c // 2, c % 2
        xin = np.concatenate([xp[s, 1024 * half:1024 * half + 1024], xs[16 * c:16 * c + 16].reshape(64, D)], axis=0)
        m = dict(shared)
        m["xin"] = np.ascontiguousarray(xin)
        m["sgla"] = np.ascontiguousarray(sg[:, 16 * c:16 * c + 16])
        m["shg"] = np.ascontiguousarray(sh[:, 16 * c:16 * c + 16])
        m["pin_gla"] = zg
        m["pin_hg"] = zh
        m["consts"] = cst
        m["coremask"] = np.full((128, 1), float(half), np.float32)
        in_maps.append(m)
    nc = _get_nc()
    r2 = run_bass_kernel_spmd(nc, in_maps, core_ids=list(range(NCORES))).results
    y_prompt = np.zeros((4, 2048, D), np.float32)
    y_sample = np.zeros((128, 4, D), np.float32)
    st_gp = np.zeros((2, 4, 4, 128, 256), np.float32)
    st_hp = np.zeros((2, 4, 8, 128, 128), np.float32)
    st_gs = np.zeros((2, 128, 4, 128, 256), np.float32)
    st_hs = np.zeros((2, 128, 8, 128, 128), np.float32)
    for c in range(NCORES):
        s, half = c // 2, c % 2
        r = r2[c]
        y_prompt[s, 1024 * half:1024 * half + 1024] = r["y"][0:1024]
        y_sample[16 * c:16 * c + 16] = r["y"][1024:1088].reshape(16, 4, D)
        st_gs[:, 16 * c:16 * c + 16] = r["sst_gla"]
        st_hs[:, 16 * c:16 * c + 16] = r["sst_hg"]
        if half == 1:
            st_gp[:, s] = r["pst_gla"]
            st_hp[:, s] = r["pst_hg"]
    return (y_prompt, y_sample, st_gp, st_hp, st_gs, st_hs)
```

```python
import numpy as np
import concourse.bass as bass
import concourse.mybir as mybir
from concourse.bass_utils import run_bass_kernel_spmd

F32 = mybir.dt.float32
BF16 = mybir.dt.bfloat16
AF = mybir.ActivationFunctionType
ALU = mybir.AluOpType

D = 2048
T = 1088
NJ = 44
NCORES = 8
EPS = 1e-6


class _Rec:
    def __init__(self):
        self.calls = []

    def __getattr__(self, name):
        def call(*a, **kw):
            self.calls.append((name, a, kw))
            return self
        return call


class Prog:
    ENGS = ("tensor", "vector", "scalar", "gpsimd", "sync")

    def __init__(self, nc):
        self.nc = nc
        self.ops = []
        self.last_w = {}
        self.readers = {}
        self.dma_keys = {}
        self.bar = []
        self.last_eng = {}

    def barrier(self):
        self.bar = list(self.last_eng.values()) + list(self.dma_keys.values())

    def op(self, eng, fn, reads=(), writes=(), dma=None):
        i = len(self.ops)
        if eng != "tensor":
            extra = [k for k in reads if isinstance(k, tuple) and k[0] == "ps" and k not in writes]
            if extra:
                writes = list(writes) + extra
        deps = set((b, "raw") for b in self.bar)
        for k in reads:
            if k in self.last_w:
                deps.add((self.last_w[k], "raw"))
        for k in writes:
            if k in self.last_w:
                deps.add((self.last_w[k], "waw"))
            for r in self.readers.get(k, {}).values():
                deps.add((r, "war"))
        for k in writes:
            self.last_w[k] = i
            self.readers[k] = {}
        for k in reads:
            self.readers.setdefault(k, {})[eng if dma is None else ("dma", i)] = i
        if dma is not None:
            prev = self.dma_keys.get(dma)
            if prev is not None:
                deps.add((prev, "waw"))
            self.dma_keys[dma] = i
        else:
            self.last_eng[eng] = i
        rec = _Rec()
        fn(rec)
        assert len(rec.calls) == 1, rec.calls
        self.ops.append(dict(eng=eng, call=rec.calls[0], deps=deps, dma=dma, sig=False))
        return i

    def emit(self):
        nc = self.nc
        ops = self.ops
        for i, o in enumerate(ops):
            best = {}
            nn = []
            for (p, kind) in o["deps"]:
                po = ops[p]
                if po["dma"] is not None:
                    nn.append(p)
                    continue
                if po["eng"] == o["eng"] and o["dma"] is None:
                    if o["eng"] == "tensor" or kind == "war":
                        continue
                e2 = po["eng"]
                if e2 not in best or best[e2] < p:
                    best[e2] = p
            o["need"] = nn + list(best.values())
            for p in best.values():
                ops[p]["sig"] = True
        cnt = {e: 0 for e in self.ENGS}
        dcnt = {}
        for o in ops:
            if o["dma"] is not None:
                dcnt[o["dma"]] = dcnt.get(o["dma"], 0) + (1 if (isinstance(o["dma"], tuple) and o["dma"][0] == "cc") else 16)
                o["val"] = dcnt[o["dma"]]
            elif o["sig"]:
                cnt[o["eng"]] += 1
                o["val"] = cnt[o["eng"]]
        sems = {e: nc.alloc_semaphore("s_" + e) for e in self.ENGS}
        dsems = {k: nc.alloc_semaphore("d_%d" % n) for n, k in enumerate(dcnt)}
        self.stats = dict(cnt=dict(cnt), ndma=len(dcnt), nops=len(ops))

        def run(ename, eng):
            waited = {}
            for o in ops:
                if o["eng"] != ename:
                    continue
                for p in o["need"]:
                    po = ops[p]
                    if po["dma"] is not None:
                        s = dsems[po["dma"]]
                        key = ("d", po["dma"])
                    else:
                        s = sems[po["eng"]]
                        key = ("e", po["eng"])
                    v = po["val"]
                    if waited.get(key, 0) >= v:
                        continue
                    waited[key] = v
                    eng.wait_ge(s, v)
                name, a, kw = o["call"]
                ins = getattr(eng, name)(*a, **kw)
                if o["dma"] is not None:
                    ins.then_inc(dsems[o["dma"]], 1 if (isinstance(o["dma"], tuple) and o["dma"][0] == "cc") else 16)
                elif o["sig"]:
                    ins.then_inc(sems[ename], 1)
            if ename == "sync":
                for e in self.ENGS:
                    if e != "sync" and cnt[e] > 0:
                        eng.wait_ge(sems[e], cnt[e])
                for k, v in dcnt.items():
                    eng.wait_ge(dsems[k], v)

        with nc.Block() as block:
            @block.tensor
            def _(e):
                run("tensor", e)

            @block.vector
            def _(e):
                run("vector", e)

            @block.scalar
            def _(e):
                run("scalar", e)

            @block.gpsimd
            def _(e):
                run("gpsimd", e)

            @block.sync
            def _(e):
                run("sync", e)


GROUPS = [
    dict(name="gla", kind="gla", NQ=4, NU=8, upq=2, H0=0, mch0=0),
    dict(name="hga", kind="hgrn", NQ=4, NU=4, upq=1, H0=0, mch0=8),
    dict(name="hgb", kind="hgrn", NQ=4, NU=4, upq=1, H0=4, mch0=12),
]
C_Q, C_K, C_V, C_A, C_R, C_HQ, C_HF, C_HI, C_HG = 0, 512, 1024, 2048, 2064, 3088, 4112, 5136, 6160
NTILES = [(0, 512), (512, 512), (1024, 64)]


def build(stage=99, fused=True):
    nc = bass.Bass("TRN2", target_bir_lowering=False)

    def din(name, shape):
        return nc.dram_tensor(name, list(shape), F32, kind="ExternalInput").ap()

    def dout(name, shape):
        return nc.dram_tensor(name, list(shape), F32, kind="ExternalOutput").ap()

    xin = din("xin", [T, D])
    sgla = din("sgla", [2, 16, 4, 128, 256])
    shg = din("shg", [2, 16, 8, 128, 128])
    pin_gla = din("pin_gla", [2, 4, 128, 256])
    pin_hg = din("pin_hg", [2, 8, 128, 128])
    ng = din("norm_gains", [2, 6, D])
    w1i = din("ffn1_w_in", [2, D, 2 * 5632])
    w1o = din("ffn1_w_out", [2, 5632, D])
    w2i = din("ffn2_w_in", [2, D, 2 * 5632])
    w2o = din("ffn2_w_out", [2, 5632, D])
    wmi = din("mix_w_in", [2, D, 7184])
    wgu = din("gla_w_gate_up", [2, 16, 512])
    bgt = din("gla_b_gate", [2, 512])
    gnm = din("gla_norm", [2, 256])
    hgm = din("hgrn_gamma", [2, 1024])
    hnm = din("hgrn_norm", [2, 128])
    wmo = din("mix_w_out", [2, D, D])
    cst = din("consts", [128, 336])
    cmk = din("coremask", [128, 1])
    y = dout("y", [T, D])
    pst_gla = dout("pst_gla", [2, 4, 128, 256])
    pst_hg = dout("pst_hg", [2, 8, 128, 128])
    sst_gla = dout("sst_gla", [2, 16, 4, 128, 256])
    sst_hg = dout("sst_hg", [2, 16, 8, 128, 128])
    xsp = nc.dram_tensor("xspill", [128, 16 * T], F32).ap()

    P = Prog(nc)
    A = nc.alloc_sbuf_tensor("arena", [128, 52000], F32)
    PS = [nc.alloc_psum_tensor("ps%d" % i, [128, 512], F32) for i in range(8)]

    def W(off, n):
        return A[:, off:off + n]

    def WB(off, nw):
        return A[:, off:off + nw].bitcast(BF16)

    XTf = W(0, 16 * T)
    XT = XTf.rearrange("p (k t) -> p k t", t=T)
    PAR = W(17408, 512)
    GAIN = PAR[:, 0:192]
    NEGB = PAR[:, 192:200]
    GNW = PAR[:, 200:204]
    HNW = PAR[:, 204:206]
    LB = PAR[:, 206:222]
    OML = PAR[:, 222:238]
    ONEC = PAR[:, 300:301]
    EPSC = PAR[:, 301:302]
    RAW3 = PAR[:, 320:350]
    MASKC = PAR[:, 302:303]
    CON = W(17920, 512)
    IDF = CON[:, 0:128]
    CMASK = CON[:, 128:192]
    SMASK = CON[:, 192:256]
    BSEL = CON[:, 256:272]
    RMASK = CON[:, 272:336]
    IDB = WB(17920 + 336, 64)
    ONESB = WB(17920 + 400, 64)
    PB = 18432
    ACT = WB(PB, 11968).rearrange("p (j t) -> p j t", t=544)
    OUT = W(PB + 11968, 8704).rearrange("p (m t) -> p m t", t=544)
    HT = WB(PB + 11968, 4352).rearrange("p (k t) -> p k t", t=544)
    RSTD = W(PB + 20672, 544)
    TMP = [W(PB + 21216, 544), W(PB + 21760, 544)]
    SQM = [WB(PB + 22304, 272), WB(PB + 22576, 272)]
    RING = [WB(PB + 22848 + 2816 * s, 2816) for s in range(3)]
    RG = [r[:, 0:2048].rearrange("p (k c) -> p k c", c=128) for r in RING]
    RU = [r[:, 2048:4096].rearrange("p (k c) -> p k c", c=128) for r in RING]
    RO = [r[:, 0:5632].rearrange("p (j c) -> p j c", c=128) for r in RING]
    XSTG = [W(PB, 2048), W(PB + 2048, 2048)]
    STG = [W(PB + 4096, 128), W(PB + 4224, 128), W(PB + 4352, 128)]
    MERGED = WB(PB, 8704).rearrange("p (k t) -> p k t", t=T)
    HTM = WB(PB + 8704, 8704).rearrange("p (k t) -> p k t", t=T)
    OUTM = W(PB + 8704, 17408).rearrange("p (m t) -> p m t", t=T)
    QT = WB(0, 2176).rearrange("p (h t) -> p h t", t=T)
    KT = WB(2176, 2176).rearrange("p (h t) -> p h t", t=T)
    KTOK = WB(4352, 2304).rearrange("p (b c) -> p b c", c=512)
    VTOK = WB(6656, 4608).rearrange("p (b c) -> p b c", c=1024)
    GATE = WB(11264, 4352).rearrange("p (u t) -> p u t", t=T)
    ALR = WB(15616, 544)
    WUP = WB(16160, 256)
    TABo = 16416
    TM = W(TABo, 64).rearrange("p (h c) -> p h c", c=16)
    TPV = W(TABo + 64, 64).rearrange("p (h c) -> p h c", c=16)
    EE = W(TABo + 128, 192).rearrange("p (h e c) -> p h e c", e=3, c=16)
    ELS = W(TABo + 320, 64).rearrange("p (h c) -> p h c", c=16)
    WALR = WB(16928, 128).rearrange("p (k c) -> p k c", c=16)
    MB = PB + 17408
    ZG = W(MB, T)
    D1 = W(MB + 1088, T)
    EK = W(MB + 2176, T)
    KR = W(MB + 3264, T)
    SST = W(MB + 4352, 1024)
    SBF = WB(MB + 5376, 512)
    SRING = W(MB + 5888, 2048)
    SBFS = WB(MB + 7936, 1024)
    VM = WB(MB + 8960, 1024)
    SCS = WB(MB + 9984, 256)
    SQN = WB(MB + 10240, 256)
    RSTDN = W(MB + 10496, 512)
    T1 = W(MB + 11008, 512)
    SCSS = WB(MB + 11520, 128)
    WR = [WB(MB + 11648 + 2048 * s, 2048).rearrange("p (k c) -> p k c", c=256) for s in range(2)]

    state = dict(ring=0, wr=0)

    def V(fn, r=(), w=()):
        return P.op("vector", fn, r, w)

    def S(fn, r=(), w=()):
        return P.op("scalar", fn, r, w)

    def TE(fn, r=(), w=()):
        return P.op("tensor", fn, r, w)

    def DQ(fn, r=(), w=(), key=None):
        return P.op("sync", fn, r, w, dma=key)

    def DG(fn, r=(), w=(), key=None):
        return P.op("gpsimd", fn, r, w, dma=key)

    def mm(out, lhsT, rhs, start, stop, r, w=()):
        return TE(lambda e: e.matmul(out, lhsT=lhsT, rhs=rhs, start=start, stop=stop, skip_group_check=True), r, w)

    DQ(lambda e: e.dma_start(out=CON[:, 0:336], in_=cst), w=["con"], key="cst")
    V(lambda e: e.tensor_copy(out=IDB, in_=IDF), ["con"], ["idb"])
    DQ(lambda e: e.dma_start(out=MASKC, in_=cmk), w=["par"], key="cmk")
    V(lambda e: e.memset(ONESB, 1.0), (), ["onesb"])
    V(lambda e: e.memset(ONEC, 1.0), (), ["par"])
    V(lambda e: e.memset(EPSC, EPS), (), ["par"])
    ngv = ng.rearrange("l i (k p) -> (l i k) p", p=128)
    DQ(lambda e: e.dma_start(out=STG[0][0:96, :], in_=ngv[0:96, :]), w=["stg0"], key="p0")
    DQ(lambda e: e.dma_start(out=STG[1][0:96, :], in_=ngv[96:192, :]), w=["stg1"], key="p1")
    DQ(lambda e: e.dma_start(out=STG[2][0:8, :], in_=bgt.rearrange("l (h p) -> (l h) p", p=128)), w=["stg2a"], key="p2")
    DQ(lambda e: e.dma_start(out=STG[2][8:12, :], in_=gnm.rearrange("l (c p) -> (l c) p", p=128)), w=["stg2b"], key="p3")
    DQ(lambda e: e.dma_start(out=STG[2][12:14, :], in_=hnm), w=["stg2c"], key="p4")
    DQ(lambda e: e.dma_start(out=STG[2][14:30, :], in_=hgm.rearrange("l (h p) -> (l h) p", p=128)), w=["stg2d"], key="p5")
    TE(lambda e: e.transpose(out=PS[0][:, 0:96], in_=STG[0][0:96, :], identity=IDF[0:96, 0:96]), ["stg0", "con"], [("ps", 0)])
    V(lambda e: e.tensor_copy(out=GAIN[:, 0:96], in_=PS[0][:, 0:96]), [("ps", 0)], ["par"])
    TE(lambda e: e.transpose(out=PS[1][:, 0:96], in_=STG[1][0:96, :], identity=IDF[0:96, 0:96]), ["stg1", "con"], [("ps", 1)])
    V(lambda e: e.tensor_copy(out=GAIN[:, 96:192], in_=PS[1][:, 0:96]), [("ps", 1)], ["par"])
    TE(lambda e: e.transpose(out=PS[2][:, 0:30], in_=STG[2][0:30, :], identity=IDF[0:30, 0:30]),
       ["stg2a", "stg2b", "stg2c", "stg2d", "con"], [("ps", 2)])
    V(lambda e: e.tensor_copy(out=RAW3, in_=PS[2][:, 0:30]), [("ps", 2)], ["par"])
    V(lambda e: e.tensor_scalar(out=NEGB, in0=PAR[:, 320:328], scalar1=-1.0, scalar2=None, op0=ALU.mult), ["par"], ["par"])
    V(lambda e: e.tensor_copy(out=GNW, in_=PAR[:, 328:332]), ["par"], ["par"])
    V(lambda e: e.tensor_copy(out=HNW, in_=PAR[:, 332:334]), ["par"], ["par"])
    g0, g1 = PAR[:, 334:342], PAR[:, 342:350]
    mx, e0, e1, ss, rs, p0, p1, c1 = [PAR[:, 352 + 8 * i:360 + 8 * i] for i in range(8)]
    V(lambda e: e.tensor_tensor(out=mx, in0=g0, in1=g1, op=ALU.max), ["par"], ["par"])
    V(lambda e: e.tensor_tensor(out=e0, in0=g0, in1=mx, op=ALU.subtract), ["par"], ["par"])
    V(lambda e: e.tensor_tensor(out=e1, in0=g1, in1=mx, op=ALU.subtract), ["par"], ["par"])
    S(lambda e: e.activation(out=e0, in_=e0, func=AF.Exp), ["par"], ["par"])
    S(lambda e: e.activation(out=e1, in_=e1, func=AF.Exp), ["par"], ["par"])
    V(lambda e: e.tensor_tensor(out=ss, in0=e0, in1=e1, op=ALU.add), ["par"], ["par"])
    V(lambda e: e.reciprocal(out=rs, in_=ss), ["par"], ["par"])
    V(lambda e: e.tensor_tensor(out=p0, in0=e0, in1=rs, op=ALU.mult), ["par"], ["par"])
    V(lambda e: e.tensor_tensor(out=p1, in0=e1, in1=rs, op=ALU.mult), ["par"], ["par"])
    V(lambda e: e.tensor_tensor(out=c1, in0=p0, in1=p1, op=ALU.add), ["par"], ["par"])
    V(lambda e: e.tensor_tensor(out=LB[:, 0:8], in0=p0, in1=p0, op=ALU.subtract), ["par"], ["par"])
    V(lambda e: e.tensor_tensor(out=LB[:, 8:16], in0=c1, in1=p0, op=ALU.subtract), ["par"], ["par"])
    V(lambda e: e.tensor_scalar(out=OML, in0=LB, scalar1=-1.0, scalar2=1.0, op0=ALU.mult, op1=ALU.add), ["par"], ["par"])

    for blk in range(9):
        rows = 128 if blk < 8 else 64
        xs = XSTG[blk % 2]
        DQ(lambda e, xs=xs, blk=blk, rows=rows: e.dma_start(out=xs[0:rows, :], in_=xin[blk * 128:blk * 128 + rows, :]),
           w=[("xstg", blk % 2)], key=("xl", blk % 2))
        for q in range(4):
            bk = (blk * 4 + q) % 2
            for i in range(4):
                k = 4 * q + i
                TE(lambda e, bk=bk, i=i, k=k, xs=xs, rows=rows: e.transpose(
                    out=PS[bk][:, i * 128:i * 128 + rows], in_=xs[0:rows, k * 128:(k + 1) * 128], identity=IDF[0:rows, 0:rows]),
                   [("xstg", blk % 2), "con"], [("ps", bk)] if i in (0, 3) else [])
            V(lambda e, bk=bk, q=q, blk=blk, rows=rows: e.tensor_copy(
                out=XT[:, 4 * q:4 * q + 4, blk * 128:blk * 128 + rows],
                in_=PS[bk][:, 0:512].rearrange("p (i t) -> p i t", t=128)[:, :, 0:rows]),
              [("ps", bk)], ["xt"])
    P.barrier()

    def rstd_from(psA, nA, psB, nB, dstA, dstB, scale):
        S(lambda e: e.activation(out=dstA, in_=psA, func=AF.Sqrt, bias=EPSC, scale=scale), [("ps", nA), "par"], ["rstd"])
        S(lambda e: e.activation(out=dstB, in_=psB, func=AF.Sqrt, bias=EPSC, scale=scale), [("ps", nB), "par"], ["rstd"])

    ACT2 = WB(PB, 23936).rearrange("p (j t) -> p j t", t=T)
    HT2 = WB(0, 8704).rearrange("p (k t) -> p k t", t=T)
    HTT = WB(PB + 23936, 8704).rearrange("p (k t) -> p k t", t=T)
    RING2 = [WB(PB + 23936 + 2816 * s_, 2816) for s_ in range(3)]
    RG2 = [r[:, 0:2048].rearrange("p (k c) -> p k c", c=128) for r in RING2]
    RU2 = [r[:, 2048:4096].rearrange("p (k c) -> p k c", c=128) for r in RING2]
    RO2 = [r[:, 0:5632].rearrange("p (j c) -> p j c", c=128) for r in RING2]
    SQ2 = [WB(PB + 23936 + 8704 + 256 * i_, 256) for i_ in range(2)]
    SQ2B = [WB(PB + 23936 + 8448 + 544 * i_, 544) for i_ in range(2)]
    TMP2 = [W(8704 + 512 * i_, 512) for i_ in range(4)]
    RSTD2 = W(PB, T)
    RSTD2P = W(PB + 17408, T)
    XF = W(PB, 17408).rearrange("p (k t) -> p k t", t=T)

    def rstd_ln(ps_ap, bank, dst, scale, wkey):
        S(lambda e: e.activation(out=dst, in_=ps_ap, func=AF.Ln, bias=EPSC, scale=scale), [("ps", bank), "par"], [wkey])
        S(lambda e: e.activation(out=dst, in_=dst, func=AF.Exp, scale=-0.5), [wkey], [wkey])

    def ffn(l, wi, wo, gi_pre, gi_post, nxt=True):
        wiv = wi[l].rearrange("(k p) c -> p k c", p=128)
        wov = wo[l].rearrange("(j p) c -> p j c", p=128)
        gpre = GAIN[:, (l * 6 + gi_pre) * 16:(l * 6 + gi_pre) * 16 + 16]
        gpost = GAIN[:, (l * 6 + gi_post) * 16:(l * 6 + gi_post) * 16 + 16]
        DQ(lambda e: e.dma_start(out=xsp, in_=XTf), ["xt"], ["xsp"], key="spill")
        for nt, (t0, n) in enumerate(NTILES):
            b = 5 + nt
            if not state.get("ssq_ready"):
                for k in range(16):
                    sq = SQ2[k % 2]
                    S(lambda e, sq=sq, k=k, t0=t0, n=n: e.activation(out=sq[:, 0:n], in_=XT[:, k, t0:t0 + n], func=AF.Square), ["xt"], [("sq2", k % 2)])
                    mm(PS[b][:, 0:n], ONESB, sq[:, 0:n], k == 0, k == 15, [("sq2", k % 2), "onesb"], [("ps", b)] if k in (0, 15) else [])
            rstd_ln(PS[b][:, 0:n], b, RSTD2[:, t0:t0 + n], 1.0 / D, "rstd2")
        state["ssq_ready"] = False
        for k in range(16):
            V(lambda e, k=k: e.scalar_tensor_tensor(out=HTT[:, k, :], in0=XT[:, k, :], scalar=gpre[:, k:k + 1], in1=RSTD2, op0=ALU.mult, op1=ALU.mult),
              ["xt", "rstd2", "par"], ["htt"])
        V(lambda e: e.tensor_copy(out=HT2[:, :, :], in_=HTT[:, :, :]), ["htt"], ["ht2", "xt"])
        nfirst = 0
        for j in range(NJ):
            s = state["ring"] % 3
            state["ring"] += 1
            xw = ["htt"] if nfirst < 3 else []
            nfirst += 1
            DG(lambda e, s=s, j=j: e.dma_start(out=RG2[s], in_=wiv[:, :, j * 128:(j + 1) * 128]), w=[("ra", s)] + xw, key=("ring", s, 0))
            DG(lambda e, s=s, j=j: e.dma_start(out=RU2[s], in_=wiv[:, :, 5632 + j * 128:5632 + (j + 1) * 128]), w=[("rb", s)] + xw, key=("ring", s, 1))
            for nt, (t0, n) in enumerate(NTILES):
                pidx = (3 * j + nt) % 4
                bg, bu = 2 * pidx, 2 * pidx + 1
                for k in range(16):
                    mm(PS[bg][:, 0:n], RG2[s][:, k, :], HT2[:, k, t0:t0 + n], k == 0, k == 15, [("ra", s), "ht2"], [("ps", bg)] if k in (0, 15) else [])
                for k in range(16):
                    mm(PS[bu][:, 0:n], RU2[s][:, k, :], HT2[:, k, t0:t0 + n], k == 0, k == 15, [("rb", s), "ht2"], [("ps", bu)] if k in (0, 15) else [])
                tm = TMP2[pidx]
                S(lambda e, tm=tm, bg=bg, n=n: e.activation(out=tm[:, 0:n], in_=PS[bg][:, 0:n], func=AF.Silu), [("ps", bg)], [("tmp2", pidx)])
                V(lambda e, tm=tm, bu=bu, j=j, t0=t0, n=n: e.tensor_tensor(out=ACT2[:, j, t0:t0 + n], in0=tm[:, 0:n], in1=PS[bu][:, 0:n], op=ALU.mult),
                  [("tmp2", pidx), ("ps", bu)], [("act", j, nt)])
        for m in range(16):
            s = state["ring"] % 3
            state["ring"] += 1
            DG(lambda e, s=s, m=m: e.dma_start(out=RO2[s], in_=wov[:, :, m * 128:(m + 1) * 128]), w=[("ra", s), ("rb", s)], key=("ring", s, 0))
            for nt, (t0, n) in enumerate(NTILES):
                if nt < 2:
                    b, co = 3 * (m % 2) + nt, 0
                else:
                    b, co = 2, 64 * (m % 2)
                for j in range(NJ):
                    mm(PS[b][:, co:co + n], RO2[s][:, j, :], ACT2[:, j, t0:t0 + n], j == 0, j == NJ - 1, [("ra", s), ("act", j, nt)],
                       [("ps", b)] if j in (0, NJ - 1) else [])
                V(lambda e, m=m, b=b, co=co, t0=t0, n=n: e.tensor_copy(out=XT[:, m, t0:t0 + n], in_=PS[b][:, co:co + n]), [("ps", b)],
                  [("out2", m)] + (["ht2", "xt"] + [("tmp2", i_) for i_ in range(4)] if (m == 0 and nt == 0) else []))
            sqb = SQ2B[m % 2]
            S(lambda e, sqb=sqb, m=m: e.activation(out=sqb[:, 0:T], in_=XT[:, m, :], func=AF.Square), [("out2", m)], [("sq2b", m % 2)])
            for mq in ([m - 1] if m > 0 else []) + ([15] if m == 15 else []):
                sqq = SQ2B[mq % 2]
                for nt, (t0, n) in enumerate(NTILES):
                    mm(PS[5 + nt][:, 0:n], ONESB, sqq[:, t0:t0 + n], mq == 0, mq == 15, [("sq2b", mq % 2), "onesb"], [("ps", 5 + nt)] if mq in (0, 15) else [])
        P.barrier()
        DQ(lambda e: e.dma_start(out=XF[:, :, :], in_=xsp.rearrange("p (k t) -> p k t", t=T)), ["xsp"], ["xf"], key="fill")
        for nt, (t0, n) in enumerate(NTILES):
            rstd_ln(PS[5 + nt][:, 0:n], 5 + nt, RSTD2P[:, t0:t0 + n], 1.0 / D, "rstd2p")
        V(lambda e: e.tensor_scalar(out=RSTD2P, in0=RSTD2P, scalar1=0.5, scalar2=None, op0=ALU.mult), ["rstd2p"], ["rstd2p"])
        for m in range(16):
            V(lambda e, m=m: e.tensor_tensor(out=XT[:, m, :], in0=XT[:, m, :], in1=RSTD2P, op=ALU.mult), [("out2", m), "rstd2p"], [("out2", m)])
            V(lambda e, m=m: e.scalar_tensor_tensor(out=XT[:, m, :], in0=XT[:, m, :], scalar=gpost[:, m:m + 1], in1=XF[:, m, :],
                                                    op0=ALU.mult, op1=ALU.add), [("out2", m), "xf", "par"], ["xt", ("xtm", m)])
            if nxt:
                sqb = SQ2B[m % 2]
                S(lambda e, sqb=sqb, m=m: e.activation(out=sqb[:, 0:T], in_=XT[:, m, :], func=AF.Square), [("xtm", m)], [("sq2b", m % 2)])
                for nt, (t0, n) in enumerate(NTILES):
                    mm(PS[5 + nt][:, 0:n], ONESB, sqb[:, t0:t0 + n], m == 0, m == 15, [("sq2b", m % 2), "onesb"], [("ps", 5 + nt)] if m in (0, 15) else [])
        state["ssq_ready"] = bool(nxt)

    def load_wr(src3, col0, ncols=256):
        s = state["wr"] % 2
        state["wr"] += 1
        DG(lambda e, s=s: e.dma_start(out=WR[s][:, :, 0:ncols], in_=src3[:, :, col0:col0 + ncols]), w=[("wr", s)], key=("wr", s))
        return s

    def proj_fm(wchunk, wkey, rhs3, rkey, bset):
        for nt, (t0, n) in enumerate(NTILES):
            b = bset[nt]
            for k in range(16):
                mm(PS[b][:, 0:n], wchunk[:, k, :], rhs3[:, k, t0:t0 + n], k == 0, k == 15, [wkey, rkey], [("ps", b)] if k in (0, 15) else [])

    def mixer(l):
        wmv = wmi[l].rearrange("(k p) c -> p k c", p=128)
        gpre = GAIN[:, (l * 6 + 2) * 16:(l * 6 + 2) * 16 + 16]
        gpost = GAIN[:, (l * 6 + 3) * 16:(l * 6 + 3) * 16 + 16]
        DQ(lambda e: e.dma_start(out=xsp, in_=XTf), ["xt"], ["xsp"], key="spill")
        for nt, (t0, n) in enumerate(NTILES):
            b = 5 + nt
            if state.get("ssq_ready"):
                continue
            for k in range(16):
                sq = SQM[k % 2]
                S(lambda e, sq=sq, k=k, t0=t0, n=n: e.activation(out=sq[:, 0:n], in_=XT[:, k, t0:t0 + n], func=AF.Square), ["xt"], [("sqa", k % 2)])
                mm(PS[b][:, 0:n], ONESB, sq[:, 0:n], k == 0, k == 15, [("sqa", k % 2), "onesb"], [("ps", b)] if k in (0, 15) else [])
        state["ssq_ready"] = False
        RS = W(MB, T)
        for nt, (t0, n) in enumerate(NTILES):
            rstd_ln(PS[5 + nt][:, 0:n], 5 + nt, RS[:, t0:t0 + n], 1.0 / D, "rsx")
        for k in range(16):
            V(lambda e, k=k: e.scalar_tensor_tensor(out=HTM[:, k, :], in0=XT[:, k, :], scalar=gpre[:, k:k + 1], in1=RS, op0=ALU.mult, op1=ALU.mult),
              ["xt", "rsx", "par"], ["htm"])
        P.barrier()

        for gi, grp in enumerate(GROUPS):
            NQ, NU, upq, H0, mch0 = grp["NQ"], grp["NU"], grp["upq"], grp["H0"], grp["mch0"]
            gla = grp["kind"] == "gla"
            NV = NU * 128
            if gla:
                DG(lambda e: e.dma_start(out=WALR, in_=wmv[:, :, C_A:C_A + 16]), w=["walr"], key="walr")
                DG(lambda e: e.dma_start(out=WUP[0:16, 0:512], in_=wgu[l]), w=["wup"], key="wup")
                for nt, (t0, n) in enumerate(NTILES):
                    for k in range(16):
                        mm(PS[nt][0:16, 0:n], WALR[:, k, :], HTM[:, k, t0:t0 + n], k == 0, k == 15, ["walr", "htm"], [("ps", nt)] if k in (0, 15) else [])
                    S(lambda e, nt=nt, t0=t0, n=n: e.activation(out=ALR[0:16, t0:t0 + n], in_=PS[nt][0:16, 0:n], func=AF.Copy), [("ps", nt)], ["alr"])
            pset = 0
            for h in range(NQ):
                H = H0 + h
                if gla:
                    bset = [3 * pset, 3 * pset + 1, 3 * pset + 2]
                    pset ^= 1
                    for nt, (t0, n) in enumerate(NTILES):
                        b = bset[nt]
                        mm(PS[b][:, 0:n], WUP[0:16, h * 128:(h + 1) * 128], ALR[0:16, t0:t0 + n], True, True, ["wup", "alr"], [("ps", b)])
                        S(lambda e, b=b, t0=t0, n=n, h=h: e.activation(out=ZG[:, t0:t0 + n], in_=PS[b][:, 0:n], func=AF.Exp,
                                                                     bias=NEGB[:, l * 4 + h:l * 4 + h + 1], scale=-1.0), [("ps", b), "par"], ["zg"])
                    S(lambda e: e.activation(out=ZG, in_=ZG, func=AF.Ln, bias=ONEC, scale=1.0), ["zg", "par"], ["zg"])
                    V(lambda e: e.tensor_scalar(out=ZG, in0=ZG, scalar1=-1.0 / 16.0, scalar2=None, op0=ALU.mult), ["zg"], ["zg"])
                else:
                    if h % 2 == 0:
                        sf = load_wr(wmv, C_HF + (H0 + h) * 128)
                    bset = [3 * pset, 3 * pset + 1, 3 * pset + 2]
                    pset ^= 1
                    proj_fm(WR[sf][:, :, (h % 2) * 128:(h % 2) * 128 + 128], ("wr", sf), HTM, "htm", bset)
                    for nt, (t0, n) in enumerate(NTILES):
                        b = bset[nt]
                        S(lambda e, b=b, t0=t0, n=n: e.activation(out=ZG[:, t0:t0 + n], in_=PS[b][:, 0:n], func=AF.Sigmoid), [("ps", b)], ["zg"])
                        S(lambda e, b=b, t0=t0, n=n: e.activation(out=KR[:, t0:t0 + n], in_=PS[b][:, 0:n], func=AF.Sigmoid, scale=-1.0), [("ps", b)], ["kr"])
                    V(lambda e, H=H: e.tensor_scalar(out=ZG, in0=ZG, scalar1=OML[:, l * 8 + H:l * 8 + H + 1], scalar2=LB[:, l * 8 + H:l * 8 + H + 1],
                                                     op0=ALU.mult, op1=ALU.add), ["zg", "par"], ["zg"])
                    S(lambda e: e.activation(out=ZG, in_=ZG, func=AF.Ln), ["zg"], ["zg"])
                V(lambda e: e.tensor_tensor_scan(out=D1[:, 0:1024], data0=ONEC.to_broadcast([128, 1024]), data1=ZG[:, 0:1024], initial=0.0,
                                                 op0=ALU.mult, op1=ALU.add), ["zg", "par"], ["d1"])
                V(lambda e: e.tensor_tensor_scan(out=D1[:, 1024:1088], data0=RMASK, data1=ZG[:, 1024:1088], initial=0.0,
                                                 op0=ALU.mult, op1=ALU.add), ["zg", "con"], ["d1"])
                Gv = D1[:, 0:1024].rearrange("p (c t) -> p c t", t=64)
                Gs = D1[:, 1024:1088].rearrange("p (b t) -> p b t", t=4)
                V(lambda e, h=h: e.tensor_copy(out=TM[:, h, :], in_=Gv[:, :, 31]), ["d1"], ["tab"])
                V(lambda e, h=h: e.memset(TPV[:, h, 0:1], 0.0), (), ["tab"])
                V(lambda e, h=h: e.tensor_copy(out=TPV[:, h, 1:16], in_=Gv[:, 0:15, 63]), ["d1"], ["tab"])
                V(lambda e, h=h: e.tensor_tensor(out=EE[:, h, 0, :], in0=TM[:, h, :], in1=TPV[:, h, :], op=ALU.subtract), ["tab"], ["tab"])
                V(lambda e, h=h: e.tensor_tensor(out=EE[:, h, 1, :], in0=Gv[:, :, 63], in1=TPV[:, h, :], op=ALU.subtract), ["tab", "d1"], ["tab"])
                V(lambda e, h=h: e.tensor_tensor(out=EE[:, h, 2, :], in0=Gv[:, :, 63], in1=TM[:, h, :], op=ALU.subtract), ["tab", "d1"], ["tab"])
                S(lambda e, h=h: e.activation(out=EE[:, h, :, :], in_=EE[:, h, :, :], func=AF.Exp), ["tab"], ["tab"])
                S(lambda e, h=h: e.activation(out=ELS[:, h, :], in_=Gs[:, :, 3], func=AF.Exp), ["d1"], ["tab"])
                V(lambda e, h=h: e.tensor_tensor(out=Gv, in0=Gv, in1=TM[:, h, :].unsqueeze(2).to_broadcast([128, 16, 64]), op=ALU.subtract),
                  ["d1", "tab"], ["d1"])
                S(lambda e: e.activation(out=EK, in_=D1, func=AF.Exp, scale=-1.0), ["d1"], ["ek"])
                S(lambda e: e.activation(out=D1, in_=D1, func=AF.Exp), ["d1"], ["d1"])
                if gla:
                    if h % 2 == 0:
                        sk = load_wr(wmv, C_K + h * 128)
                    bset = [3 * pset, 3 * pset + 1, 3 * pset + 2]
                    pset ^= 1
                    proj_fm(WR[sk][:, :, (h % 2) * 128:(h % 2) * 128 + 128], ("wr", sk), HTM, "htm", bset)
                    for nt, (t0, n) in enumerate(NTILES):
                        b = bset[nt]
                        V(lambda e, b=b, t0=t0, n=n, h=h: e.tensor_tensor(out=KT[:, h, t0:t0 + n], in0=PS[b][:, 0:n], in1=EK[:, t0:t0 + n], op=ALU.mult),
                          [("ps", b), "ek"], ["kt"])
                else:
                    V(lambda e, h=h, H=H: e.scalar_tensor_tensor(out=KT[:, h, :], in0=KR, scalar=OML[:, l * 8 + H:l * 8 + H + 1], in1=EK,
                                                                 op0=ALU.mult, op1=ALU.mult), ["kr", "ek", "par"], ["kt"])
                if h % 2 == 0:
                    sq_ = load_wr(wmv, (C_Q + h * 128) if gla else (C_HQ + (H0 + h) * 128))
                bset = [3 * pset, 3 * pset + 1, 3 * pset + 2]
                pset ^= 1
                proj_fm(WR[sq_][:, :, (h % 2) * 128:(h % 2) * 128 + 128], ("wr", sq_), HTM, "htm", bset)
                for nt, (t0, n) in enumerate(NTILES):
                    b = bset[nt]
                    if gla:
                        V(lambda e, b=b, t0=t0, n=n, h=h: e.scalar_tensor_tensor(out=QT[:, h, t0:t0 + n], in0=PS[b][:, 0:n], scalar=128.0 ** -0.5,
                                                                                 in1=D1[:, t0:t0 + n], op0=ALU.mult, op1=ALU.mult), [("ps", b), "d1"], ["qt"])
                    else:
                        S(lambda e, b=b, t0=t0, n=n: e.activation(out=KR[:, t0:t0 + n], in_=PS[b][:, 0:n], func=AF.Silu), [("ps", b), "kt"], ["kr"])
                        V(lambda e, t0=t0, n=n, h=h: e.scalar_tensor_tensor(out=QT[:, h, t0:t0 + n], in0=KR[:, t0:t0 + n], scalar=128.0 ** -0.5,
                                                                            in1=D1[:, t0:t0 + n], op0=ALU.mult, op1=ALU.mult), ["kr", "d1"], ["qt"])
                KH = KR[:, 0:544].bitcast(BF16)
                V(lambda e, h=h: e.tensor_tensor(out=KH[:, 0:1024].rearrange("p (c t) -> p c t", t=64),
                                                 in0=KT[:, h, 0:1024].rearrange("p (c t) -> p c t", t=64),
                                                 in1=EE[:, h, 2, :].unsqueeze(2).to_broadcast([128, 16, 64]), op=ALU.mult), ["kt", "tab", "qt"], ["kr"])
                V(lambda e, h=h: e.tensor_tensor(out=KH[:, 1024:1088].rearrange("p (b t) -> p b t", t=4),
                                                 in0=KT[:, h, 1024:1088].rearrange("p (b t) -> p b t", t=4),
                                                 in1=ELS[:, h, :].unsqueeze(2).to_broadcast([128, 16, 4]), op=ALU.mult), ["kt", "tab", "qt"], ["kr"])
                pb6 = PS[6][:].bitcast(BF16)
                pb7 = PS[7][:].bitcast(BF16)
                for blk in range(8):
                    TE(lambda e, blk=blk: e.transpose(out=pb6[:, blk * 128:(blk + 1) * 128], in_=KH[:, blk * 128:(blk + 1) * 128], identity=IDB),
                       ["kr", "idb"], [("ps", 6)] if blk in (0, 7) else [])
                TE(lambda e: e.transpose(out=pb7[0:64, 0:128], in_=KH[:, 1024:1088], identity=IDB), ["kr", "idb"], [("ps", 7)])
                S(lambda e, h=h: e.activation(out=KTOK[:, 0:8, h * 128:(h + 1) * 128], in_=pb6[:, 0:1024].rearrange("p (b c) -> p b c", c=128), func=AF.Copy),
                  [("ps", 6)], ["ktok"])
                S(lambda e, h=h: e.activation(out=KTOK[0:64, 8, h * 128:(h + 1) * 128], in_=pb7[0:64, 0:128], func=AF.Copy), [("ps", 7)], ["ktok"])
            vc0 = C_V if gla else C_HI + H0 * 128
            cnt = 0
            for vt in range(NV // 256):
                sv = load_wr(wmv, vc0 + vt * 256)
                for blk in range(9):
                    rows = 128 if blk < 8 else 64
                    b = 6 + cnt % 2
                    off = 0
                    first = True
                    cnt += 1
                    for k in range(16):
                        mm(PS[b][0:rows, off:off + 256], HTM[:, k, blk * 128:blk * 128 + rows], WR[sv][:, k, :], (k == 0 and first), (k == 15),
                           [("wr", sv), "htm"], [("ps", b)] if ((k == 0 and first) or k == 15) else [])
                    S(lambda e, b=b, off=off, rows=rows, blk=blk, vt=vt: e.activation(out=VTOK[0:rows, blk, vt * 256:(vt + 1) * 256],
                                                                                    in_=PS[b][0:rows, off:off + 256], func=AF.Copy), [("ps", b)], ["vtok"])
            Sv = SST[:, 0:NV].rearrange("p (u v) -> p u v", v=128)
            if gla:
                pin_v = pin_gla[l].rearrange("h d x -> d h x")
                pst_v = pst_gla[l].rearrange("h d x -> d h x")
                S4 = SST[:, 0:NV].rearrange("p (h x) -> p h x", x=256)
            else:
                pin_v = pin_hg[l, H0:H0 + 4].rearrange("h d x -> d h x")
                pst_v = pst_hg[l, H0:H0 + 4].rearrange("h d x -> d h x")
                S4 = SST[:, 0:NV].rearrange("p (h x) -> p h x", x=128)

            def hview(ap2, n):
                return ap2.unsqueeze(2).to_broadcast([128, NQ, upq * n])

            def norm_gate(bo, bn, t0):
                NUc = NU * 64
                S(lambda e: e.activation(out=SQN[:, 0:NUc], in_=PS[bo][:, 0:NUc], func=AF.Square), [("ps", bo)], ["sqn"])
                if gla:
                    for h in range(4):
                        mm(PS[bn][:, 256 + h * 64:256 + (h + 1) * 64], ONESB, SQN[:, (2 * h) * 64:(2 * h + 1) * 64], h == 0, False,
                           ["sqn", "onesb"], [("ps", bn)] if h == 0 else [])
                        mm(PS[bn][:, 256 + h * 64:256 + (h + 1) * 64], ONESB, SQN[:, (2 * h + 1) * 64:(2 * h + 2) * 64], False, h == 3,
                           ["sqn", "onesb"], [("ps", bn)] if h == 3 else [])
                    dv = 256.0
                else:
                    mm(PS[bn][:, 256:512], ONESB, SQN[:, 0:256], True, True, ["sqn", "onesb"], [("ps", bn)])
                    dv = 128.0
                rstd_ln(PS[bn][:, 256:512], bn, RSTDN[:, 0:256], 1.0 / dv, "rstdn")
                V(lambda e: e.tensor_tensor(out=T1[:, 0:NUc].rearrange("p (h c t) -> p h c t", h=NQ, c=upq),
                                            in0=PS[bo][:, 0:NUc].rearrange("p (h c t) -> p h c t", h=NQ, c=upq),
                                            in1=RSTDN[:, 0:256].rearrange("p (h t) -> p h t", t=64).unsqueeze(2).to_broadcast([128, NQ, upq, 64]),
                                            op=ALU.mult), [("ps", bo), "rstdn"], ["t1"])
                if gla:
                    T1v = T1[:, 0:NUc].rearrange("p (h c t) -> p h c t", h=4, c=2)
                    Mv = MERGED[:, 0:8, t0:t0 + 64].rearrange("p (h c) t -> p h c t", c=2)
                    Gtv = GATE[:, 0:8, t0:t0 + 64].rearrange("p (h c) t -> p h c t", c=2)
                    for c in range(2):
                        V(lambda e, c=c: e.scalar_tensor_tensor(out=Mv[:, :, c, :], in0=T1v[:, :, c, :], scalar=GNW[:, l * 2 + c:l * 2 + c + 1],
                                                                in1=Gtv[:, :, c, :], op0=ALU.mult, op1=ALU.mult), ["t1", "gate", "par"], ["merged"])
                else:
                    V(lambda e: e.scalar_tensor_tensor(out=MERGED[:, mch0:mch0 + 4, t0:t0 + 64], in0=T1[:, 0:256].rearrange("p (h t) -> p h t", t=64),
                                                       scalar=HNW[:, l:l + 1], in1=GATE[:, 0:4, t0:t0 + 64], op0=ALU.mult, op1=ALU.mult),
                      ["t1", "gate", "par"], ["merged"])

            nPb = NV // 512
            p1mode = [False]

            def cvars(c):
                par = c % 2
                bP = [4 * par + 2, 4 * par + 3][:nPb]
                if p1mode[0]:
                    bP = ([3, 4] if par == 0 else [6, 7])[:nPb]
                return par, bP, c // 2, 64 * (c % 2)

            def state_P(c):
                par, bP, blk, r0 = cvars(c)
                for u in range(NU):
                    hq = u // upq
                    b = bP[(u * 128) // 512]
                    o = (u * 128) % 512
                    fi = (o == 0)
                    la = (o == 384) or (u == NU - 1)
                    mm(PS[b][:, o:o + 128], KTOK[r0:r0 + 64, blk, hq * 128:(hq + 1) * 128], VTOK[r0:r0 + 64, blk, u * 128:(u + 1) * 128], fi, la,
                       ["ktok", "vtok"], [("ps", b)] if (fi or la) else [])

            def state_U(c):
                par, bP, blk, r0 = cvars(c)
                X = upq * 128
                for h in range(NQ):
                    b = bP[(h * X) // 512]
                    o = (h * X) % 512
                    V(lambda e, h=h, b=b, o=o, c=c, X=X: e.scalar_tensor_tensor(out=SST[:, h * X:(h + 1) * X], in0=SST[:, h * X:(h + 1) * X],
                                                                                 scalar=EE[:, h, 1, c:c + 1], in1=PS[b][:, o:o + X],
                                                                                 op0=ALU.mult, op1=ALU.add), ["sst", "tab", "sbf", ("ps", b)], ["sst"])

            gc0 = C_R if gla else C_HG + H0 * 128
            if fused:
                V(lambda e: e.memset(SST[:, 0:NV], 0.0), (), ["sst"])
            p1mode[0] = True
            cper = 16 // NU
            for u in range(NU):
                if u % 2 == 0:
                    sg = load_wr(wmv, gc0 + u * 128)
                bset = [0, 1, 2] if fused else [3 * pset, 3 * pset + 1, 3 * pset + 2]
                pset ^= 1
                proj_fm(WR[sg][:, :, (u % 2) * 128:(u % 2) * 128 + 128], ("wr", sg), HTM, "htm", bset)
                for nt, (t0, n) in enumerate(NTILES):
                    b = bset[nt]
                    S(lambda e, b=b, t0=t0, n=n, u=u: e.activation(out=GATE[:, u, t0:t0 + n], in_=PS[b][:, 0:n], func=AF.Silu), [("ps", b)], ["gate"])
                if fused:
                    for c in range(u * cper, (u + 1) * cper):
                        state_P(c)
                        state_U(c)
            p1mode[0] = False
            if fused:
                bn = nc.dram_tensor("bnc%d%d" % (l, gi), [128, NV], F32).ap()
                gt = nc.dram_tensor("gth%d%d" % (l, gi), [256, NV], F32).ap()
                DG(lambda e: e.dma_start(out=bn, in_=SST[:, 0:NV]), ["sst"], [("bnc", l, gi)], key=("bn", gi))
                DG(lambda e: e.collective_compute("AllGather", ALU.bypass, replica_groups=[[0, 1], [2, 3], [4, 5], [6, 7]],
                                                  ins=[bn.opt()], outs=[gt.opt()]), [("bnc", l, gi)], [("gth", l, gi)], key=("cc", 0))

            def init_prompt_state():
                if fused:
                    DQ(lambda e: e.dma_start(out=SST[:, 0:NV], in_=gt[0:128, :]), [("gth", l, gi)], ["sst"], key="pin")
                    V(lambda e: e.tensor_scalar(out=SST[:, 0:NV], in0=SST[:, 0:NV], scalar1=MASKC, scalar2=None, op0=ALU.mult), ["sst", "par"], ["sst"])
                else:
                    DQ(lambda e: e.dma_start(out=S4, in_=pin_v), w=["sst"], key="pin")
            ss_ = slice(1024, 1088)
            bsc, bo = 0, 1
            for h in range(NQ):
                mm(PS[bsc][0:64, h * 64:(h + 1) * 64], KT[:, h, ss_], QT[:, h, ss_], h == 0, h == NQ - 1, ["kt", "qt"],
                   [("ps", bsc)] if h in (0, NQ - 1) else [])
            V(lambda e: e.tensor_tensor(out=SCSS[0:64, 0:NQ * 64].rearrange("p (h t) -> p h t", t=64),
                                        in0=PS[bsc][0:64, 0:NQ * 64].rearrange("p (h t) -> p h t", t=64),
                                        in1=SMASK[0:64, :].unsqueeze(1).to_broadcast([64, NQ, 64]), op=ALU.mult), [("ps", bsc), "con"], ["scss"])
            for u in range(NU):
                hq = u // upq
                mm(PS[bo][:, u * 64:(u + 1) * 64], VTOK[0:64, 8, u * 128:(u + 1) * 128], SCSS[0:64, hq * 64:(hq + 1) * 64], u == 0, False,
                   ["vtok", "scss"], [("ps", bo)] if u == 0 else [])
            SRB = [SRING, W(MB + 11648, 2048)]
            SBB = [SBFS, WB(MB + 11648 + 2048, 1024)]
            for bp in range(8):
                b0 = 2 * bp
                i2 = bp % 2
                xw_s = [("wr", 0)] if i2 == 1 else []
                xw_b = [("wr", 1)] if i2 == 1 else []
                SRc, SBc = SRB[i2], SBB[i2]
                SR4 = SRc[:, 0:2 * NV].rearrange("p (b u v) -> p b u v", b=2, v=128)
                SB4 = SBc[:, 0:2 * NV].rearrange("p (b u v) -> p b u v", b=2, v=128)
                xx = 256 if gla else 128
                SRd = SRc[:, 0:2 * NV].rearrange("p (b h x) -> p b h x", b=2, x=xx)
                sin_v, sout_v = [], []
                for bi in range(2):
                    if gla:
                        sin_v.append(sgla[l, b0 + bi].rearrange("h d x -> d h x"))
                        sout_v.append(sst_gla[l, b0 + bi].rearrange("h d x -> d h x"))
                    else:
                        sin_v.append(shg[l, b0 + bi, H0:H0 + 4].rearrange("h d x -> d h x"))
                        sout_v.append(sst_hg[l, b0 + bi, H0:H0 + 4].rearrange("h d x -> d h x"))
                kr = [("sring", i2, 0), ("sring", i2, 1)]
                for bi in range(2):
                    DQ(lambda e, bi=bi: e.dma_start(out=SRd[:, bi], in_=sin_v[bi]), w=[("sring", i2, bi)] + xw_s, key=("sin", i2, bi))
                S(lambda e: e.activation(out=SBc[:, 0:2 * NV], in_=SRc[:, 0:2 * NV], func=AF.Copy), kr + xw_s, [("sbfs", i2)] + xw_b)
                for bi in range(2):
                    bb_ = b0 + bi
                    for u in range(NU):
                        hq = u // upq
                        lastmm = (bp == 7 and bi == 1 and u == NU - 1)
                        mm(PS[bo][:, u * 64 + 4 * bb_:u * 64 + 4 * bb_ + 4], SB4[:, bi, u, :], QT[:, hq, 1024 + 4 * bb_:1024 + 4 * bb_ + 4], False, lastmm,
                           [("sbfs", i2), "qt"] + xw_b, [("ps", bo)] if lastmm else [])
                KW = NQ * 128
                X = upq * 128
                V(lambda e, b0=b0: e.tensor_tensor(out=VM[0:64, 0:2 * KW].rearrange("p (b x) -> p b x", b=2),
                                                   in0=KTOK[0:64, 8, 0:KW].unsqueeze(1).to_broadcast([64, 2, KW]),
                                                   in1=BSEL[0:64, b0:b0 + 2].unsqueeze(2).to_broadcast([64, 2, KW]), op=ALU.mult), ["ktok", "con"], ["vm"])
                KM3 = VM[0:64, 0:2 * KW].rearrange("p (b x) -> p b x", b=2)
                pb0 = 4 if (gla or bp % 2 == 0) else 6
                for bi in range(2):
                    for h in range(NQ):
                        f = bi * NQ + h
                        b = pb0 + (f * X) // 512
                        o = (f * X) % 512
                        mm(PS[b][:, o:o + X], KM3[:, bi, h * 128:(h + 1) * 128], VTOK[0:64, 8, h * X:(h + 1) * X], o == 0, True,
                           ["vm", "vtok"], [("ps", b)])
                for bi in range(2):
                    for h in range(NQ):
                        f = bi * NQ + h
                        b = pb0 + (f * X) // 512
                        o = (f * X) % 512
                        V(lambda e, bi=bi, h=h, b=b, o=o: e.scalar_tensor_tensor(out=SRc[:, bi * NV + h * X:bi * NV + (h + 1) * X],
                                                                                 in0=SRc[:, bi * NV + h * X:bi * NV + (h + 1) * X],
                                                                                 scalar=ELS[:, h, b0 + bi:b0 + bi + 1], in1=PS[b][:, o:o + X],
                                                                                 op0=ALU.mult, op1=ALU.add),
                          [("sring", i2, bi), "tab", ("sbfs", i2), ("ps", b)] + xw_s, [("sring", i2, bi)] + xw_s)
                for bi in range(2):
                    DG(lambda e, bi=bi: e.dma_start(out=sout_v[bi], in_=SRd[:, bi]), [("sring", i2, bi)] + xw_s, key=("sout", i2, bi))
            norm_gate(bo, bsc, 1024)
            init_prompt_state()
            for c in range(16):
                par = c % 2
                bsc, bo = 4 * par, 4 * par + 1
                bP = [4 * par + 2, 4 * par + 3][:nPb]
                blk, hf = c // 2, c % 2
                r0 = 64 * hf
                cs = slice(64 * c, 64 * c + 64)
                state_P(c)
                for h in range(NQ):
                    mm(PS[bsc][r0:r0 + 64, h * 64:(h + 1) * 64], KT[:, h, cs], QT[:, h, cs], h == 0, h == NQ - 1, ["kt", "qt"],
                       [("ps", bsc)] if h in (0, NQ - 1) else [])
                V(lambda e, bsc=bsc, r0=r0: e.tensor_tensor(out=SCS[r0:r0 + 64, 0:NQ * 64].rearrange("p (h t) -> p h t", t=64),
                                                           in0=PS[bsc][r0:r0 + 64, 0:NQ * 64].rearrange("p (h t) -> p h t", t=64),
                                                           in1=CMASK[r0:r0 + 64, :].unsqueeze(1).to_broadcast([64, NQ, 64]), op=ALU.mult),
                  [("ps", bsc), "con"], ["scs"])
                for h in range(NQ):
                    X = upq * 128
                    S(lambda e, h=h, c=c, X=X: e.activation(out=SBF[:, h * X:(h + 1) * X], in_=SST[:, h * X:(h + 1) * X], func=AF.Copy,
                                                            scale=EE[:, h, 0, c:c + 1]), ["sst", "tab"], ["sbf"])
                for u in range(NU):
                    hq = u // upq
                    mm(PS[bo][:, u * 64:(u + 1) * 64], VTOK[r0:r0 + 64, blk, u * 128:(u + 1) * 128], SCS[r0:r0 + 64, hq * 64:(hq + 1) * 64],
                       u == 0, False, ["vtok", "scs"], [("ps", bo)] if u == 0 else [])
                for u in range(NU):
                    hq = u // upq
                    mm(PS[bo][:, u * 64:(u + 1) * 64], SBF[:, u * 128:(u + 1) * 128], QT[:, hq, cs], False, u == NU - 1, ["sbf", "qt"],
                       [("ps", bo)] if u == NU - 1 else [])
                state_U(c)
                if c > 0:
                    pp = (c - 1) % 2
                    norm_gate(4 * pp + 1, 4 * pp, 64 * (c - 1))
            norm_gate(5, 4, 64 * 15)
            DQ(lambda e: e.dma_start(out=pst_v, in_=S4), ["sst"], key="pst")

        P.barrier()
        DQ(lambda e: e.dma_start(out=XTf, in_=xsp), ["xsp"], ["xt"], key="fill")
        wov = wmo[l].rearrange("(k p) c -> p k c", p=128)
        for mp in range(8):
            so = load_wr(wov, mp * 256)
            for ci in range(2):
                m = 2 * mp + ci
                bset = [3 * (m % 2), 3 * (m % 2) + 1, 3 * (m % 2) + 2]
                proj_fm(WR[so][:, :, ci * 128:(ci + 1) * 128], ("wr", so), MERGED, "merged", bset)
                for nt, (t0, n) in enumerate(NTILES):
                    V(lambda e, m=m, nt=nt, t0=t0, n=n, bset=bset: e.tensor_copy(out=OUTM[:, m, t0:t0 + n], in_=PS[bset[nt]][:, 0:n]),
                      [("ps", bset[nt])], [("outm", m)])
        for nt, (t0, n) in enumerate(NTILES):
            b = 6 if nt < 2 else 7
            o = 0 if nt != 1 else 0
            for m in range(16):
                sq = SQN if m % 2 == 0 else SCS
                S(lambda e, sq=sq, m=m, t0=t0, n=n: e.activation(out=sq[:, 0:n], in_=OUTM[:, m, t0:t0 + n], func=AF.Square), [("outm", m)], [("sqo", m % 2)])
                bb_ = [0, 1, 2][nt]
                mm(PS[bb_][:, 0:n], ONESB, sq[:, 0:n], m == 0, m == 15, [("sqo", m % 2), "onesb"], [("ps", bb_)] if m in (0, 15) else [])
        RS2 = W(MB + 8704, T)
        for nt, (t0, n) in enumerate(NTILES):
            rstd_ln(PS[nt][:, 0:n], nt, RS2[:, t0:t0 + n], 1.0 / D, "rs2")
        for m in range(16):
            V(lambda e, m=m: e.tensor_tensor(out=OUTM[:, m, :], in0=OUTM[:, m, :], in1=RS2, op=ALU.mult), [("outm", m), "rs2"], [("outm", m)])
            V(lambda e, m=m: e.scalar_tensor_tensor(out=XT[:, m, :], in0=OUTM[:, m, :], scalar=gpost[:, m:m + 1], in1=XT[:, m, :],
                                                    op0=ALU.mult, op1=ALU.add), [("outm", m), "xt", "par"], ["xt", ("xtm", m)])
            for nt, (t0, n) in enumerate(NTILES):
                sq = SQN if (3 * m + nt) % 2 == 0 else SCS
                S(lambda e, sq=sq, m=m, t0=t0, n=n: e.activation(out=sq[:, 0:n], in_=XT[:, m, t0:t0 + n], func=AF.Square), [("xtm", m)], [("sqo", (3 * m + nt) % 2)])
                mm(PS[5 + nt][:, 0:n], ONESB, sq[:, 0:n], m == 0, m == 15, [("sqo", (3 * m + nt) % 2), "onesb"], [("ps", 5 + nt)] if m in (0, 15) else [])
        state["ssq_ready"] = True
        P.barrier()

    nst = 0
    for l in range(2):
        if nst < stage:
            ffn(l, w1i, w1o, 0, 1)
            P.barrier()
        nst += 1
        if nst < stage:
            mixer(l)
        nst += 1
        if nst < stage:
            ffn(l, w2i, w2o, 4, 5, nxt=(l == 0))
            P.barrier()
        nst += 1

    YST = [W(PB, 2048), W(PB + 2048, 2048)]
    for blk in range(9):
        rows = 128 if blk < 8 else 64
        ys = YST[blk % 2]
        for q in range(4):
            bk = (blk * 4 + q) % 2
            for i in range(4):
                k = 4 * q + i
                TE(lambda e, bk=bk, i=i, k=k, blk=blk, rows=rows: e.transpose(out=PS[bk][0:rows, i * 128:(i + 1) * 128],
                                                                             in_=XT[:, k, blk * 128:blk * 128 + rows], identity=IDF),
                   ["xt", "con"], [("ps", bk)] if i in (0, 3) else [])
            V(lambda e, bk=bk, q=q, ys=ys, rows=rows: e.tensor_copy(out=ys[0:rows, q * 512:(q + 1) * 512], in_=PS[bk][0:rows, 0:512]),
              [("ps", bk)], [("yst", blk % 2)])
        DQ(lambda e, ys=ys, blk=blk, rows=rows: e.dma_start(out=y[blk * 128:blk * 128 + rows, :], in_=ys[0:rows, :]), [("yst", blk % 2)],
           key=("ys", blk % 2))
    P.emit()
    return nc, P


def make_consts():
    c = np.zeros((128, 336), np.float32)
    c[:, 0:128] = np.eye(128, dtype=np.float32)
    p = np.arange(128)[:, None]
    t = np.arange(64)[None, :]
    c[:, 128:192] = ((p % 64) <= t).astype(np.float32)
    c[:, 192:256] = (((p % 64) // 4 == t // 4) & ((p % 64) <= t)).astype(np.float32)
    b = np.arange(16)[None, :]
    c[:, 256:272] = ((p % 64) // 4 == b).astype(np.float32)
    c[:, 272:336] = (np.arange(64)[None, :] % 4 != 0).astype(np.float32)
    return c


_CACHE = {}


def _get_nc(stage=99):
    if stage not in _CACHE:
        _CACHE[stage] = build(stage)[0]
    return _CACHE[stage]


def kernel(**inputs):
    f = lambda a: np.ascontiguousarray(np.asarray(a, dtype=np.float32))
    xp = f(inputs["x_prompt"])
    xs = f(inputs["x_sample"])
    sg = f(inputs["state_gla"])
    sh = f(inputs["state_hgrn"])
    shared = {k: f(inputs[k]) for k in ("norm_gains", "ffn1_w_in", "ffn1_w_out", "ffn2_w_in", "ffn2_w_out", "mix_w_in",
                                        "gla_w_gate_up", "gla_b_gate", "gla_norm", "hgrn_gamma", "hgrn_norm", "mix_w_out")}
    cst = make_consts()
    zg = np.zeros((2, 4, 128, 256), np.float32)
    zh = np.zeros((2, 8, 128, 128), np.float32)
    in_maps = []
    for c in range(NCORES):
        s, half = c // 2, c % 2
        xin = np.concatenate([xp[s, 1024 * half:1024 * half + 1024], xs[16 * c:16 * c + 16].reshape(64, D)], axis=0)
        m = dict(shared)
        m["xin"] = np.ascontiguousarray(xin)
        m["sgla"] = np.ascontiguousarray(sg[:, 16 * c:16 * c + 16])
        m["shg"] = np.ascontiguousarray(sh[:, 16 * c:16 * c + 16])
        m["pin_gla"] = zg
        m["pin_hg"] = zh
        m["consts"] = cst
        m["coremask"] = np.full((128, 1), float(half), np.float32)
        in_maps.append(m)
    nc = _get_nc()
    r2 = run_bass_kernel_spmd(nc, in_maps, core_ids=list(range(NCORES))).results
    y_prompt = np.zeros((4, 2048, D), np.float32)
    y_sample = np.zeros((128, 4, D), np.float32)
    st_gp = np.zeros((2, 4, 4, 128, 256), np.float32)
    st_hp = np.zeros((2, 4, 8, 128, 128), np.float32)
    st_gs = np.zeros((2, 128, 4, 128, 256), np.float32)
    st_hs = np.zeros((2, 128, 8, 128, 128), np.float32)
    for c in range(NCORES):
        s, half = c // 2, c % 2
        r = r2[c]
        y_prompt[s, 1024 * half:1024 * half + 1024] = r["y"][0:1024]
        y_sample[16 * c:16 * c + 16] = r["y"][1024:1088].reshape(16, 4, D)
        st_gs[:, 16 * c:16 * c + 16] = r["sst_gla"]
        st_hs[:, 16 * c:16 * c + 16] = r["sst_hg"]
        if half == 1:
            st_gp[:, s] = r["pst_gla"]
            st_hp[:, s] = r["pst_hg"]
    return (y_prompt, y_sample, st_gp, st_hp, st_gs, st_hs)
```
